# Optimizing a Trainium2 kernel written in Bass

```python
import math
import jax, jax.numpy as jnp
from jax import lax
import numpy as np

D_MODEL = 1024
BATCH = 8
SEQ = 4096
DEPTH = 4

CHUNK = 64
Q_BLOCK = 128
EPS = 1e-6
MIX_WIDTH = D_MODEL
ATT_WIDTH = MIX_WIDTH // 2
CONV_WIDTH = MIX_WIDTH // 4
SSM_WIDTH = MIX_WIDTH - ATT_WIDTH - CONV_WIDTH
ATT_HEAD_DIM = 64
ATT_V_DIM = 2 * ATT_HEAD_DIM
N_ATT_HEADS = ATT_WIDTH // ATT_V_DIM
QK_WIDTH = N_ATT_HEADS * 2 * ATT_HEAD_DIM
CONV_KERNEL = 31
SSM_GROUP = 16
N_SSM_GROUPS = SSM_WIDTH // SSM_GROUP
SSM_STATE = 64
D_FF = 4 * D_MODEL
PLE_DIM = 256
IN_WIDTH = 2 * QK_WIDTH + ATT_WIDTH + 2 * CONV_WIDTH + SSM_WIDTH
SPLITS = (QK_WIDTH, 2 * QK_WIDTH, 2 * QK_WIDTH + ATT_WIDTH, 2 * QK_WIDTH + ATT_WIDTH + 2 * CONV_WIDTH)

kernel_name = "hybrid_diffattn_conformer_s5_block"


def rms_norm(x, g):
    x32 = x.astype(jnp.float32)
    y = x32 * lax.rsqrt(jnp.mean(jnp.square(x32), axis=-1, keepdims=True) + EPS)
    return (y * g.astype(jnp.float32)).astype(x.dtype)


def layer_norm(x, g, b):
    x32 = x.astype(jnp.float32)
    mu = jnp.mean(x32, axis=-1, keepdims=True)
    var = jnp.mean(jnp.square(x32 - mu), axis=-1, keepdims=True)
    y = (x32 - mu) * lax.rsqrt(var + EPS)
    return (y * g.astype(jnp.float32) + b.astype(jnp.float32)).astype(x.dtype)


def diff_attention(q, k, v, q_gain, k_gain, lam, out_gain, lam_init):
    bsz, seq = q.shape[0], q.shape[1]
    q = rms_norm(q, q_gain)
    k = rms_norm(k, k_gain)
    n_blk = seq // Q_BLOCK
    qb = q.reshape(bsz, n_blk, Q_BLOCK, N_ATT_HEADS, 2, ATT_HEAD_DIM).swapaxes(0, 1)
    kpos = jnp.arange(seq)
    kchunk = kpos // CHUNK
    slopes = 2.0 ** (-8.0 * jnp.arange(1, N_ATT_HEADS + 1, dtype=jnp.float32) / N_ATT_HEADS)
    scale = ATT_HEAD_DIM ** -0.5

    def block(args):
        qi, bi = args
        qpos = bi * Q_BLOCK + jnp.arange(Q_BLOCK)
        s = jnp.einsum('bqhcd,bkhcd->bhcqk', qi, k).astype(jnp.float32) * scale
        dist = jnp.abs(qpos[:, None] - kpos[None, :]).astype(jnp.float32)
        bias = -slopes[:, None, None, None] * dist[None, None]
        allowed = kchunk[None, :] <= (qpos // CHUNK)[:, None]
        s = jnp.where(allowed, s + bias, -1e30)
        pr = jax.nn.softmax(s, axis=-1)
        a = pr[:, :, 0] - lam * pr[:, :, 1]
        return jnp.einsum('bhqk,bkhd->bqhd', a.astype(v.dtype), v)

    o = lax.map(block, (qb, jnp.arange(n_blk)))
    o = o.swapaxes(0, 1).reshape(bsz, seq, N_ATT_HEADS, ATT_V_DIM)
    o = rms_norm(o, out_gain) * (1.0 - lam_init)
    return o.reshape(bsz, seq, N_ATT_HEADS * ATT_V_DIM)


def conformer_conv(u, w, b, ln_g, ln_b):
    a, gate = jnp.split(u, 2, axis=-1)
    hcv = a * jax.nn.sigmoid(gate)
    hcv = lax.conv_general_dilated(
        hcv, w[:, None, :], window_strides=(1,), padding=[(CONV_KERNEL - 1, 0)],
        dimension_numbers=('NWC', 'WIO', 'NWC'), feature_group_count=CONV_WIDTH) + b
    return jax.nn.silu(layer_norm(hcv, ln_g, ln_b))


def _ssm_combine(left, right):
    a_l, b_l = left
    a_r, b_r = right
    return a_l * a_r, a_r * b_l + b_r


def s5_ssm(u, a_re, a_im, log_dt, b_re, b_im, c_re, c_im, d_skip, glu_w, glu_b):
    bsz, seq = u.shape[0], u.shape[1]
    f32 = jnp.float32
    ug = u.reshape(bsz, seq, N_SSM_GROUPS, SSM_GROUP).astype(f32)
    lam = lax.complex(a_re.astype(f32), a_im.astype(f32))
    dt = jnp.exp(log_dt.astype(f32))[:, None]
    abar = jnp.exp(lam * dt)
    bmat = lax.complex(b_re.astype(f32), b_im.astype(f32))
    bbar = ((abar - 1.0) / lam)[..., None] * bmat
    bu = lax.complex(jnp.einsum('gpc,bsgc->bsgp', jnp.real(bbar), ug),
                     jnp.einsum('gpc,bsgc->bsgp', jnp.imag(bbar), ug))
    a_seq = jnp.broadcast_to(abar[None, None], (1, seq, N_SSM_GROUPS, SSM_STATE))
    _, xs = lax.associative_scan(_ssm_combine, (a_seq, bu), axis=1)
    y = (jnp.einsum('gcp,bsgp->bsgc', c_re.astype(f32), jnp.real(xs))
         - jnp.einsum('gcp,bsgp->bsgc', c_im.astype(f32), jnp.imag(xs))
         + d_skip.astype(f32) * ug)
    y = jax.nn.gelu(y.reshape(bsz, seq, SSM_WIDTH)).astype(u.dtype)
    return y * jax.nn.sigmoid(y @ glu_w + glu_b)


def setup_inputs(seed: int = 0) -> dict:
    key = jax.random.key(seed)
    ks = iter(jax.random.split(key, 40))
    L = DEPTH
    f32 = jnp.float32

    def nrm(shape, scale):
        return jax.random.normal(next(ks), shape, f32) * scale

    def gain(shape):
        return 1.0 + 0.05 * jax.random.normal(next(ks), shape, f32)

    G, P, C = N_SSM_GROUPS, SSM_STATE, SSM_GROUP
    return {
        "x": nrm((BATCH, SEQ, D_MODEL), 1.0),
        "p": nrm((DEPTH, BATCH, SEQ, PLE_DIM), 1.0),
        "mix_norm": gain((L, D_MODEL)),
        "w_in": nrm((L, D_MODEL, IN_WIDTH), D_MODEL ** -0.5),
        "q_norm": gain((L, ATT_HEAD_DIM)),
        "k_norm": gain((L, ATT_HEAD_DIM)),
        "lam_q1": nrm((L, ATT_HEAD_DIM), 0.1),
        "lam_k1": nrm((L, ATT_HEAD_DIM), 0.1),
        "lam_q2": nrm((L, ATT_HEAD_DIM), 0.1),
        "lam_k2": nrm((L, ATT_HEAD_DIM), 0.1),
        "att_out_norm": gain((L, ATT_V_DIM)),
        "conv_w": nrm((L, CONV_KERNEL, CONV_WIDTH), CONV_KERNEL ** -0.5),
        "conv_b": nrm((L, CONV_WIDTH), 0.02),
        "conv_ln_g": gain((L, CONV_WIDTH)),
        "conv_ln_b": nrm((L, CONV_WIDTH), 0.02),
        "ssm_a_re": -0.5 + nrm((L, G, P), 0.01),
        "ssm_a_im": math.pi * jnp.arange(P, dtype=f32)[None, None, :] + nrm((L, G, P), 0.01),
        "ssm_log_dt": jax.random.uniform(next(ks), (L, G), f32, math.log(1e-3), math.log(1e-1)),
        "ssm_b_re": nrm((L, G, P, C), (2 * C) ** -0.5),
        "ssm_b_im": nrm((L, G, P, C), (2 * C) ** -0.5),
        "ssm_c_re": nrm((L, G, C, P), P ** -0.5),
        "ssm_c_im": nrm((L, G, C, P), P ** -0.5),
        "ssm_d": nrm((L, G, C), 1.0),
        "ssm_glu_w": nrm((L, SSM_WIDTH, SSM_WIDTH), SSM_WIDTH ** -0.5),
        "ssm_glu_b": nrm((L, SSM_WIDTH), 0.02),
        "w_out": nrm((L, MIX_WIDTH, D_MODEL), MIX_WIDTH ** -0.5),
        "mlp_norm": gain((L, D_MODEL)),
        "w_ff1": nrm((L, D_MODEL, D_FF), D_MODEL ** -0.5),
        "w_ff2": nrm((L, D_FF, D_MODEL), D_FF ** -0.5),
        "ple_norm": gain((L, D_MODEL)),
        "w_ple_gate": nrm((L, D_MODEL, D_MODEL), D_MODEL ** -0.5),
        "w_ple_proj": nrm((L, PLE_DIM, D_MODEL), PLE_DIM ** -0.5),
    }


def reference(x, p, mix_norm, w_in, q_norm, k_norm, lam_q1, lam_k1, lam_q2, lam_k2,
              att_out_norm, conv_w, conv_b, conv_ln_g, conv_ln_b, ssm_a_re, ssm_a_im,
              ssm_log_dt, ssm_b_re, ssm_b_im, ssm_c_re, ssm_c_im, ssm_d, ssm_glu_w,
              ssm_glu_b, w_out, mlp_norm, w_ff1, w_ff2, ple_norm, w_ple_gate, w_ple_proj):
    bsz, seq, _ = x.shape
    h = x
    for i in range(DEPTH):
        lam_init = 0.8 - 0.6 * math.exp(-0.3 * i)
        n = rms_norm(h, mix_norm[i])
        z = n @ w_in[i]
        zq, zk, zv, zc, zs = jnp.split(z, SPLITS, axis=-1)
        q = zq.reshape(bsz, seq, N_ATT_HEADS, 2, ATT_HEAD_DIM)
        k = zk.reshape(bsz, seq, N_ATT_HEADS, 2, ATT_HEAD_DIM)
        v = zv.reshape(bsz, seq, N_ATT_HEADS, ATT_V_DIM)
        lam = (jnp.exp(jnp.sum(lam_q1[i].astype(jnp.float32) * lam_k1[i].astype(jnp.float32)))
               - jnp.exp(jnp.sum(lam_q2[i].astype(jnp.float32) * lam_k2[i].astype(jnp.float32)))
               + lam_init)
        att = diff_attention(q, k, v, q_norm[i], k_norm[i], lam, att_out_norm[i], lam_init)
        cnv = conformer_conv(zc, conv_w[i], conv_b[i], conv_ln_g[i], conv_ln_b[i])
        ssm = s5_ssm(zs, ssm_a_re[i], ssm_a_im[i], ssm_log_dt[i], ssm_b_re[i], ssm_b_im[i],
                     ssm_c_re[i], ssm_c_im[i], ssm_d[i], ssm_glu_w[i], ssm_glu_b[i])
        mixed = jnp.concatenate([att, cnv, ssm], axis=-1)
        h = h + mixed @ w_out[i]
        n2 = rms_norm(h, mlp_norm[i])
        h = h + jnp.square(jax.nn.relu(n2 @ w_ff1[i])) @ w_ff2[i]
        gate = jax.nn.sigmoid(rms_norm(h, ple_norm[i]) @ w_ple_gate[i])
        h = h + gate * (p[i] @ w_ple_proj[i])
    return h
```

```python
import math
import numpy as np
import concourse.bass as bass
import concourse.mybir as mybir
from concourse.bass_utils import run_bass_kernel_spmd

F32 = mybir.dt.float32
BF16 = mybir.dt.bfloat16
ALU = mybir.AluOpType
AF = mybir.ActivationFunctionType

L = 4
D = 1024
NT = 4096
TB = 512
NB = NT // TB
KT = D // 128
EPS = 1e-6
NSLAB = 101
SLAB_WIN, SLAB_WOUT, SLAB_W1, SLAB_W2, SLAB_GATE, SLAB_PLE, SLAB_GLU = 0, 18, 26, 58, 90, 98, 100
TS = 256
NCH = NT // TS
SLOPES = [2.0 ** (-8.0 * (i + 1) / 4) for i in range(4)]

SP_G1, SP_G2, SP_G3, SP_QG, SP_KG, SP_LAM, SP_OG, SP_CW, SP_CB, SP_LNG, SP_LNB, SP_ARE, SP_AIM, SP_LDT, SP_SD, SP_GB = (
    0, 8, 16, 24, 25, 26, 282, 283, 345, 347, 349, 351, 359, 367, 375, 377)
SP_N = 379
C_ID, C_BD, C_DH = 0, 128, 256
C_N = 768


class Res:
    __slots__ = ("name", "w", "rs")

    def __init__(self, name):
        self.name = name
        self.w = None
        self.rs = {}


class Sched:
    NDMA = 12

    def __init__(self, nc):
        self.nc = nc
        self.engs = {"pe": nc.tensor, "act": nc.scalar, "dve": nc.vector, "pool": nc.gpsimd, "sp": nc.sync}
        self.sem = {e: nc.alloc_semaphore("sem_" + e) for e in self.engs}
        self.cnt = {e: 0 for e in self.engs}
        self.prog = {e: [] for e in self.engs}
        self.known = {e: {} for e in self.engs}
        self.dsem = {}
        self.dcnt = {}
        self.dnext = {}
        for e in ("sp", "pool", "act"):
            for i in range(self.NDMA):
                k = ("dma", e, i)
                self.dsem[k] = nc.alloc_semaphore("dsem_%s_%d" % (e, i))
                self.dcnt[k] = 0
            self.dnext[e] = 0
        self.n_ops = 0

    def _semof(self, k):
        return self.sem[k] if isinstance(k, str) else self.dsem[k]

    def _deps(self, e, r, w, extra=()):
        deps = {}

        def need(ev):
            if ev is None:
                return
            k, v = ev
            if deps.get(k, 0) < v:
                deps[k] = v
        for x in r:
            ev = x.w
            if ev is not None and not (ev[0] == e and e == "pe"):
                need(ev)
        for x in w:
            ev = x.w
            if ev is not None and not (ev[0] == e and e == "pe"):
                need(ev)
            for k, v in x.rs.items():
                if not (k == e and e == "pe"):
                    need((k, v))
        for ev in extra:
            need(ev)
        waits = []
        kn = self.known[e]
        for k, v in deps.items():
            if kn.get(k, 0) < v:
                kn[k] = v
                waits.append((k, v))
        return waits

    def _commit(self, ev, r, w):
        k, v = ev
        for x in r:
            if x.rs.get(k, 0) < v:
                x.rs[k] = v
        for x in w:
            x.w = ev
            x.rs = {}

    def op(self, e, fn, r=(), w=()):
        waits = self._deps(e, r, w)
        self.cnt[e] += 1
        ev = (e, self.cnt[e])
        self.prog[e].append((waits, fn, ev, 1))
        self._commit(ev, r, w)
        self.n_ops += 1
        return ev

    def dma(self, e, out_ap, in_ap, r=(), w=()):
        i = self.dnext[e]
        self.dnext[e] = (i + 1) % self.NDMA
        k = ("dma", e, i)
        prev = (k, self.dcnt[k]) if self.dcnt[k] else None
        waits = self._deps(e, r, w, extra=[prev] if prev else [])
        self.dcnt[k] += 16
        ev = (k, self.dcnt[k])
        self.prog[e].append((waits, (lambda eng, o=out_ap, i_=in_ap: eng.dma_start(out=o, in_=i_)), ev, 16))
        self._commit(ev, r, w)
        self.n_ops += 1
        return ev

    def barrier(self):
        allev = [(e, c) for e, c in self.cnt.items() if c] + [(k, c) for k, c in self.dcnt.items() if c]
        for e in self.engs:
            waits = []
            kn = self.known[e]
            for k, v in allev:
                if k == e:
                    continue
                if kn.get(k, 0) < v:
                    kn[k] = v
                    waits.append((k, v))
            if waits:
                self.prog[e].append((waits, None, None, 0))

    def emit(self):
        nc = self.nc
        sched = self

        def run(eng, e):
            for waits, fn, ev, inc in sched.prog[e]:
                for k, v in waits:
                    eng.wait_ge(sched._semof(k), v)
                if fn is not None:
                    ins = fn(eng)
                    ins.then_inc(sched._semof(ev[0]), inc)
        with nc.Block() as block:
            @block.sync
            def _(eng):
                run(eng, "sp")

            @block.tensor
            def _(eng):
                run(eng, "pe")

            @block.scalar
            def _(eng):
                run(eng, "act")

            @block.vector
            def _(eng):
                run(eng, "dve")

            @block.gpsimd
            def _(eng):
                run(eng, "pool")


class Arena:
    def __init__(self, nc, base, limit):
        self.nc, self.base, self.limit, self.top, self.n = nc, base, limit, base, 0

    def alloc(self, name, shape, dt):
        esz = 2 if dt == BF16 else 4
        nbytes = int(np.prod(shape[1:])) * esz
        off = (self.top + 31) // 32 * 32
        assert off + nbytes <= self.limit, "SBUF arena overflow: %s %d" % (name, off + nbytes - self.limit)
        self.top = off + nbytes
        self.n += 1
        return self.nc.alloc_sbuf_tensor_at("%s_%d" % (name, self.n), list(shape), dt, offset=off)

    def mark(self):
        return self.top

    def reset(self, m):
        self.top = m


class T:
    def __init__(self, h, name, n=0):
        self.h = h
        self.r = Res(name)
        self.sub = [Res("%s.%d" % (name, i)) for i in range(n)]


class Builder:
    def __init__(self, n_layers=L, debug=None, phases=None):
        self.n_layers = n_layers
        self.debug = debug or ()
        self.phases = phases
        nc = self.nc = bass.Bass("TRN2", target_bir_lowering=False)
        self.S = Sched(nc)
        dt = nc.dram_tensor
        self.xT = dt("xT", [D, NT], F32, kind="ExternalInput").ap()
        self.pT = dt("pT", [L * 256, NT], F32, kind="ExternalInput").ap()
        self.wf = dt("wf", [L * NSLAB * 128, 1024], F32, kind="ExternalInput").ap()
        self.spm = dt("spm", [128, L * SP_N], F32, kind="ExternalInput").ap()
        self.ssmm = dt("ssmm", [L * 128, 4096], F32, kind="ExternalInput").ap()
        self.cst = dt("cst", [128, C_N], F32, kind="ExternalInput").ap()
        self.bctd = dt("bct", [128, 144], F32, kind="ExternalInput").ap()
        self.outT = dt("outT", [D, NT], F32, kind="ExternalOutput").ap()
        self.wb = dt("wb", [L * NSLAB * 128, 1024], BF16, kind="Internal").ap()
        self.qT_d = dt("qT_d", [512, NT], BF16, kind="Internal").ap()
        self.kT_d = dt("kT_d", [512, NT], BF16, kind="Internal").ap()
        self.v_d = dt("v_d", [NT, 512], BF16, kind="Internal").ap()
        self.mixT_d = dt("mixT_d", [D, NT], BF16, kind="ExternalInput" if "in_mix" in self.debug else "Internal").ap()
        self.hcv_d = dt("hcv_d", [256, 32 + NT], BF16, kind="Internal").ap()
        self.u_d = dt("u_d", [256, NT], BF16, kind="Internal").ap()
        self.r_hcv = [[Res("hcv") for _ in range(NB)] for _ in range(2)]
        self.r_u = [[Res("u") for _ in range(NB)] for _ in range(2)]
        self.r_halo = Res("halo")
        self.r_wb = [[Res("wb%d_%d" % (l, c)) for c in range(13)] for l in range(L)]
        self.r_qT = [[Res("qT") for _ in range(NB)] for _ in range(4)]
        self.r_kT = [[Res("kT") for _ in range(NB)] for _ in range(4)]
        self.r_v = [Res("v") for _ in range(NT // 128)]
        self.r_mix = [[Res("mix") for _ in range(NB)] for _ in range(8)]
        self.r_dummy = Res("dummy")
        self.dbg = {}
        for name, shape, d_ in (("d_qT", [512, NT], BF16), ("d_kT", [512, NT], BF16), ("d_v", [NT, 512], BF16),
                                ("d_mix", [D, NT], BF16), ("d_h", [D, NT], F32)):
            if name in self.debug:
                self.dbg[name] = dt(name, shape, d_, kind="ExternalOutput").ap()
        self.A = Arena(nc, 16640, 229376)
        A = self.A
        self.hT = A.alloc("hT", [128, KT * NT], F32)
        self.r_h = [[Res("h%d_%d" % (k, b)) for b in range(NB)] for k in range(KT)]
        self.spm_t = T(A.alloc("spm", [128, L * SP_N], F32), "spm")
        self.cbf = T(A.alloc("cbf", [128, 128 * 9], BF16), "cbf")
        self.misc = T(A.alloc("misc", [128, 16], F32), "misc")
        self.psum = [T(nc.alloc_psum_tensor("ps%d" % i, [128, 512], F32), "ps%d" % i) for i in range(8)]
        self.pnext = 0
        self.nrot = 8
        self.base_mark = A.mark()
        self.cst_t = T(A.alloc("cst", [128, C_N], F32), "cst")
        self.zt = T(A.alloc("zt", [128, 32], BF16), "zt")
        A.reset(self.base_mark)

    def bank(self):
        b = self.psum[self.pnext]
        self.pnext = (self.pnext + 1) % self.nrot
        return b

    def hcol(self, k, b):
        return k * NT + b * TB

    def sp_col(self, l, off, n=1):
        c = l * SP_N + off
        return self.spm_t.h[:, c:c + n]

    def mm(self, out_ap, lhsT, rhs, start, stop, r, w):
        return self.S.op("pe", lambda e: e.matmul(out_ap, lhsT, rhs, start=start, stop=stop), r=r, w=w)

    def act(self, out_ap, in_ap, func, r, w, bias=None, scale=None):
        kw = {}
        if bias is not None:
            kw["bias"] = bias
        if scale is not None:
            kw["scale"] = scale
        return self.S.op("act", lambda e: e.activation(out=out_ap, in_=in_ap, func=func, **kw), r=r, w=w)

    def tt(self, eng, out_ap, in0, in1, op, r, w):
        return self.S.op(eng, lambda e: e.tensor_tensor(out=out_ap, in0=in0, in1=in1, op=op), r=r, w=w)

    def ts(self, eng, out_ap, in0, s1, op0, r, w, s2=None, op1=None):
        if op1 is None:
            return self.S.op(eng, lambda e: e.tensor_scalar(out=out_ap, in0=in0, scalar1=s1, scalar2=None, op0=op0), r=r, w=w)
        return self.S.op(eng, lambda e: e.tensor_scalar(out=out_ap, in0=in0, scalar1=s1, scalar2=s2, op0=op0, op1=op1), r=r, w=w)

    def stt(self, eng, out_ap, in0, sc, in1, op0, op1, r, w):
        return self.S.op(eng, lambda e: e.scalar_tensor_tensor(out=out_ap, in0=in0, scalar=sc, in1=in1, op0=op0, op1=op1), r=r, w=w)

    def cp(self, eng, out_ap, in_ap, r, w):
        return self.S.op(eng, lambda e: e.tensor_copy(out=out_ap, in_=in_ap), r=r, w=w)

    def memset(self, eng, ap, val, w):
        return self.S.op(eng, lambda e: e.memset(ap, val), r=(), w=w)

    def ring_init(self, nslot):
        self.ring = [T(self.A.alloc("slab", [128, 1024], BF16), "slab%d" % i) for i in range(nslot)]
        self.rnext = 0

    def slab(self, l, s):
        t = self.ring[self.rnext]
        self.rnext = (self.rnext + 1) % len(self.ring)
        row = (l * NSLAB + s) * 128
        self.S.dma("sp", t.h[:, :], self.wb[row:row + 128, :], r=[self.r_wb[l][s // 8]], w=[t.r])
        return t

    def cast_weights(self, l):
        for c in range(13):
            r0 = (l * NSLAB + c * 8) * 128
            r1 = (l * NSLAB + min(NSLAB, c * 8 + 8)) * 128
            self.S.dma("pool", self.wb[r0:r1, :], self.wf[r0:r1, :], r=(), w=[self.r_wb[l][c]])

    def prologue(self):
        S = self.S
        self.cast_weights(0)
        S.dma("sp", self.cst_t.h[:, :], self.cst, w=[self.cst_t.r])
        S.dma("sp", self.spm_t.h[:, :], self.spm, w=[self.spm_t.r])
        for k in range(KT):
            for b in range(NB):
                c = self.hcol(k, b)
                S.dma("sp", self.hT[:, c:c + TB], self.xT[k * 128:(k + 1) * 128, b * TB:(b + 1) * TB], w=[self.r_h[k][b]])
        cb = self.cbf
        self.cp("dve", cb.h[:, 0:256], self.cst_t.h[:, 0:256], r=[self.cst_t.r], w=[cb.r])
        self.memset("dve", cb.h[:, 256:384], 1.0 / 1024, w=[cb.r])
        self.memset("dve", cb.h[:, 384:512], 1.0 / 128, w=[cb.r])
        self.memset("dve", cb.h[:, 512:640], 1.0, w=[cb.r])
        self.cp("dve", cb.h[:, 640:1152], self.cst_t.h[:, C_DH:C_DH + 512], r=[self.cst_t.r], w=[cb.r])
        self.memset("dve", self.misc.h[:, 0:1], EPS, w=[self.misc.r])
        self.memset("dve", self.misc.h[:, 1:2], math.pi / 2, w=[self.misc.r])
        self.memset("dve", self.misc.h[:, 2:3], 0.0, w=[self.misc.r])
        zt = self.zt
        self.memset("dve", zt.h[:, :], 0.0, w=[zt.r])
        for ct in range(2):
            S.dma("sp", self.hcv_d[ct * 128:(ct + 1) * 128, 0:32], zt.h[:, :], r=[zt.r], w=[self.r_halo])
        self.ident_bf = cb.h[:, 0:128]
        self.bd64 = cb.h[:, 128:256]
        self.ones1024 = cb.h[:, 256:384]
        self.ones128 = cb.h[:, 384:512]
        self.ones1 = cb.h[:, 512:640]
        self.eps_ap = self.misc.h[:, 0:1]
        self.halfpi_ap = self.misc.h[:, 1:2]
        S.barrier()

    def dh(self, h):
        return self.cbf.h[:, 640 + 128 * h:640 + 128 * (h + 1)]

    def rstd_from(self, ps_t, n, out_t, eps_ap=None):
        eps_ap = self.eps_ap if eps_ap is None else eps_ap
        self.act(out_t.h[:, 0:n], ps_t.h[:, 0:n], AF.Ln, r=[ps_t.r, self.misc.r], w=[out_t.r], bias=eps_ap)
        self.act(out_t.h[:, 0:n], out_t.h[:, 0:n], AF.Exp, r=[out_t.r], w=[out_t.r], scale=-0.5)

    def rmsnorm_a(self, b, sq8):
        for k in range(KT):
            c = self.hcol(k, b)
            self.act(sq8.h[:, k * TB:(k + 1) * TB], self.hT[:, c:c + TB], AF.Square, r=[self.r_h[k][b]], w=[sq8.sub[k]])

    def rmsnorm_b(self, l, goff, b, nT, sq8, rstd):
        ps = self.bank()
        for k in range(KT):
            self.mm(ps.h[:, :], self.ones1024, sq8.h[:, k * TB:(k + 1) * TB], k == 0, k == KT - 1, r=[sq8.sub[k], self.cbf.r], w=[ps.r])
        self.rstd_from(ps, TB, rstd)
        for k in range(KT):
            c = self.hcol(k, b)
            self.stt("dve", nT.h[:, k * TB:(k + 1) * TB], self.hT[:, c:c + TB], self.sp_col(l, goff + k), rstd.h[:, :],
                     ALU.mult, ALU.mult, r=[self.r_h[k][b], self.spm_t.r, rstd.r], w=[nT.sub[k]])

    def rmsnorm(self, l, goff, b, nT, sq, rstd):
        ps = self.bank()
        for k in range(KT):
            c = self.hcol(k, b)
            s = sq[k % len(sq)]
            self.act(s.h[:, :], self.hT[:, c:c + TB], AF.Square, r=[self.r_h[k][b]], w=[s.r])
            self.mm(ps.h[:, :], self.ones1024, s.h[:, :], k == 0, k == KT - 1, r=[s.r, self.cbf.r], w=[ps.r])
        self.rstd_from(ps, TB, rstd)
        for k in range(KT):
            c = self.hcol(k, b)
            eng = "dve"
            self.stt(eng, nT.h[:, k * TB:(k + 1) * TB], self.hT[:, c:c + TB], self.sp_col(l, goff + k), rstd.h[:, :],
                     ALU.mult, ALU.mult, r=[self.r_h[k][b], self.spm_t.r, rstd.r], w=[nT.sub[k]])


    def phase1(self, l):
        A, S = self.A, self.S
        m0 = A.mark()
        self.ring_init(8)
        nTs = [T(A.alloc("nT", [128, KT * TB], BF16), "nT%d" % i, KT) for i in range(2)]
        sq8 = T(A.alloc("sq8", [128, KT * TB], BF16), "sq8", KT)
        rstd = T(A.alloc("rstd", [128, TB], F32), "rstd")
        sqk = [T(A.alloc("sqk", [128, TB], BF16), "sqk%d" % i) for i in range(2)]
        rs2 = [T(A.alloc("rs2", [128, TB], F32), "rs2%d" % i) for i in range(2)]
        ob = [T(A.alloc("ob", [128, TB], BF16), "ob%d" % i) for i in range(4)]
        sg = [T(A.alloc("sg", [128, TB], F32), "sg%d" % i) for i in range(2)]
        onext = [0]

        def obuf():
            t = ob[onext[0]]
            onext[0] = (onext[0] + 1) % len(ob)
            return t
        self.ts("dve", self.misc.h[:, 3:4], self.sp_col(l, SP_QG), 0.125, ALU.mult, r=[self.spm_t.r], w=[self.misc.r])
        self.rmsnorm_a(0, sq8)
        self.rmsnorm_b(l, SP_G1, 0, nTs[0], sq8, rstd)
        for b in range(NB):
            nT = nTs[b % 2]

            def qk_post(mt, ps, b=b):
                i = mt % 2
                ps2 = self.bank()
                self.mm(ps2.h[:, :], self.bd64, sqk[i].h[:, :], True, True, r=[sqk[i].r, self.cbf.r], w=[ps2.r])
                self.rstd_from(ps2, TB, rs2[i])
                gain = self.misc.h[:, 3:4] if mt < 4 else self.sp_col(l, SP_KG)
                o = obuf()
                self.stt("dve", o.h[:, :], ps.h[:, :], gain, rs2[i].h[:, :], ALU.mult, ALU.mult,
                         r=[ps.r, rs2[i].r, self.misc.r, self.spm_t.r], w=[o.r])
                h = mt % 4
                dst, rr = (self.qT_d, self.r_qT) if mt < 4 else (self.kT_d, self.r_kT)
                S.dma("pool", dst[h * 128:(h + 1) * 128, b * TB:(b + 1) * TB], o.h[:, :], r=[o.r], w=[rr[h][b]])
            pend = None
            for mt in range(8):
                sl = self.slab(l, SLAB_WIN + mt)
                ps = self.bank()
                for k in range(KT):
                    self.mm(ps.h[:, :], sl.h[:, k * 128:(k + 1) * 128], nT.h[:, k * TB:(k + 1) * TB], k == 0, k == KT - 1,
                            r=[sl.r, nT.sub[k]], w=[ps.r])
                self.act(sqk[mt % 2].h[:, :], ps.h[:, :], AF.Square, r=[ps.r], w=[sqk[mt % 2].r])
                if pend is not None:
                    qk_post(*pend)
                pend = (mt, ps)
            if b + 1 < NB:
                self.rmsnorm_a(b + 1, sq8)
            vs = [self.slab(l, SLAB_WIN + 8 + j) for j in range(4)]
            for tt in range(4):
                ps = self.bank()
                for k in range(KT):
                    sl = vs[k // 2]
                    self.mm(ps.h[:, :], nT.h[:, k * TB + tt * 128:k * TB + (tt + 1) * 128], sl.h[:, (k % 2) * 512:(k % 2 + 1) * 512],
                            k == 0, k == KT - 1, r=[sl.r, nT.sub[k]], w=[ps.r])
                if tt == 0:
                    qk_post(*pend)
                o = obuf()
                self.act(o.h[:, :], ps.h[:, :], AF.Copy, r=[ps.r], w=[o.r])
                gt = b * 4 + tt
                S.dma("pool", self.v_d[gt * 128:(gt + 1) * 128, :], o.h[:, :], r=[o.r], w=[self.r_v[gt]])
            cs = [self.slab(l, SLAB_WIN + 12 + j) for j in range(4)]
            for ct in range(2):
                psa = self.bank()
                psg = self.bank()
                for k in range(KT):
                    self.mm(psa.h[:, :], cs[ct].h[:, k * 128:(k + 1) * 128], nT.h[:, k * TB:(k + 1) * TB], k == 0, k == KT - 1,
                            r=[cs[ct].r, nT.sub[k]], w=[psa.r])
                for k in range(KT):
                    self.mm(psg.h[:, :], cs[2 + ct].h[:, k * 128:(k + 1) * 128], nT.h[:, k * TB:(k + 1) * TB], k == 0, k == KT - 1,
                            r=[cs[2 + ct].r, nT.sub[k]], w=[psg.r])
                self.act(sg[ct].h[:, :], psg.h[:, :], AF.Sigmoid, r=[psg.r], w=[sg[ct].r])
                o = obuf()
                self.tt("dve", o.h[:, :], psa.h[:, :], sg[ct].h[:, :], ALU.mult, r=[psa.r, sg[ct].r], w=[o.r])
                S.dma("pool", self.hcv_d[ct * 128:(ct + 1) * 128, 32 + b * TB:32 + (b + 1) * TB], o.h[:, :], r=[o.r], w=[self.r_hcv[ct][b]])
            for ct in range(2):
                sl = self.slab(l, SLAB_WIN + 16 + ct)
                ps = self.bank()
                for k in range(KT):
                    self.mm(ps.h[:, :], sl.h[:, k * 128:(k + 1) * 128], nT.h[:, k * TB:(k + 1) * TB], k == 0, k == KT - 1,
                            r=[sl.r, nT.sub[k]], w=[ps.r])
                o = obuf()
                self.act(o.h[:, :], ps.h[:, :], AF.Copy, r=[ps.r], w=[o.r])
                S.dma("pool", self.u_d[ct * 128:(ct + 1) * 128, b * TB:(b + 1) * TB], o.h[:, :], r=[o.r], w=[self.r_u[ct][b]])
            if b + 1 < NB:
                self.rmsnorm_b(l, SP_G1, b + 1, nTs[(b + 1) % 2], sq8, rstd)
        S.barrier()
        A.reset(m0)

    def phase2(self, l):
        A, S = self.A, self.S
        m0 = A.mark()
        dg = T(A.alloc("dg", [128, 62 * 128], BF16), "dg", 62)
        idf = T(A.alloc("idf", [128, 128], F32), "idf")
        onesf = T(A.alloc("onesf", [128, 128], F32), "onesf")
        hc = [T(A.alloc("hc", [128, 2 * 544], BF16), "hc%d" % i, 2) for i in range(2)]
        cvs = [T(A.alloc("cv", [128, 2 * TB], F32), "cv%d" % i, 2) for i in range(2)]
        cens = [T(A.alloc("cen", [128, 2 * TB], F32), "cen%d" % i, 2) for i in range(2)]
        sqfs = [T(A.alloc("sqf", [128, 2 * TB], F32), "sqf%d" % i, 2) for i in range(2)]
        rss = [T(A.alloc("rs", [128, TB], F32), "rs%d" % i) for i in range(2)]
        obs = [T(A.alloc("ob", [128, TB], BF16), "ob%d" % i) for i in range(4)]
        S.dma("sp", idf.h[:, :], self.cst[:, C_ID:C_ID + 128], w=[idf.r])
        self.memset("dve", onesf.h[:, :], 1.0 / 256, w=[onesf.r])
        for ct in range(2):
            for j in range(31):
                i = ct * 31 + j
                self.ts("dve" if i % 2 else "pool", dg.h[:, i * 128:(i + 1) * 128], idf.h[:, :], self.sp_col(l, SP_CW + i), ALU.mult,
                        r=[idf.r, self.spm_t.r], w=[dg.sub[i]])
        def stage_a(b):
            h_ = hc[b % 2]
            cv = cvs[b % 2]
            for ct in range(2):
                rr = [self.r_hcv[ct][b], self.r_halo] + ([self.r_hcv[ct][b - 1]] if b else [])
                S.dma("sp", h_.h[:, ct * 544:ct * 544 + 544], self.hcv_d[ct * 128:(ct + 1) * 128, b * TB:b * TB + 544], r=rr, w=[h_.sub[ct]])
            for ct in range(2):
                ps = self.bank()
                for j in range(31):
                    i = ct * 31 + j
                    self.mm(ps.h[:, :], dg.h[:, i * 128:(i + 1) * 128], h_.h[:, ct * 544 + 2 + j:ct * 544 + 2 + j + TB], j == 0, j == 30,
                            r=[dg.sub[i], h_.sub[ct]], w=[ps.r])
                self.act(cv.h[:, ct * TB:(ct + 1) * TB], ps.h[:, :], AF.Identity, r=[ps.r, self.spm_t.r], w=[cv.sub[ct]],
                         bias=self.sp_col(l, SP_CB + ct))

        def stage_b(b):
            cv, cen, sqf, rs, ob = cvs[b % 2], cens[b % 2], sqfs[b % 2], rss[b % 2], obs[(b % 2) * 2:(b % 2) * 2 + 2]
            psm = self.bank()
            for ct in range(2):
                self.mm(psm.h[:, :], onesf.h[:, :], cv.h[:, ct * TB:(ct + 1) * TB], ct == 0, ct == 1, r=[onesf.r, cv.sub[ct]], w=[psm.r])
            for ct in range(2):
                self.tt("dve", cen.h[:, ct * TB:(ct + 1) * TB], cv.h[:, ct * TB:(ct + 1) * TB], psm.h[:, :], ALU.subtract,
                        r=[cv.sub[ct], psm.r], w=[cen.sub[ct]])
                self.act(sqf.h[:, ct * TB:(ct + 1) * TB], cen.h[:, ct * TB:(ct + 1) * TB], AF.Square, r=[cen.sub[ct]], w=[sqf.sub[ct]])
            psv = self.bank()
            for ct in range(2):
                self.mm(psv.h[:, :], onesf.h[:, :], sqf.h[:, ct * TB:(ct + 1) * TB], ct == 0, ct == 1, r=[onesf.r, sqf.sub[ct]], w=[psv.r])
            self.rstd_from(psv, TB, rs)
            for ct in range(2):
                self.tt("dve", cen.h[:, ct * TB:(ct + 1) * TB], cen.h[:, ct * TB:(ct + 1) * TB], rs.h[:, :], ALU.mult,
                        r=[rs.r, cen.sub[ct]], w=[cen.sub[ct]])
                o = ob[ct]
                self.act(o.h[:, :], cen.h[:, ct * TB:(ct + 1) * TB], AF.Silu, r=[cen.sub[ct], self.spm_t.r], w=[o.r],
                         bias=self.sp_col(l, SP_LNB + ct), scale=self.sp_col(l, SP_LNG + ct))
                S.dma("sp", self.mixT_d[512 + ct * 128:512 + (ct + 1) * 128, b * TB:(b + 1) * TB], o.h[:, :], r=[o.r], w=[self.r_mix[4 + ct][b]])
        stage_a(0)
        for b in range(NB):
            if b + 1 < NB:
                stage_a(b + 1)
            stage_b(b)
        S.barrier()
        A.reset(m0)


    def phase3(self, l):
        A, S = self.A, self.S
        m0 = A.mark()
        cosT = T(A.alloc("cosT", [128, 8 * TS], F32), "cosT", 8)
        sinT = T(A.alloc("sinT", [128, 8 * TS], F32), "sinT", 8)
        rT = T(A.alloc("rT", [128, 8 * TS], F32), "rT", 8)
        Bb = T(A.alloc("Bb", [128, 16 * 128], BF16), "Bb")
        CA = T(A.alloc("CA", [128, 8 * 128], BF16), "CA")
        CB = T(A.alloc("CB", [128, 8 * 128], BF16), "CB")
        glu = T(A.alloc("glu", [128, 1024], BF16), "glu")
        CAn = T(A.alloc("CAn", [128, 8 * 128], BF16), "CAn")
        idf = T(A.alloc("idf", [128, 256], F32), "idf")
        dD = T(A.alloc("dD", [128, 256], BF16), "dD")
        sm = T(A.alloc("sm", [128, 8 * 24], F32), "sm")
        ini = [T(A.alloc("ini", [128, 16], F32), "ini%d" % i, 8) for i in range(2)]
        m1 = A.mark()

        def col(i, k=None):
            return sm.h[:, 8 * i:8 * i + 8] if k is None else sm.h[:, 8 * i + k:8 * i + k + 1]
        (DT, TH, LR, R_, C_, S_, T1, T2, T3, ABR, ABI, DEN, INV, BRE, BIM, NBR, NBI, U1, U2, CT, ST) = range(21)
        spc = self.spm_t.r
        are, aim, ldt = self.sp_col(l, SP_ARE, 8), self.sp_col(l, SP_AIM, 8), self.sp_col(l, SP_LDT, 8)

        def sact(o, i, f, **kw):
            self.act(col(o), i, f, r=[sm.r, spc, self.misc.r], w=[sm.r], **kw)

        def stt_(o, a, b, op):
            self.tt("dve", col(o), a, b, op, r=[sm.r, spc], w=[sm.r])
        sact(DT, ldt, AF.Exp)
        stt_(TH, aim, col(DT), ALU.mult)
        stt_(LR, are, col(DT), ALU.mult)
        sact(R_, col(LR), AF.Exp)
        sact(S_, col(TH), AF.Sin, scale=1.0 / 32)
        sact(C_, col(TH), AF.Sin, scale=1.0 / 32, bias=self.halfpi_ap)

        def square_cs():
            stt_(T1, col(C_), col(C_), ALU.mult)
            stt_(T2, col(S_), col(S_), ALU.mult)
            stt_(T3, col(C_), col(S_), ALU.mult)
            stt_(C_, col(T1), col(T2), ALU.subtract)
            self.ts("dve", col(S_), col(T3), 2.0, ALU.mult, r=[sm.r], w=[sm.r])
        for _ in range(5):
            square_cs()
        stt_(ABR, col(R_), col(C_), ALU.mult)
        stt_(ABI, col(R_), col(S_), ALU.mult)
        self.ts("dve", col(ABR), col(ABR), -1.0, ALU.add, r=[sm.r], w=[sm.r])
        stt_(T1, are, are, ALU.mult)
        stt_(T2, aim, aim, ALU.mult)
        stt_(DEN, col(T1), col(T2), ALU.add)
        S.op("dve", lambda e: e.reciprocal(out=col(INV), in_=col(DEN)), r=[sm.r], w=[sm.r])
        stt_(T1, col(ABR), are, ALU.mult)
        stt_(T2, col(ABI), aim, ALU.mult)
        stt_(T3, col(T1), col(T2), ALU.add)
        stt_(BRE, col(T3), col(INV), ALU.mult)
        stt_(T1, col(ABI), are, ALU.mult)
        stt_(T2, col(ABR), aim, ALU.mult)
        stt_(T3, col(T1), col(T2), ALU.subtract)
        stt_(BIM, col(T3), col(INV), ALU.mult)
        self.ts("dve", col(NBI), col(BIM), -1.0, ALU.mult, r=[sm.r], w=[sm.r])
        tmp = T(A.alloc("tmp", [128, 2 * TS], F32), "tmp")
        onesr = T(A.alloc("onesr", [128, TS], F32), "onesr")
        Cst = T(A.alloc("Cst", [128, 2048], F32), "Cst")
        self.memset("dve", onesr.h[:, :], 1.0, w=[onesr.r])
        for k in range(8):
            self.memset("dve", cosT.h[:, k * TS:k * TS + 1], 1.0, w=[cosT.sub[k]])
            self.memset("dve", sinT.h[:, k * TS:k * TS + 1], 0.0, w=[sinT.sub[k]])
            self.act(rT.h[:, k * TS:(k + 1) * TS], onesr.h[:, :], AF.Copy, r=[onesr.r, sm.r], w=[rT.sub[k]], scale=col(R_, k))
        tb = T(A.alloc("tb", [128, 2048], F32), "tb")
        c3 = cosT.h[:, :].rearrange("p (k t) -> p k t", k=8)
        s3 = sinT.h[:, :].rearrange("p (k t) -> p k t", k=8)
        allr = [sm.r] + cosT.sub + sinT.sub
        m = 1
        while m < TS:
            cb = col(C_).rearrange("p (k o) -> p k o", o=1).broadcast_to([128, 8, m])
            sb = col(S_).rearrange("p (k o) -> p k o", o=1).broadcast_to([128, 8, m])
            t1 = tb.h[:, 0:8 * m].rearrange("p (k t) -> p k t", k=8)
            t2 = tb.h[:, 1024:1024 + 8 * m].rearrange("p (k t) -> p k t", k=8)
            pre, pim = c3[:, :, 0:m], s3[:, :, 0:m]
            self.tt("dve", t1, pre, cb, ALU.mult, r=allr, w=[tb.r])
            self.tt("dve", t2, pim, sb, ALU.mult, r=allr, w=[tb.r])
            self.tt("dve", c3[:, :, m:2 * m], t1, t2, ALU.subtract, r=[tb.r], w=cosT.sub)
            self.tt("dve", t1, pre, sb, ALU.mult, r=allr, w=[tb.r])
            self.tt("dve", t2, pim, cb, ALU.mult, r=allr, w=[tb.r])
            self.tt("dve", s3[:, :, m:2 * m], t1, t2, ALU.add, r=[tb.r], w=sinT.sub)
            square_cs()
            m *= 2
        self.cp("dve", col(CT), col(C_), r=[sm.r], w=[sm.r])
        self.cp("dve", col(ST), col(S_), r=[sm.r], w=[sm.r])
        S.dma("sp", idf.h[:, 0:128], self.cst[:, C_ID:C_ID + 128], w=[idf.r])
        self.ts("dve", idf.h[:, 128:256], idf.h[:, 0:128], -1.0, ALU.mult, r=[idf.r], w=[idf.r])
        r0 = l * 128
        S.dma("pool", Bb.h[:, :], self.ssmm[r0:r0 + 128, 0:2048], w=[Bb.r])
        S.dma("sp", Cst.h[:, :], self.ssmm[r0:r0 + 128, 2048:4096], w=[Cst.r])
        row = (l * NSLAB + SLAB_GLU) * 128
        S.dma("sp", glu.h[:, :], self.wb[row:row + 128, :], r=[self.r_wb[l][SLAB_GLU // 8]], w=[glu.r])
        for k in range(8):
            cre, cim = Cst.h[:, k * 128:(k + 1) * 128], Cst.h[:, (8 + k) * 128:(9 + k) * 128]
            self.ts("dve", tmp.h[:, 0:128], cim, col(BIM, k), ALU.mult, r=[Cst.r, sm.r], w=[tmp.r])
            self.stt("dve", CA.h[:, k * 128:(k + 1) * 128], cre, col(BRE, k), tmp.h[:, 0:128], ALU.mult, ALU.subtract, r=[Cst.r, sm.r, tmp.r], w=[CA.r])
            self.ts("dve", tmp.h[:, 128:256], cim, col(BRE, k), ALU.mult, r=[Cst.r, sm.r], w=[tmp.r])
            self.stt("dve", CB.h[:, k * 128:(k + 1) * 128], cre, col(NBI, k), tmp.h[:, 128:256], ALU.mult, ALU.subtract, r=[Cst.r, sm.r, tmp.r], w=[CB.r])
        self.ts("dve", CAn.h[:, :], CA.h[:, :], -1.0, ALU.mult, r=[CA.r], w=[CAn.r])
        for ct in range(2):
            self.ts("dve", dD.h[:, ct * 128:(ct + 1) * 128], idf.h[:, 0:128], self.sp_col(l, SP_SD + ct), ALU.mult, r=[idf.r, spc], w=[dD.r])
        S.barrier()
        A.reset(m1)
        Aw = [T(A.alloc("Aw", [128, 2 * TS], F32), "Aw%d" % i) for i in range(2)]
        Bw = [T(A.alloc("Bw", [128, 2 * TS], F32), "Bw%d" % i) for i in range(2)]
        vv = [T(A.alloc("vv", [128, 2 * TS], F32), "vv%d" % i) for i in range(2)]
        A2 = [T(A.alloc("A2", [128, 2 * TS], BF16), "A2%d" % i) for i in range(2)]
        B2 = [T(A.alloc("B2", [128, 2 * TS], BF16), "B2%d" % i) for i in range(2)]
        ub = [T(A.alloc("ub", [128, 2 * TS], BF16), "ub%d" % i, 2) for i in range(2)]
        ysb = T(A.alloc("ysb", [128, TS], F32), "ysb")
        g1 = T(A.alloc("g1", [128, TS], F32), "g1")
        g2 = T(A.alloc("g2", [128, TS], F32), "g2")
        yg = T(A.alloc("yg", [128, 2 * TS], BF16), "yg", 2)
        sgl = T(A.alloc("sgl", [128, TS], F32), "sgl")
        ob = [T(A.alloc("ob", [128, TS], BF16), "ob%d" % i) for i in range(2)]
        it = [T(A.alloc("it", [128, 2], F32), "it%d" % i) for i in range(2)]
        self.ts("dve", col(T1), col(ST), -1.0, ALU.mult, r=[sm.r], w=[sm.r])
        steps = [(c, k) for c in range(NCH) for k in range(8)]
        yps_h = [None]
        wps_h = {}
        Id, nId = idf.h[:, 0:128], idf.h[:, 128:256]

        def h2(ap):
            return ap.rearrange("p (h t) -> p h t", h=2)

        def bc2(ap):
            return ap.rearrange("p (o t) -> p o t", o=1).broadcast_to([128, 2, TS])

        bps_h = {}

        def stage0(i):
            c, k = steps[i]
            t0 = c * TS
            u_ = ub[c % 3]
            if k == 0:
                for ct in range(2):
                    S.dma("sp", u_.h[:, ct * TS:(ct + 1) * TS], self.u_d[ct * 128:(ct + 1) * 128, t0:t0 + TS], r=[self.r_u[ct][t0 // TB]], w=[u_.sub[ct]])
            ct = k // 4
            bps = self.bank()
            bps_h[i] = bps
            self.mm(bps.h[:, 0:TS], Bb.h[:, k * 128:(k + 1) * 128], u_.h[:, ct * TS:(ct + 1) * TS], True, True, r=[Bb.r, u_.sub[ct]], w=[bps.r])
            self.mm(bps.h[:, TS:2 * TS], Bb.h[:, (8 + k) * 128:(9 + k) * 128], u_.h[:, ct * TS:(ct + 1) * TS], True, True, r=[Bb.r, u_.sub[ct]], w=[bps.r])

        def stage1(i):
            c, k = steps[i]
            i2 = i % 2
            o = k * TS
            bps = bps_h.pop(i)
            a_, b_ = Aw[i2], Bw[i2]
            ck, sk = cosT.h[:, o:o + TS], sinT.h[:, o:o + TS]
            self.tt("dve", h2(a_.h[:, :]), h2(bps.h[:, :]), bc2(ck), ALU.mult, r=[bps.r, cosT.sub[k]], w=[a_.r])
            self.tt("dve", h2(b_.h[:, :]), h2(bps.h[:, :]), bc2(sk), ALU.mult, r=[bps.r, sinT.sub[k]], w=[b_.r])
            wps = self.bank()
            wps_h[i] = wps
            self.mm(wps.h[:, 0:TS], Id, a_.h[:, 0:TS], True, False, r=[idf.r, a_.r], w=[wps.r])
            self.mm(wps.h[:, 0:TS], Id, b_.h[:, TS:2 * TS], False, True, r=[idf.r, b_.r], w=[wps.r])
            self.mm(wps.h[:, TS:2 * TS], Id, a_.h[:, TS:2 * TS], True, False, r=[idf.r, a_.r], w=[wps.r])
            self.mm(wps.h[:, TS:2 * TS], nId, b_.h[:, 0:TS], False, True, r=[idf.r, b_.r], w=[wps.r])

        def stage2(i):
            c, k = steps[i]
            t0 = c * TS
            u_ = ub[c % 3]
            ct = k // 4
            i2 = i % 2
            o = k * TS
            ini_c, ini_n = ini[c % 2], ini[(c + 1) % 2]
            v_, a2, b2 = vv[i2], A2[i2], B2[i2]
            wps = wps_h.pop(i)
            ck, sk = cosT.h[:, o:o + TS], sinT.h[:, o:o + TS]
            for hf in range(2):
                init = 0.0 if c == 0 else ini_c.h[:, hf * 8 + k:hf * 8 + k + 1]
                S.op("dve", lambda e, hf=hf, v_=v_, wps=wps, init=init, o=o: e.tensor_tensor_scan(
                    out=v_.h[:, hf * TS:(hf + 1) * TS], data0=rT.h[:, o:o + TS], data1=wps.h[:, hf * TS:(hf + 1) * TS],
                    initial=init, op0=ALU.mult, op1=ALU.add), r=[rT.sub[k], wps.r] + ([ini_c.sub[k]] if c else []), w=[v_.r])
            if c < NCH - 1:
                vre, vim = v_.h[:, TS - 1:TS], v_.h[:, 2 * TS - 1:2 * TS]
                it_ = it[i2]
                self.act(it_.h[:, 0:1], vim, AF.Copy, r=[v_.r, sm.r], w=[it_.r], scale=col(T1, k))
                self.act(it_.h[:, 1:2], vre, AF.Copy, r=[v_.r, sm.r], w=[it_.r], scale=col(ST, k))
                self.act(ini_n.h[:, k:k + 1], vre, AF.Identity, r=[v_.r, sm.r, it_.r], w=[ini_n.sub[k]], scale=col(CT, k), bias=it_.h[:, 0:1])
                self.act(ini_n.h[:, 8 + k:9 + k], vim, AF.Identity, r=[v_.r, sm.r, it_.r], w=[ini_n.sub[k]], scale=col(CT, k), bias=it_.h[:, 1:2])
            self.tt("pool", h2(a2.h[:, :]), h2(v_.h[:, :]), bc2(ck), ALU.mult, r=[v_.r, cosT.sub[k]], w=[a2.r])
            self.tt("pool", h2(b2.h[:, :]), h2(v_.h[:, :]), bc2(sk), ALU.mult, r=[v_.r, sinT.sub[k]], w=[b2.r])
            yps = self.psum[6 + ct]
            kk = slice(k * 128, (k + 1) * 128)
            self.mm(yps.h[:, 0:TS], CA.h[:, kk], a2.h[:, 0:TS], k % 4 == 0, False, r=[CA.r, a2.r], w=[yps.r])
            self.mm(yps.h[:, 0:TS], CAn.h[:, kk], b2.h[:, TS:2 * TS], False, False, r=[CAn.r, b2.r], w=[yps.r])
            self.mm(yps.h[:, 0:TS], CB.h[:, kk], b2.h[:, 0:TS], False, False, r=[CB.r, b2.r], w=[yps.r])
            self.mm(yps.h[:, 0:TS], CB.h[:, kk], a2.h[:, TS:2 * TS], False, False, r=[CB.r, a2.r], w=[yps.r])
            if k % 4 == 3:
                self.mm(yps.h[:, 0:TS], dD.h[:, ct * 128:(ct + 1) * 128], u_.h[:, ct * TS:(ct + 1) * TS], False, True, r=[dD.r, u_.sub[ct]], w=[yps.r])

        def epilogue(i):
            c, k = steps[i]
            t0 = c * TS
            ct = k // 4
            yps = self.psum[6 + ct]
            self.act(yg.h[:, ct * TS:(ct + 1) * TS], yps.h[:, 0:TS], AF.Gelu_apprx_tanh, r=[yps.r], w=[yg.sub[ct]])
            if k == 7:
                for mt in range(2):
                    zps = self.psum[5]
                    for kt in range(2):
                        a = mt * 2 + kt
                        self.mm(zps.h[:, mt * TS:(mt + 1) * TS], glu.h[:, a * 128:(a + 1) * 128], yg.h[:, kt * TS:(kt + 1) * TS], kt == 0, kt == 1, r=[glu.r, yg.sub[kt]], w=[zps.r])
                    self.act(sgl.h[:, :], zps.h[:, mt * TS:(mt + 1) * TS], AF.Sigmoid, r=[zps.r, spc], w=[sgl.r], bias=self.sp_col(l, SP_GB + mt))
                    o_ = ob[mt]
                    self.tt("pool", o_.h[:, :], yg.h[:, mt * TS:(mt + 1) * TS], sgl.h[:, :], ALU.mult, r=[yg.sub[mt], sgl.r], w=[o_.r])
                    S.dma("sp", self.mixT_d[768 + mt * 128:768 + (mt + 1) * 128, t0:t0 + TS], o_.h[:, :], r=[o_.r], w=[self.r_mix[6 + mt][t0 // TB]])
        ub.append(T(A.alloc("ub", [128, 2 * TS], BF16), "ub2", 2))
        n = len(steps)
        self.nrot = 5
        self.pnext = 0
        stage0(0)
        stage0(1)
        stage1(0)
        for i in range(n):
            if i + 2 < n:
                stage0(i + 2)
            if i + 1 < n:
                stage1(i + 1)
            stage2(i)
            if i >= 2 and steps[i - 2][1] % 4 == 3:
                epilogue(i - 2)
        for i in (n - 2, n - 1):
            if steps[i][1] % 4 == 3:
                epilogue(i)
        self.nrot = 8
        S.barrier()
        A.reset(m0)

    def phase4(self, l):
        A, S = self.A, self.S
        m0 = A.mark()
        lam_init = 0.8 - 0.6 * math.exp(-0.3 * l)
        QTh = T(A.alloc("QTh", [128, NT], BF16), "QTh", NB)
        KTh = T(A.alloc("KTh", [128, NT], BF16), "KTh", NB)
        Vh = T(A.alloc("Vh", [128, NT], BF16), "Vh", 32)
        bct = T(A.alloc("bct", [128, 4 * 36], F32), "bct")
        pt = [T(A.alloc("pt", [128, TB], BF16), "pt%d" % i) for i in range(4)]
        fin = [T(A.alloc("fin", [128, TB], F32), "fin%d" % i) for i in range(5)]
        sqa = T(A.alloc("sqa", [128, TB], BF16), "sqa")
        rs = T(A.alloc("rs", [128, TB], F32), "rs")
        ob = [T(A.alloc("ob", [128, TB], BF16), "ob%d" % i) for i in range(2)]
        lt = T(A.alloc("lt", [128, 128], F32), "lt")
        mc = self.misc
        S.dma("sp", bct.h[:, :], self.bctd, w=[bct.r])
        for j in range(2):
            c = l * SP_N + SP_LAM + 128 * j
            self.tt("dve", lt.h[:, 64 * j:64 * j + 64], self.spm_t.h[:, c:c + 64], self.spm_t.h[:, c + 64:c + 128], ALU.mult,
                    r=[self.spm_t.r], w=[lt.r])
            S.op("dve", lambda e, j=j: e.reduce_sum(out=mc.h[:, 8 + j:9 + j], in_=lt.h[:, 64 * j:64 * j + 64], axis=mybir.AxisListType.X),
                 r=[lt.r], w=[mc.r])
        self.act(mc.h[:, 8:10], mc.h[:, 8:10], AF.Exp, r=[mc.r], w=[mc.r])
        self.tt("dve", mc.h[:, 5:6], mc.h[:, 9:10], mc.h[:, 8:9], ALU.subtract, r=[mc.r], w=[mc.r])
        self.ts("dve", mc.h[:, 5:6], mc.h[:, 5:6], -lam_init, ALU.add, r=[mc.r], w=[mc.r])
        self.ts("dve", mc.h[:, 6:7], self.sp_col(l, SP_OG), 1.0 - lam_init, ALU.mult, r=[self.spm_t.r], w=[mc.r])
        neg_lam = mc.h[:, 5:6]
        ogs = mc.h[:, 6:7]
        O = [self.psum[0], self.psum[1]]
        Ls = [self.psum[2], self.psum[3]]
        sc = [self.psum[4], self.psum[5], self.psum[6], self.psum[7]]
        vd = self.v_d.rearrange("(t p) c -> p t c", p=128)
        pend = [None]
        raw = [T(A.alloc("raw", [128, TB], F32), "raw%d" % i) for i in range(4)]

        def fin1():
            for c in range(2):
                self.cp("dve", raw[c].h[:, :], O[c].h[:, :], r=[O[c].r], w=[raw[c].r])
                self.act(raw[2 + c].h[:, :], Ls[c].h[:, :], AF.Copy, r=[Ls[c].r], w=[raw[2 + c].r])

        def fin2a():
            for c in range(2):
                S.op("dve", lambda e, c=c: e.reciprocal(out=fin[c].h[:, :], in_=raw[2 + c].h[:, :]), r=[raw[2 + c].r], w=[fin[c].r])
                self.tt("dve", fin[2 + c].h[:, :], raw[c].h[:, :], fin[c].h[:, :], ALU.mult, r=[raw[c].r, fin[c].r], w=[fin[2 + c].r])
            a = fin[4]
            self.stt("dve", a.h[:, :], fin[3].h[:, :], neg_lam, fin[2].h[:, :], ALU.mult, ALU.add, r=[fin[2].r, fin[3].r, mc.r], w=[a.r])
            self.act(sqa.h[:, :], a.h[:, :], AF.Square, r=[a.r], w=[sqa.r])

        def fin2b(hq, psn):
            h_, qb_ = hq
            a = fin[4]
            self.mm(psn.h[:, :], self.ones128, sqa.h[:, :], True, True, r=[sqa.r, self.cbf.r], w=[psn.r])
            self.rstd_from(psn, TB, rs)
            o = ob[qb_ % 2]
            self.stt("dve", o.h[:, :], a.h[:, :], ogs, rs.h[:, :], ALU.mult, ALU.mult, r=[a.r, rs.r, mc.r], w=[o.r])
            S.dma("pool", self.mixT_d[h_ * 128:(h_ + 1) * 128, qb_ * TB:(qb_ + 1) * TB], o.h[:, :], r=[o.r], w=[self.r_mix[h_][qb_]])
        for h in range(4):
            G = 256 if h == 0 else 512
            for b in range(NB):
                S.dma("pool", QTh.h[:, b * TB:(b + 1) * TB], self.qT_d[h * 128:(h + 1) * 128, b * TB:(b + 1) * TB], r=[self.r_qT[h][b]], w=[QTh.sub[b]])
                S.dma("pool", KTh.h[:, b * TB:(b + 1) * TB], self.kT_d[h * 128:(h + 1) * 128, b * TB:(b + 1) * TB], r=[self.r_kT[h][b]], w=[KTh.sub[b]])
            for g in range(4):
                S.dma("pool", Vh.h[:, g * 1024:(g + 1) * 1024].rearrange("p (t d) -> p t d", d=128), vd[:, g * 8:(g + 1) * 8, h * 128:(h + 1) * 128],
                      r=self.r_v[g * 8:(g + 1) * 8], w=Vh.sub[g * 8:(g + 1) * 8])
            steps = [(qb, kt) for qb in range(NB) for kt in range(4 * qb + 4)]
            GN = len(steps)

            def geom(g):
                qb, kt = steps[g]
                jl = kt - 4 * qb
                return qb, kt, jl, (128 * jl if jl >= 0 else 0), 4 * qb + 4

            def emitS(g):
                qb, kt, jl, c0, nk = geom(g)
                q0, k0 = qb * TB, kt * 128
                for c in range(2):
                    Sb = sc[(g % 2) * 2 + c]
                    pr = slice(64 * c, 64 * c + 64)
                    self.mm(Sb.h[:, c0:TB], KTh.h[pr, k0:k0 + 128], QTh.h[pr, q0 + c0:q0 + TB], True, jl < 0,
                            r=[KTh.sub[kt // 4], QTh.sub[qb]], w=[Sb.r])
                    if jl >= 0:
                        self.mm(Sb.h[:, c0:c0 + 128], self.ident_bf, self.dh(h), False, True, r=[self.cbf.r], w=[Sb.r])

            def emitE(g):
                qb, kt, jl, c0, nk = geom(g)
                q0 = qb * TB
                for c in range(2):
                    Sb = sc[(g % 2) * 2 + c]
                    p_ = pt[(g % 2) * 2 + c]
                    for gi in range(TB // G):
                        lo, hi = max(c0, gi * G), (gi + 1) * G
                        if lo >= hi:
                            continue
                        mprime = (q0 + gi * G - kt * 128) // 128
                        bcol = bct.h[:, h * 36 + mprime + 3:h * 36 + mprime + 4]
                        self.act(p_.h[:, lo:hi], Sb.h[:, lo:hi], AF.Exp, r=[Sb.r, bct.r], w=[p_.r], bias=bcol)

            def emitAV(g):
                qb, kt, jl, c0, nk = geom(g)
                for c in range(2):
                    p_ = pt[(g % 2) * 2 + c]
                    self.mm(O[c].h[:, c0:TB], Vh.h[:, kt * 128:(kt + 1) * 128], p_.h[:, c0:TB], kt == 0, kt == nk - 1,
                            r=[Vh.sub[kt], p_.r], w=[O[c].r])
                    self.mm(Ls[c].h[:, c0:TB], self.ones1, p_.h[:, c0:TB], kt == 0, kt == nk - 1, r=[self.cbf.r, p_.r], w=[Ls[c].r])
            emitS(0)
            emitS(1)
            for g in range(GN):
                qb, kt, jl, c0, nk = geom(g)
                emitE(g)
                if pend[0] is not None and kt == 1:
                    fin2a()
                if pend[0] is not None and kt == min(nk - 1, 10):
                    fin2b(pend[0], sc[(g % 2) * 2])
                    pend[0] = None
                if g + 2 < GN:
                    emitS(g + 2)
                emitAV(g)
                if kt == nk - 1:
                    fin1()
                    pend[0] = (h, qb)
            fin2a()
            fin2b(pend[0], sc[0])
            pend[0] = None
        S.barrier()
        A.reset(m0)

    def phase5(self, l):
        A, S = self.A, self.S
        m0 = A.mark()
        self.ring_init(6)
        mixb = [T(A.alloc("mixb", [128, KT * TB], BF16), "mixb%d" % i, KT) for i in range(1)]
        nT = T(A.alloc("nT", [128, KT * TB], BF16), "nT", KT)
        hid = T(A.alloc("hid", [128, 32 * TB], BF16), "hid", 32)
        sq = [T(A.alloc("sq", [128, TB], BF16), "sq%d" % i) for i in range(2)]
        rstd = T(A.alloc("rstd", [128, TB], F32), "rstd")
        rl = [T(A.alloc("rl", [128, TB], F32), "rl%d" % i) for i in range(2)]
        pTb = T(A.alloc("pTb", [128, 2 * TB], BF16), "pTb", 2)
        for b in range(NB):
            mb = mixb[0]
            for k in range(KT):
                S.dma("pool", mb.h[:, k * TB:(k + 1) * TB], self.mixT_d[k * 128:(k + 1) * 128, b * TB:(b + 1) * TB],
                      r=[self.r_mix[k][b]], w=[mb.sub[k]])
            for mt in range(KT):
                sl = self.slab(l, SLAB_WOUT + mt)
                ps = self.bank()
                for k in range(KT):
                    self.mm(ps.h[:, :], sl.h[:, k * 128:(k + 1) * 128], mb.h[:, k * TB:(k + 1) * TB], k == 0, k == KT - 1,
                            r=[sl.r, mb.sub[k]], w=[ps.r])
                c = self.hcol(mt, b)
                self.tt("dve", self.hT[:, c:c + TB], self.hT[:, c:c + TB], ps.h[:, :], ALU.add, r=[ps.r], w=[self.r_h[mt][b]])
            self.rmsnorm(l, SP_G2, b, nT, sq, rstd)
            for ft in range(32):
                sl = self.slab(l, SLAB_W1 + ft)
                ps = self.bank()
                for k in range(KT):
                    self.mm(ps.h[:, :], sl.h[:, k * 128:(k + 1) * 128], nT.h[:, k * TB:(k + 1) * TB], k == 0, k == KT - 1,
                            r=[sl.r, nT.sub[k]], w=[ps.r])
                r_ = rl[ft % 2]
                self.act(r_.h[:, :], ps.h[:, :], AF.Relu, r=[ps.r], w=[r_.r])
                self.tt("pool" if ft % 2 else "dve", hid.h[:, ft * TB:(ft + 1) * TB], r_.h[:, :], r_.h[:, :], ALU.mult, r=[r_.r], w=[hid.sub[ft]])
            for mt in range(KT):
                ps = self.bank()
                for q in range(4):
                    sl = self.slab(l, SLAB_W2 + mt * 4 + q)
                    for k in range(8):
                        f = q * 8 + k
                        self.mm(ps.h[:, :], sl.h[:, k * 128:(k + 1) * 128], hid.h[:, f * TB:(f + 1) * TB], f == 0, f == 31,
                                r=[sl.r, hid.sub[f]], w=[ps.r])
                c = self.hcol(mt, b)
                self.tt("dve", self.hT[:, c:c + TB], self.hT[:, c:c + TB], ps.h[:, :], ALU.add, r=[ps.r], w=[self.r_h[mt][b]])
            self.rmsnorm(l, SP_G3, b, nT, sq, rstd)
            for k in range(2):
                r0 = l * 256 + k * 128
                S.dma("pool", pTb.h[:, k * TB:(k + 1) * TB], self.pT[r0:r0 + 128, b * TB:(b + 1) * TB], w=[pTb.sub[k]])
            for mt in range(KT):
                if mt % 4 == 0:
                    pls = self.slab(l, SLAB_PLE + mt // 4)
                sl = self.slab(l, SLAB_GATE + mt)
                psg = self.bank()
                for k in range(KT):
                    self.mm(psg.h[:, :], sl.h[:, k * 128:(k + 1) * 128], nT.h[:, k * TB:(k + 1) * TB], k == 0, k == KT - 1,
                            r=[sl.r, nT.sub[k]], w=[psg.r])
                psp = self.bank()
                for k in range(2):
                    a = (mt % 4) * 2 + k
                    self.mm(psp.h[:, :], pls.h[:, a * 128:(a + 1) * 128], pTb.h[:, k * TB:(k + 1) * TB], k == 0, k == 1,
                            r=[pls.r, pTb.sub[k]], w=[psp.r])
                r_ = rl[mt % 2]
                self.act(r_.h[:, :], psg.h[:, :], AF.Sigmoid, r=[psg.r], w=[r_.r])
                self.tt("dve", r_.h[:, :], r_.h[:, :], psp.h[:, :], ALU.mult, r=[psp.r, r_.r], w=[r_.r])
                c = self.hcol(mt, b)
                self.tt("pool", self.hT[:, c:c + TB], self.hT[:, c:c + TB], r_.h[:, :], ALU.add, r=[r_.r], w=[self.r_h[mt][b]])
        S.barrier()
        A.reset(m0)

    def epilogue(self):
        S = self.S
        for k in range(KT):
            for b in range(0, NB, 2):
                c = self.hcol(k, b)
                S.dma("sp", self.outT[k * 128:(k + 1) * 128, b * TB:(b + 2) * TB], self.hT[:, c:c + 2 * TB],
                      r=[self.r_h[k][b], self.r_h[k][b + 1]], w=[self.r_dummy])
        S.barrier()

    def dump(self, name, src):
        if name in self.dbg:
            self.S.barrier()
            self.S.dma("sp", self.dbg[name], src, w=[self.r_dummy])
            self.S.barrier()

    def mark(self, name):
        self.marks.append((name, dict(self.S.cnt)))

    def build(self):
        self.marks = []
        self.prologue()
        for l in range(self.n_layers):
            ph = self.phases
            self.mark("L%d.P1" % l)
            if ph is None or 1 in ph:
                self.phase1(l)
            self.mark("L%d.P2" % l)
            if l + 1 < self.n_layers:
                self.cast_weights(l + 1)
            if ph is None or 2 in ph:
                self.phase2(l)
            self.mark("L%d.P3" % l)
            if ph is None or 3 in ph:
                self.phase3(l)
            self.mark("L%d.P4" % l)
            if ph is None or 4 in ph:
                self.phase4(l)
            self.mark("L%d.P5" % l)
            if l == 0:
                self.dump("d_qT", self.qT_d)
                self.dump("d_kT", self.kT_d)
                self.dump("d_v", self.v_d)
                self.dump("d_mix", self.mixT_d)
            if ph is None or 5 in ph:
                self.phase5(l)
        self.mark("END")
        self.epilogue()
        self.S.emit()
        return self.nc


def _tiles(W, order):
    return [W[kt * 128:(kt + 1) * 128, mt * 128:(mt + 1) * 128] for (kt, mt) in order]


def pack_weights(inp):
    out = np.zeros((L, NSLAB, 128, 8, 128), np.float32)
    for l in range(L):
        tl = []
        w_in = inp["w_in"][l]
        for grp0, nmt in ((0, 4), (4, 4)):
            tl += _tiles(w_in, [(kt, grp0 + mt) for mt in range(nmt) for kt in range(8)])
        tl += _tiles(w_in, [(kt, 8 + n) for kt in range(8) for n in range(4)])
        tl += _tiles(w_in, [(kt, 12 + mt) for mt in range(4) for kt in range(8)])
        tl += _tiles(w_in, [(kt, 16 + mt) for mt in range(2) for kt in range(8)])
        tl += _tiles(inp["w_out"][l], [(kt, mt) for mt in range(8) for kt in range(8)])
        tl += _tiles(inp["w_ff1"][l], [(kt, mt) for mt in range(32) for kt in range(8)])
        tl += _tiles(inp["w_ff2"][l], [(kt, mt) for mt in range(8) for kt in range(32)])
        tl += _tiles(inp["w_ple_gate"][l], [(kt, mt) for mt in range(8) for kt in range(8)])
        tl += _tiles(inp["w_ple_proj"][l], [(kt, mt) for mt in range(8) for kt in range(2)])
        tl += _tiles(inp["ssm_glu_w"][l], [(kt, mt) for mt in range(2) for kt in range(2)])
        for i, t in enumerate(tl):
            out[l, i // 8, :, i % 8, :] = t
    return out.reshape(L * NSLAB * 128, 1024)


def pack_small(inp):
    sp = np.zeros((128, L, SP_N), np.float32)
    p = np.arange(128)
    for l in range(L):
        for off, key in ((SP_G1, "mix_norm"), (SP_G2, "mlp_norm"), (SP_G3, "ple_norm")):
            sp[:, l, off:off + 8] = inp[key][l].reshape(8, 128).T
        sp[:, l, SP_QG] = inp["q_norm"][l][p % 64]
        sp[:, l, SP_KG] = inp["k_norm"][l][p % 64]
        for i, key in enumerate(("lam_q1", "lam_k1", "lam_q2", "lam_k2")):
            sp[:, l, SP_LAM + 64 * i:SP_LAM + 64 * (i + 1)] = inp[key][l][None, :]
        sp[:, l, SP_OG] = inp["att_out_norm"][l]
        for ct in range(2):
            sp[:, l, SP_CW + ct * 31:SP_CW + (ct + 1) * 31] = inp["conv_w"][l][:, ct * 128:(ct + 1) * 128].T
        sp[:, l, SP_CB:SP_CB + 2] = inp["conv_b"][l].reshape(2, 128).T
        sp[:, l, SP_LNG:SP_LNG + 2] = inp["conv_ln_g"][l].reshape(2, 128).T
        sp[:, l, SP_LNB:SP_LNB + 2] = inp["conv_ln_b"][l].reshape(2, 128).T
        for k in range(8):
            g = 2 * k + p // 64
            sp[:, l, SP_ARE + k] = inp["ssm_a_re"][l][g, p % 64]
            sp[:, l, SP_AIM + k] = inp["ssm_a_im"][l][g, p % 64]
            sp[:, l, SP_LDT + k] = inp["ssm_log_dt"][l][g]
        sp[:, l, SP_SD:SP_SD + 2] = inp["ssm_d"][l].reshape(2, 128).T
        sp[:, l, SP_GB:SP_GB + 2] = inp["ssm_glu_b"][l].reshape(2, 128).T
    return sp.reshape(128, L * SP_N)


def pack_ssm(inp):
    out = np.zeros((L, 128, 32, 128), np.float32)
    for l in range(L):
        for k in range(8):
            for half in range(2):
                g = 2 * k + half
                chl = (g % 8) * 16
                rows = slice(half * 64, half * 64 + 64)
                out[l, chl:chl + 16, k, rows] = inp["ssm_b_re"][l][g].T
                out[l, chl:chl + 16, 8 + k, rows] = inp["ssm_b_im"][l][g].T
                out[l, rows, 16 + k, chl:chl + 16] = inp["ssm_c_re"][l][g].T
                out[l, rows, 24 + k, chl:chl + 16] = inp["ssm_c_im"][l][g].T
    return out.reshape(L * 128, 4096)


def make_consts():
    cst = np.zeros((128, C_N), np.float32)
    cst[:, C_ID:C_ID + 128] = np.eye(128, dtype=np.float32)
    p = np.arange(128)
    cst[:, C_BD:C_BD + 128] = (p[:, None] // 64 == p[None, :] // 64).astype(np.float32) / 64.0
    k = p[:, None]
    q = p[None, :]
    for h in range(4):
        same = (k // 64) == (q // 64)
        cst[:, C_DH + 128 * h:C_DH + 128 * (h + 1)] = np.where(k <= q, 0.0, np.where(same, -2.0 * SLOPES[h] * (k - q), -30000.0))
    bct = np.zeros((128, 4, 36), np.float32)
    for h in range(4):
        for mi in range(36):
            bct[:, h, mi] = SLOPES[h] * (p - 128.0 * (mi - 3))
    return cst, bct.reshape(128, 144)


_NC_CACHE = {}


def kernel(**inp):
    inp = {k: np.asarray(v) for k, v in inp.items()}
    key = "full"
    if key not in _NC_CACHE:
        _NC_CACHE[key] = Builder().build()
    nc = _NC_CACHE[key]
    wf = pack_weights(inp)
    spm = pack_small(inp)
    ssmm = pack_ssm(inp)
    cst, bct = make_consts()
    in_maps = []
    for c in range(8):
        in_maps.append({
            "xT": np.ascontiguousarray(inp["x"][c].T),
            "pT": np.ascontiguousarray(inp["p"][:, c].transpose(0, 2, 1)).reshape(L * 256, NT),
            "wf": wf, "spm": spm, "ssmm": ssmm, "cst": cst, "bct": bct,
        })
    res = run_bass_kernel_spmd(nc, in_maps, core_ids=list(range(8)))
    out = np.stack([np.ascontiguousarray(res.results[c]["outT"].T) for c in range(8)], axis=0)
    return out.astype(np.float32)
```

```python
import math
import numpy as np
import concourse.bass as bass
import concourse.mybir as mybir
from concourse.bass_utils import run_bass_kernel_spmd

F32 = mybir.dt.float32
BF16 = mybir.dt.bfloat16
ALU = mybir.AluOpType
AF = mybir.ActivationFunctionType

L = 4
D = 1024
NT = 4096
TB = 512
NB = NT // TB
KT = D // 128
EPS = 1e-6
NSLAB = 101
SLAB_WIN, SLAB_WOUT, SLAB_W1, SLAB_W2, SLAB_GATE, SLAB_PLE, SLAB_GLU = 0, 18, 26, 58, 90, 98, 100
TS = 256
NCH = NT // TS
SLOPES = [2.0 ** (-8.0 * (i + 1) / 4) for i in range(4)]

SP_G1, SP_G2, SP_G3, SP_QG, SP_KG, SP_LAM, SP_OG, SP_CW, SP_CB, SP_LNG, SP_LNB, SP_ARE, SP_AIM, SP_LDT, SP_SD, SP_GB = (
    0, 8, 16, 24, 25, 26, 282, 283, 345, 347, 349, 351, 359, 367, 375, 377)
SP_N = 379
C_ID, C_BD, C_DH = 0, 128, 256
C_N = 768


class Res:
    __slots__ = ("name", "w", "rs")

    def __init__(self, name):
        self.name = name
        self.w = None
        self.rs = {}


class Sched:
    NDMA = 12

    def __init__(self, nc):
        self.nc = nc
        self.engs = {"pe": nc.tensor, "act": nc.scalar, "dve": nc.vector, "pool": nc.gpsimd, "sp": nc.sync}
        self.sem = {e: nc.alloc_semaphore("sem_" + e) for e in self.engs}
        self.cnt = {e: 0 for e in self.engs}
        self.prog = {e: [] for e in self.engs}
        self.known = {e: {} for e in self.engs}
        self.dsem = {}
        self.dcnt = {}
        self.dnext = {}
        for e in ("sp", "pool", "act"):
            for i in range(self.NDMA):
                k = ("dma", e, i)
                self.dsem[k] = nc.alloc_semaphore("dsem_%s_%d" % (e, i))
                self.dcnt[k] = 0
            self.dnext[e] = 0
        self.n_ops = 0

    def _semof(self, k):
        return self.sem[k] if isinstance(k, str) else self.dsem[k]

    def _deps(self, e, r, w, extra=()):
        deps = {}

        def need(ev):
            if ev is None:
                return
            k, v = ev
            if deps.get(k, 0) < v:
                deps[k] = v
        for x in r:
            ev = x.w
            if ev is not None and not (ev[0] == e and e == "pe"):
                need(ev)
        for x in w:
            ev = x.w
            if ev is not None and not (ev[0] == e and e == "pe"):
                need(ev)
            for k, v in x.rs.items():
                if not (k == e and e == "pe"):
                    need((k, v))
        for ev in extra:
            need(ev)
        waits = []
        kn = self.known[e]
        for k, v in deps.items():
            if kn.get(k, 0) < v:
                kn[k] = v
                waits.append((k, v))
        return waits

    def _commit(self, ev, r, w):
        k, v = ev
        for x in r:
            if x.rs.get(k, 0) < v:
                x.rs[k] = v
        for x in w:
            x.w = ev
            x.rs = {}

    def op(self, e, fn, r=(), w=()):
        waits = self._deps(e, r, w)
        self.cnt[e] += 1
        ev = (e, self.cnt[e])
        self.prog[e].append((waits, fn, ev, 1))
        self._commit(ev, r, w)
        self.n_ops += 1
        return ev

    def dma(self, e, out_ap, in_ap, r=(), w=()):
        i = self.dnext[e]
        self.dnext[e] = (i + 1) % self.NDMA
        k = ("dma", e, i)
        prev = (k, self.dcnt[k]) if self.dcnt[k] else None
        waits = self._deps(e, r, w, extra=[prev] if prev else [])
        self.dcnt[k] += 16
        ev = (k, self.dcnt[k])
        self.prog[e].append((waits, (lambda eng, o=out_ap, i_=in_ap: eng.dma_start(out=o, in_=i_)), ev, 16))
        self._commit(ev, r, w)
        self.n_ops += 1
        return ev

    def barrier(self):
        allev = [(e, c) for e, c in self.cnt.items() if c] + [(k, c) for k, c in self.dcnt.items() if c]
        for e in self.engs:
            waits = []
            kn = self.known[e]
            for k, v in allev:
                if k == e:
                    continue
                if kn.get(k, 0) < v:
                    kn[k] = v
                    waits.append((k, v))
            if waits:
                self.prog[e].append((waits, None, None, 0))

    def emit(self):
        nc = self.nc
        sched = self

        def run(eng, e):
            for waits, fn, ev, inc in sched.prog[e]:
                for k, v in waits:
                    eng.wait_ge(sched._semof(k), v)
                if fn is not None:
                    ins = fn(eng)
                    ins.then_inc(sched._semof(ev[0]), inc)
        with nc.Block() as block:
            @block.sync
            def _(eng):
                run(eng, "sp")

            @block.tensor
            def _(eng):
                run(eng, "pe")

            @block.scalar
            def _(eng):
                run(eng, "act")

            @block.vector
            def _(eng):
                run(eng, "dve")

            @block.gpsimd
            def _(eng):
                run(eng, "pool")


class Arena:
    def __init__(self, nc, base, limit):
        self.nc, self.base, self.limit, self.top, self.n = nc, base, limit, base, 0

    def alloc(self, name, shape, dt):
        esz = 2 if dt == BF16 else 4
        nbytes = int(np.prod(shape[1:])) * esz
        off = (self.top + 31) // 32 * 32
        assert off + nbytes <= self.limit, "SBUF arena overflow: %s %d" % (name, off + nbytes - self.limit)
        self.top = off + nbytes
        self.n += 1
        return self.nc.alloc_sbuf_tensor_at("%s_%d" % (name, self.n), list(shape), dt, offset=off)

    def mark(self):
        return self.top

    def reset(self, m):
        self.top = m


class T:
    def __init__(self, h, name, n=0):
        self.h = h
        self.r = Res(name)
        self.sub = [Res("%s.%d" % (name, i)) for i in range(n)]


class Builder:
    def __init__(self, n_layers=L, debug=None, phases=None):
        self.n_layers = n_layers
        self.debug = debug or ()
        self.phases = phases
        nc = self.nc = bass.Bass("TRN2", target_bir_lowering=False)
        self.S = Sched(nc)
        dt = nc.dram_tensor
        self.xT = dt("xT", [D, NT], F32, kind="ExternalInput").ap()
        self.pT = dt("pT", [L * 256, NT], F32, kind="ExternalInput").ap()
        self.wf = dt("wf", [L * NSLAB * 128, 1024], F32, kind="ExternalInput").ap()
        self.spm = dt("spm", [128, L * SP_N], F32, kind="ExternalInput").ap()
        self.ssmm = dt("ssmm", [L * 128, 4096], F32, kind="ExternalInput").ap()
        self.cst = dt("cst", [128, C_N], F32, kind="ExternalInput").ap()
        self.bctd = dt("bct", [128, 144], F32, kind="ExternalInput").ap()
        self.outT = dt("outT", [D, NT], F32, kind="ExternalOutput").ap()
        self.wb = dt("wb", [L * NSLAB * 128, 1024], BF16, kind="Internal").ap()
        self.qT_d = dt("qT_d", [512, NT], BF16, kind="Internal").ap()
        self.kT_d = dt("kT_d", [512, NT], BF16, kind="Internal").ap()
        self.v_d = dt("v_d", [NT, 512], BF16, kind="Internal").ap()
        self.mixT_d = dt("mixT_d", [D, NT], BF16, kind="ExternalInput" if "in_mix" in self.debug else "Internal").ap()
        self.hcv_d = dt("hcv_d", [256, 32 + NT], BF16, kind="Internal").ap()
        self.u_d = dt("u_d", [256, NT], BF16, kind="Internal").ap()
        self.r_hcv = [[Res("hcv") for _ in range(NB)] for _ in range(2)]
        self.r_u = [[Res("u") for _ in range(NB)] for _ in range(2)]
        self.r_halo = Res("halo")
        self.r_wb = [[Res("wb%d_%d" % (l, c)) for c in range(13)] for l in range(L)]
        self.r_qT = [[Res("qT") for _ in range(NB)] for _ in range(4)]
        self.r_kT = [[Res("kT") for _ in range(NB)] for _ in range(4)]
        self.r_v = [Res("v") for _ in range(NT // 128)]
        self.r_mix = [[Res("mix") for _ in range(NB)] for _ in range(8)]
        self.r_dummy = Res("dummy")
        self.dbg = {}
        for name, shape, d_ in (("d_qT", [512, NT], BF16), ("d_kT", [512, NT], BF16), ("d_v", [NT, 512], BF16),
                                ("d_mix", [D, NT], BF16), ("d_h", [D, NT], F32)):
            if name in self.debug:
                self.dbg[name] = dt(name, shape, d_, kind="ExternalOutput").ap()
        self.A = Arena(nc, 16640, 229376)
        A = self.A
        self.hT = A.alloc("hT", [128, KT * NT], F32)
        self.r_h = [[Res("h%d_%d" % (k, b)) for b in range(NB)] for k in range(KT)]
        self.spm_t = T(A.alloc("spm", [128, L * SP_N], F32), "spm")
        self.cbf = T(A.alloc("cbf", [128, 128 * 9], BF16), "cbf")
        self.misc = T(A.alloc("misc", [128, 16], F32), "misc")
        self.psum = [T(nc.alloc_psum_tensor("ps%d" % i, [128, 512], F32), "ps%d" % i) for i in range(8)]
        self.pnext = 0
        self.nrot = 8
        self.base_mark = A.mark()
        self.cst_t = T(A.alloc("cst", [128, C_N], F32), "cst")
        self.zt = T(A.alloc("zt", [128, 32], BF16), "zt")
        A.reset(self.base_mark)

    def bank(self):
        b = self.psum[self.pnext]
        self.pnext = (self.pnext + 1) % self.nrot
        return b

    def hcol(self, k, b):
        return k * NT + b * TB

    def sp_col(self, l, off, n=1):
        c = l * SP_N + off
        return self.spm_t.h[:, c:c + n]

    def mm(self, out_ap, lhsT, rhs, start, stop, r, w):
        return self.S.op("pe", lambda e: e.matmul(out_ap, lhsT, rhs, start=start, stop=stop), r=r, w=w)

    def act(self, out_ap, in_ap, func, r, w, bias=None, scale=None):
        kw = {}
        if bias is not None:
            kw["bias"] = bias
        if scale is not None:
            kw["scale"] = scale
        return self.S.op("act", lambda e: e.activation(out=out_ap, in_=in_ap, func=func, **kw), r=r, w=w)

    def tt(self, eng, out_ap, in0, in1, op, r, w):
        return self.S.op(eng, lambda e: e.tensor_tensor(out=out_ap, in0=in0, in1=in1, op=op), r=r, w=w)

    def ts(self, eng, out_ap, in0, s1, op0, r, w, s2=None, op1=None):
        if op1 is None:
            return self.S.op(eng, lambda e: e.tensor_scalar(out=out_ap, in0=in0, scalar1=s1, scalar2=None, op0=op0), r=r, w=w)
        return self.S.op(eng, lambda e: e.tensor_scalar(out=out_ap, in0=in0, scalar1=s1, scalar2=s2, op0=op0, op1=op1), r=r, w=w)

    def stt(self, eng, out_ap, in0, sc, in1, op0, op1, r, w):
        return self.S.op(eng, lambda e: e.scalar_tensor_tensor(out=out_ap, in0=in0, scalar=sc, in1=in1, op0=op0, op1=op1), r=r, w=w)

    def cp(self, eng, out_ap, in_ap, r, w):
        return self.S.op(eng, lambda e: e.tensor_copy(out=out_ap, in_=in_ap), r=r, w=w)

    def memset(self, eng, ap, val, w):
        return self.S.op(eng, lambda e: e.memset(ap, val), r=(), w=w)

    def ring_init(self, nslot):
        self.ring = [T(self.A.alloc("slab", [128, 1024], BF16), "slab%d" % i) for i in range(nslot)]
        self.rnext = 0

    def slab(self, l, s):
        t = self.ring[self.rnext]
        self.rnext = (self.rnext + 1) % len(self.ring)
        row = (l * NSLAB + s) * 128
        self.S.dma("sp", t.h[:, :], self.wb[row:row + 128, :], r=[self.r_wb[l][s // 8]], w=[t.r])
        return t

    def cast_weights(self, l):
        for c in range(13):
            r0 = (l * NSLAB + c * 8) * 128
            r1 = (l * NSLAB + min(NSLAB, c * 8 + 8)) * 128
            self.S.dma("pool", self.wb[r0:r1, :], self.wf[r0:r1, :], r=(), w=[self.r_wb[l][c]])

    def prologue(self):
        S = self.S
        self.cast_weights(0)
        S.dma("sp", self.cst_t.h[:, :], self.cst, w=[self.cst_t.r])
        S.dma("sp", self.spm_t.h[:, :], self.spm, w=[self.spm_t.r])
        for b in range(NB):
            for k in range(KT):
                c = self.hcol(k, b)
                S.dma("sp", self.hT[:, c:c + TB], self.xT[k * 128:(k + 1) * 128, b * TB:(b + 1) * TB], w=[self.r_h[k][b]])
        cb = self.cbf
        self.cp("dve", cb.h[:, 0:256], self.cst_t.h[:, 0:256], r=[self.cst_t.r], w=[cb.r])
        self.memset("dve", cb.h[:, 256:384], 1.0 / 1024, w=[cb.r])
        self.memset("dve", cb.h[:, 384:512], 1.0 / 128, w=[cb.r])
        self.memset("dve", cb.h[:, 512:640], 1.0, w=[cb.r])
        self.cp("dve", cb.h[:, 640:1152], self.cst_t.h[:, C_DH:C_DH + 512], r=[self.cst_t.r], w=[cb.r])
        self.memset("dve", self.misc.h[:, 0:1], EPS, w=[self.misc.r])
        self.memset("dve", self.misc.h[:, 1:2], math.pi / 2, w=[self.misc.r])
        self.memset("dve", self.misc.h[:, 2:3], 0.0, w=[self.misc.r])
        zt = self.zt
        self.memset("dve", zt.h[:, :], 0.0, w=[zt.r])
        for ct in range(2):
            S.dma("sp", self.hcv_d[ct * 128:(ct + 1) * 128, 0:32], zt.h[:, :], r=[zt.r], w=[self.r_halo])
        self.ident_bf = cb.h[:, 0:128]
        self.bd64 = cb.h[:, 128:256]
        self.ones1024 = cb.h[:, 256:384]
        self.ones128 = cb.h[:, 384:512]
        self.ones1 = cb.h[:, 512:640]
        self.eps_ap = self.misc.h[:, 0:1]
        self.halfpi_ap = self.misc.h[:, 1:2]
        S.barrier()

    def dh(self, h):
        return self.cbf.h[:, 640 + 128 * h:640 + 128 * (h + 1)]

    def rstd_from(self, ps_t, n, out_t, eps_ap=None):
        eps_ap = self.eps_ap if eps_ap is None else eps_ap
        self.act(out_t.h[:, 0:n], ps_t.h[:, 0:n], AF.Ln, r=[ps_t.r, self.misc.r], w=[out_t.r], bias=eps_ap)
        self.act(out_t.h[:, 0:n], out_t.h[:, 0:n], AF.Exp, r=[out_t.r], w=[out_t.r], scale=-0.5)

    def rmsnorm_a(self, b, sq8):
        for k in range(KT):
            c = self.hcol(k, b)
            self.act(sq8.h[:, k * TB:(k + 1) * TB], self.hT[:, c:c + TB], AF.Square, r=[self.r_h[k][b]], w=[sq8.sub[k]])

    def rmsnorm_b(self, l, goff, b, nT, sq8, rstd):
        ps = self.bank()
        for k in range(KT):
            self.mm(ps.h[:, :], self.ones1024, sq8.h[:, k * TB:(k + 1) * TB], k == 0, k == KT - 1, r=[sq8.sub[k], self.cbf.r], w=[ps.r])
        self.rstd_from(ps, TB, rstd)
        for k in range(KT):
            c = self.hcol(k, b)
            self.stt("dve", nT.h[:, k * TB:(k + 1) * TB], self.hT[:, c:c + TB], self.sp_col(l, goff + k), rstd.h[:, :],
                     ALU.mult, ALU.mult, r=[self.r_h[k][b], self.spm_t.r, rstd.r], w=[nT.sub[k]])

    def rmsnorm(self, l, goff, b, nT, sq, rstd):
        ps = self.bank()
        for k in range(KT):
            c = self.hcol(k, b)
            s = sq[k % len(sq)]
            self.act(s.h[:, :], self.hT[:, c:c + TB], AF.Square, r=[self.r_h[k][b]], w=[s.r])
            self.mm(ps.h[:, :], self.ones1024, s.h[:, :], k == 0, k == KT - 1, r=[s.r, self.cbf.r], w=[ps.r])
        self.rstd_from(ps, TB, rstd)
        for k in range(KT):
            c = self.hcol(k, b)
            eng = "dve"
            self.stt(eng, nT.h[:, k * TB:(k + 1) * TB], self.hT[:, c:c + TB], self.sp_col(l, goff + k), rstd.h[:, :],
                     ALU.mult, ALU.mult, r=[self.r_h[k][b], self.spm_t.r, rstd.r], w=[nT.sub[k]])


    def phase1(self, l):
        A, S = self.A, self.S
        m0 = A.mark()
        self.ring_init(8)
        nTs = [T(A.alloc("nT", [128, KT * TB], BF16), "nT%d" % i, KT) for i in range(2)]
        sq8 = T(A.alloc("sq8", [128, KT * TB], BF16), "sq8", KT)
        rstd = T(A.alloc("rstd", [128, TB], F32), "rstd")
        sqk = [T(A.alloc("sqk", [128, TB], BF16), "sqk%d" % i) for i in range(2)]
        rs2 = [T(A.alloc("rs2", [128, TB], F32), "rs2%d" % i) for i in range(2)]
        ob = [T(A.alloc("ob", [128, TB], BF16), "ob%d" % i) for i in range(4)]
        sg = [T(A.alloc("sg", [128, TB], F32), "sg%d" % i) for i in range(2)]
        onext = [0]

        def obuf():
            t = ob[onext[0]]
            onext[0] = (onext[0] + 1) % len(ob)
            return t
        self.ts("dve", self.misc.h[:, 3:4], self.sp_col(l, SP_QG), 0.125, ALU.mult, r=[self.spm_t.r], w=[self.misc.r])
        self.rmsnorm_a(0, sq8)
        self.rmsnorm_b(l, SP_G1, 0, nTs[0], sq8, rstd)
        for b in range(NB):
            nT = nTs[b % 2]

            def qk_post(mt, ps, b=b):
                i = mt % 2
                ps2 = self.bank()
                self.mm(ps2.h[:, :], self.bd64, sqk[i].h[:, :], True, True, r=[sqk[i].r, self.cbf.r], w=[ps2.r])
                self.rstd_from(ps2, TB, rs2[i])
                gain = self.misc.h[:, 3:4] if mt < 4 else self.sp_col(l, SP_KG)
                o = obuf()
                self.stt("dve", o.h[:, :], ps.h[:, :], gain, rs2[i].h[:, :], ALU.mult, ALU.mult,
                         r=[ps.r, rs2[i].r, self.misc.r, self.spm_t.r], w=[o.r])
                h = mt % 4
                dst, rr = (self.qT_d, self.r_qT) if mt < 4 else (self.kT_d, self.r_kT)
                S.dma("pool", dst[h * 128:(h + 1) * 128, b * TB:(b + 1) * TB], o.h[:, :], r=[o.r], w=[rr[h][b]])
            pend = None
            for mt in range(8):
                sl = self.slab(l, SLAB_WIN + mt)
                ps = self.bank()
                for k in range(KT):
                    self.mm(ps.h[:, :], sl.h[:, k * 128:(k + 1) * 128], nT.h[:, k * TB:(k + 1) * TB], k == 0, k == KT - 1,
                            r=[sl.r, nT.sub[k]], w=[ps.r])
                self.act(sqk[mt % 2].h[:, :], ps.h[:, :], AF.Square, r=[ps.r], w=[sqk[mt % 2].r])
                if pend is not None:
                    qk_post(*pend)
                pend = (mt, ps)
            if b + 1 < NB:
                self.rmsnorm_a(b + 1, sq8)
            vs = [self.slab(l, SLAB_WIN + 8 + j) for j in range(4)]
            for tt in range(4):
                ps = self.bank()
                for k in range(KT):
                    sl = vs[k // 2]
                    self.mm(ps.h[:, :], nT.h[:, k * TB + tt * 128:k * TB + (tt + 1) * 128], sl.h[:, (k % 2) * 512:(k % 2 + 1) * 512],
                            k == 0, k == KT - 1, r=[sl.r, nT.sub[k]], w=[ps.r])
                if tt == 0:
                    qk_post(*pend)
                o = obuf()
                self.act(o.h[:, :], ps.h[:, :], AF.Copy, r=[ps.r], w=[o.r])
                gt = b * 4 + tt
                S.dma("pool", self.v_d[gt * 128:(gt + 1) * 128, :], o.h[:, :], r=[o.r], w=[self.r_v[gt]])
            cs = [self.slab(l, SLAB_WIN + 12 + j) for j in range(4)]
            for ct in range(2):
                psa = self.bank()
                psg = self.bank()
                for k in range(KT):
                    self.mm(psa.h[:, :], cs[ct].h[:, k * 128:(k + 1) * 128], nT.h[:, k * TB:(k + 1) * TB], k == 0, k == KT - 1,
                            r=[cs[ct].r, nT.sub[k]], w=[psa.r])
                for k in range(KT):
                    self.mm(psg.h[:, :], cs[2 + ct].h[:, k * 128:(k + 1) * 128], nT.h[:, k * TB:(k + 1) * TB], k == 0, k == KT - 1,
                            r=[cs[2 + ct].r, nT.sub[k]], w=[psg.r])
                self.act(sg[ct].h[:, :], psg.h[:, :], AF.Sigmoid, r=[psg.r], w=[sg[ct].r])
                o = obuf()
                self.tt("dve", o.h[:, :], psa.h[:, :], sg[ct].h[:, :], ALU.mult, r=[psa.r, sg[ct].r], w=[o.r])
                S.dma("pool", self.hcv_d[ct * 128:(ct + 1) * 128, 32 + b * TB:32 + (b + 1) * TB], o.h[:, :], r=[o.r], w=[self.r_hcv[ct][b]])
            for ct in range(2):
                sl = self.slab(l, SLAB_WIN + 16 + ct)
                ps = self.bank()
                for k in range(KT):
                    self.mm(ps.h[:, :], sl.h[:, k * 128:(k + 1) * 128], nT.h[:, k * TB:(k + 1) * TB], k == 0, k == KT - 1,
                            r=[sl.r, nT.sub[k]], w=[ps.r])
                o = obuf()
                self.act(o.h[:, :], ps.h[:, :], AF.Copy, r=[ps.r], w=[o.r])
                S.dma("pool", self.u_d[ct * 128:(ct + 1) * 128, b * TB:(b + 1) * TB], o.h[:, :], r=[o.r], w=[self.r_u[ct][b]])
            if b + 1 < NB:
                self.rmsnorm_b(l, SP_G1, b + 1, nTs[(b + 1) % 2], sq8, rstd)
        S.barrier()
        A.reset(m0)

    def phase2(self, l):
        A, S = self.A, self.S
        m0 = A.mark()
        dg = T(A.alloc("dg", [128, 62 * 128], BF16), "dg", 62)
        idf = T(A.alloc("idf", [128, 128], F32), "idf")
        onesf = T(A.alloc("onesf", [128, 128], F32), "onesf")
        hc = [T(A.alloc("hc", [128, 2 * 544], BF16), "hc%d" % i, 2) for i in range(2)]
        cvs = [T(A.alloc("cv", [128, 2 * TB], F32), "cv%d" % i, 2) for i in range(2)]
        cens = [T(A.alloc("cen", [128, 2 * TB], F32), "cen%d" % i, 2) for i in range(2)]
        sqfs = [T(A.alloc("sqf", [128, 2 * TB], F32), "sqf%d" % i, 2) for i in range(2)]
        rss = [T(A.alloc("rs", [128, TB], F32), "rs%d" % i) for i in range(2)]
        obs = [T(A.alloc("ob", [128, TB], BF16), "ob%d" % i) for i in range(4)]
        S.dma("sp", idf.h[:, :], self.cst[:, C_ID:C_ID + 128], w=[idf.r])
        self.memset("dve", onesf.h[:, :], 1.0 / 256, w=[onesf.r])
        for ct in range(2):
            for j in range(31):
                i = ct * 31 + j
                self.ts("dve" if i % 2 else "pool", dg.h[:, i * 128:(i + 1) * 128], idf.h[:, :], self.sp_col(l, SP_CW + i), ALU.mult,
                        r=[idf.r, self.spm_t.r], w=[dg.sub[i]])
        def stage_a(b):
            h_ = hc[b % 2]
            cv = cvs[b % 2]
            for ct in range(2):
                rr = [self.r_hcv[ct][b], self.r_halo] + ([self.r_hcv[ct][b - 1]] if b else [])
                S.dma("sp", h_.h[:, ct * 544:ct * 544 + 544], self.hcv_d[ct * 128:(ct + 1) * 128, b * TB:b * TB + 544], r=rr, w=[h_.sub[ct]])
            for ct in range(2):
                ps = self.bank()
                for j in range(31):
                    i = ct * 31 + j
                    self.mm(ps.h[:, :], dg.h[:, i * 128:(i + 1) * 128], h_.h[:, ct * 544 + 2 + j:ct * 544 + 2 + j + TB], j == 0, j == 30,
                            r=[dg.sub[i], h_.sub[ct]], w=[ps.r])
                self.act(cv.h[:, ct * TB:(ct + 1) * TB], ps.h[:, :], AF.Identity, r=[ps.r, self.spm_t.r], w=[cv.sub[ct]],
                         bias=self.sp_col(l, SP_CB + ct))

        def stage_b(b):
            cv, cen, sqf, rs, ob = cvs[b % 2], cens[b % 2], sqfs[b % 2], rss[b % 2], obs[(b % 2) * 2:(b % 2) * 2 + 2]
            psm = self.bank()
            for ct in range(2):
                self.mm(psm.h[:, :], onesf.h[:, :], cv.h[:, ct * TB:(ct + 1) * TB], ct == 0, ct == 1, r=[onesf.r, cv.sub[ct]], w=[psm.r])
            for ct in range(2):
                self.tt("dve", cen.h[:, ct * TB:(ct + 1) * TB], cv.h[:, ct * TB:(ct + 1) * TB], psm.h[:, :], ALU.subtract,
                        r=[cv.sub[ct], psm.r], w=[cen.sub[ct]])
                self.act(sqf.h[:, ct * TB:(ct + 1) * TB], cen.h[:, ct * TB:(ct + 1) * TB], AF.Square, r=[cen.sub[ct]], w=[sqf.sub[ct]])
            psv = self.bank()
            for ct in range(2):
                self.mm(psv.h[:, :], onesf.h[:, :], sqf.h[:, ct * TB:(ct + 1) * TB], ct == 0, ct == 1, r=[onesf.r, sqf.sub[ct]], w=[psv.r])
            self.rstd_from(psv, TB, rs)
            for ct in range(2):
                self.tt("dve", cen.h[:, ct * TB:(ct + 1) * TB], cen.h[:, ct * TB:(ct + 1) * TB], rs.h[:, :], ALU.mult,
                        r=[rs.r, cen.sub[ct]], w=[cen.sub[ct]])
                o = ob[ct]
                self.act(o.h[:, :], cen.h[:, ct * TB:(ct + 1) * TB], AF.Silu, r=[cen.sub[ct], self.spm_t.r], w=[o.r],
                         bias=self.sp_col(l, SP_LNB + ct), scale=self.sp_col(l, SP_LNG + ct))
                S.dma("sp", self.mixT_d[512 + ct * 128:512 + (ct + 1) * 128, b * TB:(b + 1) * TB], o.h[:, :], r=[o.r], w=[self.r_mix[4 + ct][b]])
        stage_a(0)
        for b in range(NB):
            if b + 1 < NB:
                stage_a(b + 1)
            stage_b(b)
        S.barrier()
        A.reset(m0)


    def phase3(self, l):
        A, S = self.A, self.S
        m0 = A.mark()
        cosT = T(A.alloc("cosT", [128, 8 * TS], F32), "cosT", 8)
        sinT = T(A.alloc("sinT", [128, 8 * TS], F32), "sinT", 8)
        rT = T(A.alloc("rT", [128, 8 * TS], F32), "rT", 8)
        Bb = T(A.alloc("Bb", [128, 16 * 128], BF16), "Bb")
        CA = T(A.alloc("CA", [128, 8 * 128], BF16), "CA")
        CB = T(A.alloc("CB", [128, 8 * 128], BF16), "CB")
        glu = T(A.alloc("glu", [128, 1024], BF16), "glu")
        CAn = T(A.alloc("CAn", [128, 8 * 128], BF16), "CAn")
        idf = T(A.alloc("idf", [128, 256], F32), "idf")
        dD = T(A.alloc("dD", [128, 256], BF16), "dD")
        sm = T(A.alloc("sm", [128, 8 * 24], F32), "sm")
        ini = [T(A.alloc("ini", [128, 16], F32), "ini%d" % i, 8) for i in range(2)]
        m1 = A.mark()

        def col(i, k=None):
            return sm.h[:, 8 * i:8 * i + 8] if k is None else sm.h[:, 8 * i + k:8 * i + k + 1]
        (DT, TH, LR, R_, C_, S_, T1, T2, T3, ABR, ABI, DEN, INV, BRE, BIM, NBR, NBI, U1, U2, CT, ST) = range(21)
        spc = self.spm_t.r
        are, aim, ldt = self.sp_col(l, SP_ARE, 8), self.sp_col(l, SP_AIM, 8), self.sp_col(l, SP_LDT, 8)

        def sact(o, i, f, **kw):
            self.act(col(o), i, f, r=[sm.r, spc, self.misc.r], w=[sm.r], **kw)

        def stt_(o, a, b, op):
            self.tt("dve", col(o), a, b, op, r=[sm.r, spc], w=[sm.r])
        sact(DT, ldt, AF.Exp)
        stt_(TH, aim, col(DT), ALU.mult)
        stt_(LR, are, col(DT), ALU.mult)
        sact(R_, col(LR), AF.Exp)
        sact(S_, col(TH), AF.Sin, scale=1.0 / 32)
        sact(C_, col(TH), AF.Sin, scale=1.0 / 32, bias=self.halfpi_ap)

        def square_cs():
            stt_(T1, col(C_), col(C_), ALU.mult)
            stt_(T2, col(S_), col(S_), ALU.mult)
            stt_(T3, col(C_), col(S_), ALU.mult)
            stt_(C_, col(T1), col(T2), ALU.subtract)
            self.ts("dve", col(S_), col(T3), 2.0, ALU.mult, r=[sm.r], w=[sm.r])
        for _ in range(5):
            square_cs()
        stt_(ABR, col(R_), col(C_), ALU.mult)
        stt_(ABI, col(R_), col(S_), ALU.mult)
        self.ts("dve", col(ABR), col(ABR), -1.0, ALU.add, r=[sm.r], w=[sm.r])
        stt_(T1, are, are, ALU.mult)
        stt_(T2, aim, aim, ALU.mult)
        stt_(DEN, col(T1), col(T2), ALU.add)
        S.op("dve", lambda e: e.reciprocal(out=col(INV), in_=col(DEN)), r=[sm.r], w=[sm.r])
        stt_(T1, col(ABR), are, ALU.mult)
        stt_(T2, col(ABI), aim, ALU.mult)
        stt_(T3, col(T1), col(T2), ALU.add)
        stt_(BRE, col(T3), col(INV), ALU.mult)
        stt_(T1, col(ABI), are, ALU.mult)
        stt_(T2, col(ABR), aim, ALU.mult)
        stt_(T3, col(T1), col(T2), ALU.subtract)
        stt_(BIM, col(T3), col(INV), ALU.mult)
        self.ts("dve", col(NBI), col(BIM), -1.0, ALU.mult, r=[sm.r], w=[sm.r])
        tmp = T(A.alloc("tmp", [128, 2 * TS], F32), "tmp")
        onesr = T(A.alloc("onesr", [128, TS], F32), "onesr")
        Cst = T(A.alloc("Cst", [128, 2048], F32), "Cst")
        self.memset("dve", onesr.h[:, :], 1.0, w=[onesr.r])
        for k in range(8):
            self.memset("dve", cosT.h[:, k * TS:k * TS + 1], 1.0, w=[cosT.sub[k]])
            self.memset("dve", sinT.h[:, k * TS:k * TS + 1], 0.0, w=[sinT.sub[k]])
            self.act(rT.h[:, k * TS:(k + 1) * TS], onesr.h[:, :], AF.Copy, r=[onesr.r, sm.r], w=[rT.sub[k]], scale=col(R_, k))
        tb = T(A.alloc("tb", [128, 2048], F32), "tb")
        c3 = cosT.h[:, :].rearrange("p (k t) -> p k t", k=8)
        s3 = sinT.h[:, :].rearrange("p (k t) -> p k t", k=8)
        allr = [sm.r] + cosT.sub + sinT.sub
        m = 1
        while m < TS:
            cb = col(C_).rearrange("p (k o) -> p k o", o=1).broadcast_to([128, 8, m])
            sb = col(S_).rearrange("p (k o) -> p k o", o=1).broadcast_to([128, 8, m])
            t1 = tb.h[:, 0:8 * m].rearrange("p (k t) -> p k t", k=8)
            t2 = tb.h[:, 1024:1024 + 8 * m].rearrange("p (k t) -> p k t", k=8)
            pre, pim = c3[:, :, 0:m], s3[:, :, 0:m]
            self.tt("dve", t1, pre, cb, ALU.mult, r=allr, w=[tb.r])
            self.tt("dve", t2, pim, sb, ALU.mult, r=allr, w=[tb.r])
            self.tt("dve", c3[:, :, m:2 * m], t1, t2, ALU.subtract, r=[tb.r], w=cosT.sub)
            self.tt("dve", t1, pre, sb, ALU.mult, r=allr, w=[tb.r])
            self.tt("dve", t2, pim, cb, ALU.mult, r=allr, w=[tb.r])
            self.tt("dve", s3[:, :, m:2 * m], t1, t2, ALU.add, r=[tb.r], w=sinT.sub)
            square_cs()
            m *= 2
        self.cp("dve", col(CT), col(C_), r=[sm.r], w=[sm.r])
        self.cp("dve", col(ST), col(S_), r=[sm.r], w=[sm.r])
        S.dma("sp", idf.h[:, 0:128], self.cst[:, C_ID:C_ID + 128], w=[idf.r])
        self.ts("dve", idf.h[:, 128:256], idf.h[:, 0:128], -1.0, ALU.mult, r=[idf.r], w=[idf.r])
        r0 = l * 128
        S.dma("pool", Bb.h[:, :], self.ssmm[r0:r0 + 128, 0:2048], w=[Bb.r])
        S.dma("sp", Cst.h[:, :], self.ssmm[r0:r0 + 128, 2048:4096], w=[Cst.r])
        row = (l * NSLAB + SLAB_GLU) * 128
        S.dma("sp", glu.h[:, :], self.wb[row:row + 128, :], r=[self.r_wb[l][SLAB_GLU // 8]], w=[glu.r])
        for k in range(8):
            cre, cim = Cst.h[:, k * 128:(k + 1) * 128], Cst.h[:, (8 + k) * 128:(9 + k) * 128]
            self.ts("dve", tmp.h[:, 0:128], cim, col(BIM, k), ALU.mult, r=[Cst.r, sm.r], w=[tmp.r])
            self.stt("dve", CA.h[:, k * 128:(k + 1) * 128], cre, col(BRE, k), tmp.h[:, 0:128], ALU.mult, ALU.subtract, r=[Cst.r, sm.r, tmp.r], w=[CA.r])
            self.ts("dve", tmp.h[:, 128:256], cim, col(BRE, k), ALU.mult, r=[Cst.r, sm.r], w=[tmp.r])
            self.stt("dve", CB.h[:, k * 128:(k + 1) * 128], cre, col(NBI, k), tmp.h[:, 128:256], ALU.mult, ALU.subtract, r=[Cst.r, sm.r, tmp.r], w=[CB.r])
        self.ts("dve", CAn.h[:, :], CA.h[:, :], -1.0, ALU.mult, r=[CA.r], w=[CAn.r])
        for ct in range(2):
            self.ts("dve", dD.h[:, ct * 128:(ct + 1) * 128], idf.h[:, 0:128], self.sp_col(l, SP_SD + ct), ALU.mult, r=[idf.r, spc], w=[dD.r])
        S.barrier()
        A.reset(m1)
        Aw = [T(A.alloc("Aw", [128, 2 * TS], F32), "Aw%d" % i) for i in range(2)]
        Bw = [T(A.alloc("Bw", [128, 2 * TS], F32), "Bw%d" % i) for i in range(2)]
        vv = [T(A.alloc("vv", [128, 2 * TS], F32), "vv%d" % i) for i in range(2)]
        A2 = [T(A.alloc("A2", [128, 2 * TS], BF16), "A2%d" % i) for i in range(2)]
        B2 = [T(A.alloc("B2", [128, 2 * TS], BF16), "B2%d" % i) for i in range(2)]
        ub = [T(A.alloc("ub", [128, 2 * TS], BF16), "ub%d" % i, 2) for i in range(2)]
        ysb = T(A.alloc("ysb", [128, TS], F32), "ysb")
        g1 = T(A.alloc("g1", [128, TS], F32), "g1")
        g2 = T(A.alloc("g2", [128, TS], F32), "g2")
        yg = T(A.alloc("yg", [128, 2 * TS], BF16), "yg", 2)
        sgl = T(A.alloc("sgl", [128, TS], F32), "sgl")
        ob = [T(A.alloc("ob", [128, TS], BF16), "ob%d" % i) for i in range(2)]
        it = [T(A.alloc("it", [128, 2], F32), "it%d" % i) for i in range(2)]
        self.ts("dve", col(T1), col(ST), -1.0, ALU.mult, r=[sm.r], w=[sm.r])
        steps = [(c, k) for c in range(NCH) for k in range(8)]
        yps_h = [None]
        wps_h = {}
        Id, nId = idf.h[:, 0:128], idf.h[:, 128:256]

        def h2(ap):
            return ap.rearrange("p (h t) -> p h t", h=2)

        def bc2(ap):
            return ap.rearrange("p (o t) -> p o t", o=1).broadcast_to([128, 2, TS])

        bps_h = {}

        def stage0(i):
            c, k = steps[i]
            t0 = c * TS
            u_ = ub[c % 3]
            if k == 0:
                for ct in range(2):
                    S.dma("sp", u_.h[:, ct * TS:(ct + 1) * TS], self.u_d[ct * 128:(ct + 1) * 128, t0:t0 + TS], r=[self.r_u[ct][t0 // TB]], w=[u_.sub[ct]])
            ct = k // 4
            bps = self.bank()
            bps_h[i] = bps
            self.mm(bps.h[:, 0:TS], Bb.h[:, k * 128:(k + 1) * 128], u_.h[:, ct * TS:(ct + 1) * TS], True, True, r=[Bb.r, u_.sub[ct]], w=[bps.r])
            self.mm(bps.h[:, TS:2 * TS], Bb.h[:, (8 + k) * 128:(9 + k) * 128], u_.h[:, ct * TS:(ct + 1) * TS], True, True, r=[Bb.r, u_.sub[ct]], w=[bps.r])

        def stage1(i):
            c, k = steps[i]
            i2 = i % 2
            o = k * TS
            bps = bps_h.pop(i)
            a_, b_ = Aw[i2], Bw[i2]
            ck, sk = cosT.h[:, o:o + TS], sinT.h[:, o:o + TS]
            self.tt("dve", h2(a_.h[:, :]), h2(bps.h[:, :]), bc2(ck), ALU.mult, r=[bps.r, cosT.sub[k]], w=[a_.r])
            self.tt("dve", h2(b_.h[:, :]), h2(bps.h[:, :]), bc2(sk), ALU.mult, r=[bps.r, sinT.sub[k]], w=[b_.r])
            wps = self.bank()
            wps_h[i] = wps
            self.mm(wps.h[:, 0:TS], Id, a_.h[:, 0:TS], True, False, r=[idf.r, a_.r], w=[wps.r])
            self.mm(wps.h[:, 0:TS], Id, b_.h[:, TS:2 * TS], False, True, r=[idf.r, b_.r], w=[wps.r])
            self.mm(wps.h[:, TS:2 * TS], Id, a_.h[:, TS:2 * TS], True, False, r=[idf.r, a_.r], w=[wps.r])
            self.mm(wps.h[:, TS:2 * TS], nId, b_.h[:, 0:TS], False, True, r=[idf.r, b_.r], w=[wps.r])

        def stage2(i):
            c, k = steps[i]
            t0 = c * TS
            u_ = ub[c % 3]
            ct = k // 4
            i2 = i % 2
            o = k * TS
            ini_c, ini_n = ini[c % 2], ini[(c + 1) % 2]
            v_, a2, b2 = vv[i2], A2[i2], B2[i2]
            wps = wps_h.pop(i)
            ck, sk = cosT.h[:, o:o + TS], sinT.h[:, o:o + TS]
            for hf in range(2):
                init = 0.0 if c == 0 else ini_c.h[:, hf * 8 + k:hf * 8 + k + 1]
                S.op("dve", lambda e, hf=hf, v_=v_, wps=wps, init=init, o=o: e.tensor_tensor_scan(
                    out=v_.h[:, hf * TS:(hf + 1) * TS], data0=rT.h[:, o:o + TS], data1=wps.h[:, hf * TS:(hf + 1) * TS],
                    initial=init, op0=ALU.mult, op1=ALU.add), r=[rT.sub[k], wps.r] + ([ini_c.sub[k]] if c else []), w=[v_.r])
            if c < NCH - 1:
                vre, vim = v_.h[:, TS - 1:TS], v_.h[:, 2 * TS - 1:2 * TS]
                it_ = it[i2]
                self.act(it_.h[:, 0:1], vim, AF.Copy, r=[v_.r, sm.r], w=[it_.r], scale=col(T1, k))
                self.act(it_.h[:, 1:2], vre, AF.Copy, r=[v_.r, sm.r], w=[it_.r], scale=col(ST, k))
                self.act(ini_n.h[:, k:k + 1], vre, AF.Identity, r=[v_.r, sm.r, it_.r], w=[ini_n.sub[k]], scale=col(CT, k), bias=it_.h[:, 0:1])
                self.act(ini_n.h[:, 8 + k:9 + k], vim, AF.Identity, r=[v_.r, sm.r, it_.r], w=[ini_n.sub[k]], scale=col(CT, k), bias=it_.h[:, 1:2])
            self.tt("pool", h2(a2.h[:, :]), h2(v_.h[:, :]), bc2(ck), ALU.mult, r=[v_.r, cosT.sub[k]], w=[a2.r])
            self.tt("pool", h2(b2.h[:, :]), h2(v_.h[:, :]), bc2(sk), ALU.mult, r=[v_.r, sinT.sub[k]], w=[b2.r])
            yps = self.psum[6 + ct]
            kk = slice(k * 128, (k + 1) * 128)
            self.mm(yps.h[:, 0:TS], CA.h[:, kk], a2.h[:, 0:TS], k % 4 == 0, False, r=[CA.r, a2.r], w=[yps.r])
            self.mm(yps.h[:, 0:TS], CAn.h[:, kk], b2.h[:, TS:2 * TS], False, False, r=[CAn.r, b2.r], w=[yps.r])
            self.mm(yps.h[:, 0:TS], CB.h[:, kk], b2.h[:, 0:TS], False, False, r=[CB.r, b2.r], w=[yps.r])
            self.mm(yps.h[:, 0:TS], CB.h[:, kk], a2.h[:, TS:2 * TS], False, False, r=[CB.r, a2.r], w=[yps.r])
            if k % 4 == 3:
                self.mm(yps.h[:, 0:TS], dD.h[:, ct * 128:(ct + 1) * 128], u_.h[:, ct * TS:(ct + 1) * TS], False, True, r=[dD.r, u_.sub[ct]], w=[yps.r])

        def epilogue(i):
            c, k = steps[i]
            t0 = c * TS
            ct = k // 4
            yps = self.psum[6 + ct]
            self.act(yg.h[:, ct * TS:(ct + 1) * TS], yps.h[:, 0:TS], AF.Gelu_apprx_tanh, r=[yps.r], w=[yg.sub[ct]])
            if k == 7:
                for mt in range(2):
                    zps = self.psum[5]
                    for kt in range(2):
                        a = mt * 2 + kt
                        self.mm(zps.h[:, mt * TS:(mt + 1) * TS], glu.h[:, a * 128:(a + 1) * 128], yg.h[:, kt * TS:(kt + 1) * TS], kt == 0, kt == 1, r=[glu.r, yg.sub[kt]], w=[zps.r])
                    self.act(sgl.h[:, :], zps.h[:, mt * TS:(mt + 1) * TS], AF.Sigmoid, r=[zps.r, spc], w=[sgl.r], bias=self.sp_col(l, SP_GB + mt))
                    o_ = ob[mt]
                    self.tt("pool", o_.h[:, :], yg.h[:, mt * TS:(mt + 1) * TS], sgl.h[:, :], ALU.mult, r=[yg.sub[mt], sgl.r], w=[o_.r])
                    S.dma("sp", self.mixT_d[768 + mt * 128:768 + (mt + 1) * 128, t0:t0 + TS], o_.h[:, :], r=[o_.r], w=[self.r_mix[6 + mt][t0 // TB]])
        ub.append(T(A.alloc("ub", [128, 2 * TS], BF16), "ub2", 2))
        n = len(steps)
        self.nrot = 5
        self.pnext = 0
        stage0(0)
        stage0(1)
        stage1(0)
        for i in range(n):
            if i + 2 < n:
                stage0(i + 2)
            if i + 1 < n:
                stage1(i + 1)
            stage2(i)
            if i >= 2 and steps[i - 2][1] % 4 == 3:
                epilogue(i - 2)
        for i in (n - 2, n - 1):
            if steps[i][1] % 4 == 3:
                epilogue(i)
        self.nrot = 8
        S.barrier()
        A.reset(m0)

    def phase4(self, l):
        A, S = self.A, self.S
        m0 = A.mark()
        lam_init = 0.8 - 0.6 * math.exp(-0.3 * l)
        QTh = T(A.alloc("QTh", [128, NT], BF16), "QTh", NB)
        KTh = T(A.alloc("KTh", [128, NT], BF16), "KTh", NB)
        Vh = T(A.alloc("Vh", [128, NT], BF16), "Vh", 32)
        bct = T(A.alloc("bct", [128, 4 * 36], F32), "bct")
        pt = [T(A.alloc("pt", [128, TB], BF16), "pt%d" % i) for i in range(4)]
        fin = [T(A.alloc("fin", [128, TB], F32), "fin%d" % i) for i in range(5)]
        sqa = T(A.alloc("sqa", [128, TB], BF16), "sqa")
        rs = T(A.alloc("rs", [128, TB], F32), "rs")
        ob = [T(A.alloc("ob", [128, TB], BF16), "ob%d" % i) for i in range(2)]
        lt = T(A.alloc("lt", [128, 128], F32), "lt")
        mc = self.misc
        S.dma("sp", bct.h[:, :], self.bctd, w=[bct.r])
        for j in range(2):
            c = l * SP_N + SP_LAM + 128 * j
            self.tt("dve", lt.h[:, 64 * j:64 * j + 64], self.spm_t.h[:, c:c + 64], self.spm_t.h[:, c + 64:c + 128], ALU.mult,
                    r=[self.spm_t.r], w=[lt.r])
            S.op("dve", lambda e, j=j: e.reduce_sum(out=mc.h[:, 8 + j:9 + j], in_=lt.h[:, 64 * j:64 * j + 64], axis=mybir.AxisListType.X),
                 r=[lt.r], w=[mc.r])
        self.act(mc.h[:, 8:10], mc.h[:, 8:10], AF.Exp, r=[mc.r], w=[mc.r])
        self.tt("dve", mc.h[:, 5:6], mc.h[:, 9:10], mc.h[:, 8:9], ALU.subtract, r=[mc.r], w=[mc.r])
        self.ts("dve", mc.h[:, 5:6], mc.h[:, 5:6], -lam_init, ALU.add, r=[mc.r], w=[mc.r])
        self.ts("dve", mc.h[:, 6:7], self.sp_col(l, SP_OG), 1.0 - lam_init, ALU.mult, r=[self.spm_t.r], w=[mc.r])
        neg_lam = mc.h[:, 5:6]
        ogs = mc.h[:, 6:7]
        O = [self.psum[0], self.psum[1]]
        Ls = [self.psum[2], self.psum[3]]
        sc = [self.psum[4], self.psum[5], self.psum[6], self.psum[7]]
        vd = self.v_d.rearrange("(t p) c -> p t c", p=128)
        pend = [None]
        raw = [T(A.alloc("raw", [128, TB], F32), "raw%d" % i) for i in range(4)]

        def fin1():
            for c in range(2):
                self.cp("dve", raw[c].h[:, :], O[c].h[:, :], r=[O[c].r], w=[raw[c].r])
                self.act(raw[2 + c].h[:, :], Ls[c].h[:, :], AF.Copy, r=[Ls[c].r], w=[raw[2 + c].r])

        def fin2a():
            for c in range(2):
                S.op("dve", lambda e, c=c: e.reciprocal(out=fin[c].h[:, :], in_=raw[2 + c].h[:, :]), r=[raw[2 + c].r], w=[fin[c].r])
                self.tt("dve", fin[2 + c].h[:, :], raw[c].h[:, :], fin[c].h[:, :], ALU.mult, r=[raw[c].r, fin[c].r], w=[fin[2 + c].r])
            a = fin[4]
            self.stt("dve", a.h[:, :], fin[3].h[:, :], neg_lam, fin[2].h[:, :], ALU.mult, ALU.add, r=[fin[2].r, fin[3].r, mc.r], w=[a.r])
            self.act(sqa.h[:, :], a.h[:, :], AF.Square, r=[a.r], w=[sqa.r])

        def fin2b(hq, psn):
            h_, qb_ = hq
            a = fin[4]
            self.mm(psn.h[:, :], self.ones128, sqa.h[:, :], True, True, r=[sqa.r, self.cbf.r], w=[psn.r])
            self.rstd_from(psn, TB, rs)
            o = ob[qb_ % 2]
            self.stt("dve", o.h[:, :], a.h[:, :], ogs, rs.h[:, :], ALU.mult, ALU.mult, r=[a.r, rs.r, mc.r], w=[o.r])
            S.dma("pool", self.mixT_d[h_ * 128:(h_ + 1) * 128, qb_ * TB:(qb_ + 1) * TB], o.h[:, :], r=[o.r], w=[self.r_mix[h_][qb_]])
        for h in range(4):
            G = 256 if h == 0 else 512
            for b in range(NB):
                S.dma("pool", QTh.h[:, b * TB:(b + 1) * TB], self.qT_d[h * 128:(h + 1) * 128, b * TB:(b + 1) * TB], r=[self.r_qT[h][b]], w=[QTh.sub[b]])
                S.dma("pool", KTh.h[:, b * TB:(b + 1) * TB], self.kT_d[h * 128:(h + 1) * 128, b * TB:(b + 1) * TB], r=[self.r_kT[h][b]], w=[KTh.sub[b]])
            for g in range(4):
                S.dma("pool", Vh.h[:, g * 1024:(g + 1) * 1024].rearrange("p (t d) -> p t d", d=128), vd[:, g * 8:(g + 1) * 8, h * 128:(h + 1) * 128],
                      r=self.r_v[g * 8:(g + 1) * 8], w=Vh.sub[g * 8:(g + 1) * 8])
            for qb in range(NB):
                q0 = qb * TB
                nk = 4 * qb + 4

                def geom(kt):
                    jl = kt - 4 * qb
                    return jl, (128 * jl if jl >= 0 else 0)

                def emitS(kt):
                    jl, c0 = geom(kt)
                    k0 = kt * 128
                    for c in range(2):
                        Sb = sc[(kt % 2) * 2 + c]
                        pr = slice(64 * c, 64 * c + 64)
                        self.mm(Sb.h[:, c0:TB], KTh.h[pr, k0:k0 + 128], QTh.h[pr, q0 + c0:q0 + TB], True, jl < 0,
                                r=[KTh.sub[kt // 4], QTh.sub[qb]], w=[Sb.r])
                        if jl >= 0:
                            self.mm(Sb.h[:, c0:c0 + 128], self.ident_bf, self.dh(h), False, True, r=[self.cbf.r], w=[Sb.r])

                def emitE(kt):
                    jl, c0 = geom(kt)
                    for c in range(2):
                        Sb = sc[(kt % 2) * 2 + c]
                        p_ = pt[(kt % 2) * 2 + c]
                        for g in range(TB // G):
                            lo, hi = max(c0, g * G), (g + 1) * G
                            if lo >= hi:
                                continue
                            mprime = (q0 + g * G - kt * 128) // 128
                            bcol = bct.h[:, h * 36 + mprime + 3:h * 36 + mprime + 4]
                            self.act(p_.h[:, lo:hi], Sb.h[:, lo:hi], AF.Exp, r=[Sb.r, bct.r], w=[p_.r], bias=bcol)

                def emitAV(kt):
                    jl, c0 = geom(kt)
                    for c in range(2):
                        p_ = pt[(kt % 2) * 2 + c]
                        self.mm(O[c].h[:, c0:TB], Vh.h[:, kt * 128:(kt + 1) * 128], p_.h[:, c0:TB], kt == 0, kt == nk - 1,
                                r=[Vh.sub[kt], p_.r], w=[O[c].r])
                        self.mm(Ls[c].h[:, c0:TB], self.ones1, p_.h[:, c0:TB], kt == 0, kt == nk - 1, r=[self.cbf.r, p_.r], w=[Ls[c].r])
                emitS(0)
                emitS(1)
                for kt in range(nk):
                    emitE(kt)
                    if pend[0] is not None and kt == 1:
                        fin2a()
                    if pend[0] is not None and kt == min(nk - 1, 10):
                        fin2b(pend[0], sc[(kt % 2) * 2])
                        pend[0] = None
                    if kt + 2 < nk:
                        emitS(kt + 2)
                    emitAV(kt)
                fin1()
                pend[0] = (h, qb)
        fin2a()
        fin2b(pend[0], sc[0])
        S.barrier()
        A.reset(m0)

    def phase5(self, l):
        A, S = self.A, self.S
        m0 = A.mark()
        self.ring_init(6)
        mixb = [T(A.alloc("mixb", [128, KT * TB], BF16), "mixb%d" % i, KT) for i in range(1)]
        nT = T(A.alloc("nT", [128, KT * TB], BF16), "nT", KT)
        hid = T(A.alloc("hid", [128, 32 * TB], BF16), "hid", 32)
        sq = [T(A.alloc("sq", [128, TB], BF16), "sq%d" % i) for i in range(2)]
        rstd = T(A.alloc("rstd", [128, TB], F32), "rstd")
        rl = [T(A.alloc("rl", [128, TB], F32), "rl%d" % i) for i in range(2)]
        pTb = T(A.alloc("pTb", [128, 2 * TB], BF16), "pTb", 2)
        for b in range(NB):
            mb = mixb[0]
            for k in range(KT):
                S.dma("pool", mb.h[:, k * TB:(k + 1) * TB], self.mixT_d[k * 128:(k + 1) * 128, b * TB:(b + 1) * TB],
                      r=[self.r_mix[k][b]], w=[mb.sub[k]])
            for mt in range(KT):
                sl = self.slab(l, SLAB_WOUT + mt)
                ps = self.bank()
                for k in range(KT):
                    self.mm(ps.h[:, :], sl.h[:, k * 128:(k + 1) * 128], mb.h[:, k * TB:(k + 1) * TB], k == 0, k == KT - 1,
                            r=[sl.r, mb.sub[k]], w=[ps.r])
                c = self.hcol(mt, b)
                self.tt("dve", self.hT[:, c:c + TB], self.hT[:, c:c + TB], ps.h[:, :], ALU.add, r=[ps.r], w=[self.r_h[mt][b]])
            self.rmsnorm(l, SP_G2, b, nT, sq, rstd)
            for ft in range(32):
                sl = self.slab(l, SLAB_W1 + ft)
                ps = self.bank()
                for k in range(KT):
                    self.mm(ps.h[:, :], sl.h[:, k * 128:(k + 1) * 128], nT.h[:, k * TB:(k + 1) * TB], k == 0, k == KT - 1,
                            r=[sl.r, nT.sub[k]], w=[ps.r])
                r_ = rl[ft % 2]
                self.act(r_.h[:, :], ps.h[:, :], AF.Relu, r=[ps.r], w=[r_.r])
                self.tt("pool" if ft % 2 else "dve", hid.h[:, ft * TB:(ft + 1) * TB], r_.h[:, :], r_.h[:, :], ALU.mult, r=[r_.r], w=[hid.sub[ft]])
            for mt in range(KT):
                ps = self.bank()
                for q in range(4):
                    sl = self.slab(l, SLAB_W2 + mt * 4 + q)
                    for k in range(8):
                        f = q * 8 + k
                        self.mm(ps.h[:, :], sl.h[:, k * 128:(k + 1) * 128], hid.h[:, f * TB:(f + 1) * TB], f == 0, f == 31,
                                r=[sl.r, hid.sub[f]], w=[ps.r])
                c = self.hcol(mt, b)
                self.tt("dve", self.hT[:, c:c + TB], self.hT[:, c:c + TB], ps.h[:, :], ALU.add, r=[ps.r], w=[self.r_h[mt][b]])
            self.rmsnorm(l, SP_G3, b, nT, sq, rstd)
            for k in range(2):
                r0 = l * 256 + k * 128
                S.dma("pool", pTb.h[:, k * TB:(k + 1) * TB], self.pT[r0:r0 + 128, b * TB:(b + 1) * TB], w=[pTb.sub[k]])
            for mt in range(KT):
                if mt % 4 == 0:
                    pls = self.slab(l, SLAB_PLE + mt // 4)
                sl = self.slab(l, SLAB_GATE + mt)
                psg = self.bank()
                for k in range(KT):
                    self.mm(psg.h[:, :], sl.h[:, k * 128:(k + 1) * 128], nT.h[:, k * TB:(k + 1) * TB], k == 0, k == KT - 1,
                            r=[sl.r, nT.sub[k]], w=[psg.r])
                psp = self.bank()
                for k in range(2):
                    a = (mt % 4) * 2 + k
                    self.mm(psp.h[:, :], pls.h[:, a * 128:(a + 1) * 128], pTb.h[:, k * TB:(k + 1) * TB], k == 0, k == 1,
                            r=[pls.r, pTb.sub[k]], w=[psp.r])
                r_ = rl[mt % 2]
                self.act(r_.h[:, :], psg.h[:, :], AF.Sigmoid, r=[psg.r], w=[r_.r])
                self.tt("dve", r_.h[:, :], r_.h[:, :], psp.h[:, :], ALU.mult, r=[psp.r, r_.r], w=[r_.r])
                c = self.hcol(mt, b)
                self.tt("pool", self.hT[:, c:c + TB], self.hT[:, c:c + TB], r_.h[:, :], ALU.add, r=[r_.r], w=[self.r_h[mt][b]])
                if l == self.n_layers - 1:
                    S.dma("sp", self.outT[mt * 128:(mt + 1) * 128, b * TB:(b + 1) * TB], self.hT[:, c:c + TB],
                          r=[self.r_h[mt][b]], w=())
        S.barrier()
        A.reset(m0)

    def epilogue(self):
        self.S.barrier()

    def dump(self, name, src):
        if name in self.dbg:
            self.S.barrier()
            self.S.dma("sp", self.dbg[name], src, w=[self.r_dummy])
            self.S.barrier()

    def mark(self, name):
        self.marks.append((name, dict(self.S.cnt)))

    def build(self):
        self.marks = []
        self.prologue()
        for l in range(self.n_layers):
            ph = self.phases
            self.mark("L%d.P1" % l)
            if ph is None or 1 in ph:
                self.phase1(l)
            self.mark("L%d.P2" % l)
            if l + 1 < self.n_layers:
                self.cast_weights(l + 1)
            if ph is None or 2 in ph:
                self.phase2(l)
            self.mark("L%d.P3" % l)
            if ph is None or 3 in ph:
                self.phase3(l)
            self.mark("L%d.P4" % l)
            if ph is None or 4 in ph:
                self.phase4(l)
            self.mark("L%d.P5" % l)
            if l == 0:
                self.dump("d_qT", self.qT_d)
                self.dump("d_kT", self.kT_d)
                self.dump("d_v", self.v_d)
                self.dump("d_mix", self.mixT_d)
            if ph is None or 5 in ph:
                self.phase5(l)
        self.mark("END")
        self.epilogue()
        self.S.emit()
        return self.nc


def _tiles(W, order):
    return [W[kt * 128:(kt + 1) * 128, mt * 128:(mt + 1) * 128] for (kt, mt) in order]


def pack_weights(inp):
    out = np.zeros((L, NSLAB, 128, 8, 128), np.float32)
    for l in range(L):
        tl = []
        w_in = inp["w_in"][l]
        for grp0, nmt in ((0, 4), (4, 4)):
            tl += _tiles(w_in, [(kt, grp0 + mt) for mt in range(nmt) for kt in range(8)])
        tl += _tiles(w_in, [(kt, 8 + n) for kt in range(8) for n in range(4)])
        tl += _tiles(w_in, [(kt, 12 + mt) for mt in range(4) for kt in range(8)])
        tl += _tiles(w_in, [(kt, 16 + mt) for mt in range(2) for kt in range(8)])
        tl += _tiles(inp["w_out"][l], [(kt, mt) for mt in range(8) for kt in range(8)])
        tl += _tiles(inp["w_ff1"][l], [(kt, mt) for mt in range(32) for kt in range(8)])
        tl += _tiles(inp["w_ff2"][l], [(kt, mt) for mt in range(8) for kt in range(32)])
        tl += _tiles(inp["w_ple_gate"][l], [(kt, mt) for mt in range(8) for kt in range(8)])
        tl += _tiles(inp["w_ple_proj"][l], [(kt, mt) for mt in range(8) for kt in range(2)])
        tl += _tiles(inp["ssm_glu_w"][l], [(kt, mt) for mt in range(2) for kt in range(2)])
        for i, t in enumerate(tl):
            out[l, i // 8, :, i % 8, :] = t
    return out.reshape(L * NSLAB * 128, 1024)


def pack_small(inp):
    sp = np.zeros((128, L, SP_N), np.float32)
    p = np.arange(128)
    for l in range(L):
        for off, key in ((SP_G1, "mix_norm"), (SP_G2, "mlp_norm"), (SP_G3, "ple_norm")):
            sp[:, l, off:off + 8] = inp[key][l].reshape(8, 128).T
        sp[:, l, SP_QG] = inp["q_norm"][l][p % 64]
        sp[:, l, SP_KG] = inp["k_norm"][l][p % 64]
        for i, key in enumerate(("lam_q1", "lam_k1", "lam_q2", "lam_k2")):
            sp[:, l, SP_LAM + 64 * i:SP_LAM + 64 * (i + 1)] = inp[key][l][None, :]
        sp[:, l, SP_OG] = inp["att_out_norm"][l]
        for ct in range(2):
            sp[:, l, SP_CW + ct * 31:SP_CW + (ct + 1) * 31] = inp["conv_w"][l][:, ct * 128:(ct + 1) * 128].T
        sp[:, l, SP_CB:SP_CB + 2] = inp["conv_b"][l].reshape(2, 128).T
        sp[:, l, SP_LNG:SP_LNG + 2] = inp["conv_ln_g"][l].reshape(2, 128).T
        sp[:, l, SP_LNB:SP_LNB + 2] = inp["conv_ln_b"][l].reshape(2, 128).T
        for k in range(8):
            g = 2 * k + p // 64
            sp[:, l, SP_ARE + k] = inp["ssm_a_re"][l][g, p % 64]
            sp[:, l, SP_AIM + k] = inp["ssm_a_im"][l][g, p % 64]
            sp[:, l, SP_LDT + k] = inp["ssm_log_dt"][l][g]
        sp[:, l, SP_SD:SP_SD + 2] = inp["ssm_d"][l].reshape(2, 128).T
        sp[:, l, SP_GB:SP_GB + 2] = inp["ssm_glu_b"][l].reshape(2, 128).T
    return sp.reshape(128, L * SP_N)


def pack_ssm(inp):
    out = np.zeros((L, 128, 32, 128), np.float32)
    for l in range(L):
        for k in range(8):
            for half in range(2):
                g = 2 * k + half
                chl = (g % 8) * 16
                rows = slice(half * 64, half * 64 + 64)
                out[l, chl:chl + 16, k, rows] = inp["ssm_b_re"][l][g].T
                out[l, chl:chl + 16, 8 + k, rows] = inp["ssm_b_im"][l][g].T
                out[l, rows, 16 + k, chl:chl + 16] = inp["ssm_c_re"][l][g].T
                out[l, rows, 24 + k, chl:chl + 16] = inp["ssm_c_im"][l][g].T
    return out.reshape(L * 128, 4096)


def make_consts():
    cst = np.zeros((128, C_N), np.float32)
    cst[:, C_ID:C_ID + 128] = np.eye(128, dtype=np.float32)
    p = np.arange(128)
    cst[:, C_BD:C_BD + 128] = (p[:, None] // 64 == p[None, :] // 64).astype(np.float32) / 64.0
    k = p[:, None]
    q = p[None, :]
    for h in range(4):
        same = (k // 64) == (q // 64)
        cst[:, C_DH + 128 * h:C_DH + 128 * (h + 1)] = np.where(k <= q, 0.0, np.where(same, -2.0 * SLOPES[h] * (k - q), -30000.0))
    bct = np.zeros((128, 4, 36), np.float32)
    for h in range(4):
        for mi in range(36):
            bct[:, h, mi] = SLOPES[h] * (p - 128.0 * (mi - 3))
    return cst, bct.reshape(128, 144)


_NC_CACHE = {}


def kernel(**inp):
    inp = {k: np.asarray(v) for k, v in inp.items()}
    key = "full"
    if key not in _NC_CACHE:
        _NC_CACHE[key] = Builder().build()
    nc = _NC_CACHE[key]
    wf = pack_weights(inp)
    spm = pack_small(inp)
    ssmm = pack_ssm(inp)
    cst, bct = make_consts()
    in_maps = []
    for c in range(8):
        in_maps.append({
            "xT": np.ascontiguousarray(inp["x"][c].T),
            "pT": np.ascontiguousarray(inp["p"][:, c].transpose(0, 2, 1)).reshape(L * 256, NT),
            "wf": wf, "spm": spm, "ssmm": ssmm, "cst": cst, "bct": bct,
        })
    res = run_bass_kernel_spmd(nc, in_maps, core_ids=list(range(8)))
    out = np.stack([np.ascontiguousarray(res.results[c]["outT"].T) for c in range(8)], axis=0)
    return out.astype(np.float32)
```

```python
import math
import numpy as np
import concourse.bass as bass
import concourse.mybir as mybir
from concourse.bass_utils import run_bass_kernel_spmd

F32 = mybir.dt.float32
BF16 = mybir.dt.bfloat16
ALU = mybir.AluOpType
AF = mybir.ActivationFunctionType

L = 4
D = 1024
NT = 4096
TB = 512
NB = NT // TB
KT = D // 128
EPS = 1e-6
NSLAB = 101
SLAB_WIN, SLAB_WOUT, SLAB_W1, SLAB_W2, SLAB_GATE, SLAB_PLE, SLAB_GLU = 0, 18, 26, 58, 90, 98, 100
TS = 256
NCH = NT // TS
SLOPES = [2.0 ** (-8.0 * (i + 1) / 4) for i in range(4)]

SP_G1, SP_G2, SP_G3, SP_QG, SP_KG, SP_LAM, SP_OG, SP_CW, SP_CB, SP_LNG, SP_LNB, SP_ARE, SP_AIM, SP_LDT, SP_SD, SP_GB = (
    0, 8, 16, 24, 25, 26, 282, 283, 345, 347, 349, 351, 359, 367, 375, 377)
SP_N = 379
C_ID, C_BD, C_DH = 0, 128, 256
C_N = 768


class Res:
    __slots__ = ("name", "w", "rs")

    def __init__(self, name):
        self.name = name
        self.w = None
        self.rs = {}


class Sched:
    NDMA = 12

    def __init__(self, nc):
        self.nc = nc
        self.engs = {"pe": nc.tensor, "act": nc.scalar, "dve": nc.vector, "pool": nc.gpsimd, "sp": nc.sync}
        self.sem = {e: nc.alloc_semaphore("sem_" + e) for e in self.engs}
        self.cnt = {e: 0 for e in self.engs}
        self.prog = {e: [] for e in self.engs}
        self.known = {e: {} for e in self.engs}
        self.dsem = {}
        self.dcnt = {}
        self.dnext = {}
        for e in ("sp", "pool", "act"):
            for i in range(self.NDMA):
                k = ("dma", e, i)
                self.dsem[k] = nc.alloc_semaphore("dsem_%s_%d" % (e, i))
                self.dcnt[k] = 0
            self.dnext[e] = 0
        self.n_ops = 0

    def _semof(self, k):
        return self.sem[k] if isinstance(k, str) else self.dsem[k]

    def _deps(self, e, r, w, extra=()):
        deps = {}

        def need(ev):
            if ev is None:
                return
            k, v = ev
            if deps.get(k, 0) < v:
                deps[k] = v
        for x in r:
            ev = x.w
            if ev is not None and not (ev[0] == e and e == "pe"):
                need(ev)
        for x in w:
            ev = x.w
            if ev is not None and not (ev[0] == e and e == "pe"):
                need(ev)
            for k, v in x.rs.items():
                if not (k == e and e == "pe"):
                    need((k, v))
        for ev in extra:
            need(ev)
        waits = []
        kn = self.known[e]
        for k, v in deps.items():
            if kn.get(k, 0) < v:
                kn[k] = v
                waits.append((k, v))
        return waits

    def _commit(self, ev, r, w):
        k, v = ev
        for x in r:
            if x.rs.get(k, 0) < v:
                x.rs[k] = v
        for x in w:
            x.w = ev
            x.rs = {}

    def op(self, e, fn, r=(), w=()):
        waits = self._deps(e, r, w)
        self.cnt[e] += 1
        ev = (e, self.cnt[e])
        self.prog[e].append((waits, fn, ev, 1))
        self._commit(ev, r, w)
        self.n_ops += 1
        return ev

    def dma(self, e, out_ap, in_ap, r=(), w=()):
        i = self.dnext[e]
        self.dnext[e] = (i + 1) % self.NDMA
        k = ("dma", e, i)
        prev = (k, self.dcnt[k]) if self.dcnt[k] else None
        waits = self._deps(e, r, w, extra=[prev] if prev else [])
        self.dcnt[k] += 16
        ev = (k, self.dcnt[k])
        self.prog[e].append((waits, (lambda eng, o=out_ap, i_=in_ap: eng.dma_start(out=o, in_=i_)), ev, 16))
        self._commit(ev, r, w)
        self.n_ops += 1
        return ev

    def barrier(self):
        allev = [(e, c) for e, c in self.cnt.items() if c] + [(k, c) for k, c in self.dcnt.items() if c]
        for e in self.engs:
            waits = []
            kn = self.known[e]
            for k, v in allev:
                if k == e:
                    continue
                if kn.get(k, 0) < v:
                    kn[k] = v
                    waits.append((k, v))
            if waits:
                self.prog[e].append((waits, None, None, 0))

    def emit(self):
        nc = self.nc
        sched = self

        def run(eng, e):
            for waits, fn, ev, inc in sched.prog[e]:
                for k, v in waits:
                    eng.wait_ge(sched._semof(k), v)
                if fn is not None:
                    ins = fn(eng)
                    ins.then_inc(sched._semof(ev[0]), inc)
        with nc.Block() as block:
            @block.sync
            def _(eng):
                run(eng, "sp")

            @block.tensor
            def _(eng):
                run(eng, "pe")

            @block.scalar
            def _(eng):
                run(eng, "act")

            @block.vector
            def _(eng):
                run(eng, "dve")

            @block.gpsimd
            def _(eng):
                run(eng, "pool")


class Arena:
    def __init__(self, nc, base, limit):
        self.nc, self.base, self.limit, self.top, self.n = nc, base, limit, base, 0

    def alloc(self, name, shape, dt):
        esz = 2 if dt == BF16 else 4
        nbytes = int(np.prod(shape[1:])) * esz
        off = (self.top + 31) // 32 * 32
        assert off + nbytes <= self.limit, "SBUF arena overflow: %s %d" % (name, off + nbytes - self.limit)
        self.top = off + nbytes
        self.n += 1
        return self.nc.alloc_sbuf_tensor_at("%s_%d" % (name, self.n), list(shape), dt, offset=off)

    def mark(self):
        return self.top

    def reset(self, m):
        self.top = m


class T:
    def __init__(self, h, name, n=0):
        self.h = h
        self.r = Res(name)
        self.sub = [Res("%s.%d" % (name, i)) for i in range(n)]


class Builder:
    def __init__(self, n_layers=L, debug=None, phases=None):
        self.n_layers = n_layers
        self.debug = debug or ()
        self.phases = phases
        nc = self.nc = bass.Bass("TRN2", target_bir_lowering=False)
        self.S = Sched(nc)
        dt = nc.dram_tensor
        self.xT = dt("xT", [D, NT], F32, kind="ExternalInput").ap()
        self.pT = dt("pT", [L * 256, NT], F32, kind="ExternalInput").ap()
        self.wf = dt("wf", [L * NSLAB * 128, 1024], F32, kind="ExternalInput").ap()
        self.spm = dt("spm", [128, L * SP_N], F32, kind="ExternalInput").ap()
        self.ssmm = dt("ssmm", [L * 128, 4096], F32, kind="ExternalInput").ap()
        self.cst = dt("cst", [128, C_N], F32, kind="ExternalInput").ap()
        self.bctd = dt("bct", [128, 144], F32, kind="ExternalInput").ap()
        self.outT = dt("outT", [D, NT], F32, kind="ExternalOutput").ap()
        self.wb = dt("wb", [L * NSLAB * 128, 1024], BF16, kind="Internal").ap()
        self.qT_d = dt("qT_d", [512, NT], BF16, kind="Internal").ap()
        self.kT_d = dt("kT_d", [512, NT], BF16, kind="Internal").ap()
        self.v_d = dt("v_d", [NT, 512], BF16, kind="Internal").ap()
        self.mixT_d = dt("mixT_d", [D, NT], BF16, kind="ExternalInput" if "in_mix" in self.debug else "Internal").ap()
        self.hcv_d = dt("hcv_d", [256, 32 + NT], BF16, kind="Internal").ap()
        self.u_d = dt("u_d", [256, NT], BF16, kind="Internal").ap()
        self.r_hcv = [[Res("hcv") for _ in range(NB)] for _ in range(2)]
        self.r_u = [[Res("u") for _ in range(NB)] for _ in range(2)]
        self.r_halo = Res("halo")
        self.r_wb = [[Res("wb%d_%d" % (l, c)) for c in range(13)] for l in range(L)]
        self.r_qT = [[Res("qT") for _ in range(NB)] for _ in range(4)]
        self.r_kT = [[Res("kT") for _ in range(NB)] for _ in range(4)]
        self.r_v = [Res("v") for _ in range(NT // 128)]
        self.r_mix = [[Res("mix") for _ in range(NB)] for _ in range(8)]
        self.r_dummy = Res("dummy")
        self.dbg = {}
        for name, shape, d_ in (("d_qT", [512, NT], BF16), ("d_kT", [512, NT], BF16), ("d_v", [NT, 512], BF16),
                                ("d_mix", [D, NT], BF16), ("d_h", [D, NT], F32)):
            if name in self.debug:
                self.dbg[name] = dt(name, shape, d_, kind="ExternalOutput").ap()
        self.A = Arena(nc, 16640, 229376)
        A = self.A
        self.hT = A.alloc("hT", [128, KT * NT], F32)
        self.r_h = [[Res("h%d_%d" % (k, b)) for b in range(NB)] for k in range(KT)]
        self.spm_t = T(A.alloc("spm", [128, L * SP_N], F32), "spm")
        self.cbf = T(A.alloc("cbf", [128, 128 * 9], BF16), "cbf")
        self.misc = T(A.alloc("misc", [128, 16], F32), "misc")
        self.psum = [T(nc.alloc_psum_tensor("ps%d" % i, [128, 512], F32), "ps%d" % i) for i in range(8)]
        self.pnext = 0
        self.nrot = 8
        self.base_mark = A.mark()
        self.cst_t = T(A.alloc("cst", [128, C_N], F32), "cst")
        self.zt = T(A.alloc("zt", [128, 32], BF16), "zt")
        A.reset(self.base_mark)

    def bank(self):
        b = self.psum[self.pnext]
        self.pnext = (self.pnext + 1) % self.nrot
        return b

    def hcol(self, k, b):
        return k * NT + b * TB

    def sp_col(self, l, off, n=1):
        c = l * SP_N + off
        return self.spm_t.h[:, c:c + n]

    def mm(self, out_ap, lhsT, rhs, start, stop, r, w):
        return self.S.op("pe", lambda e: e.matmul(out_ap, lhsT, rhs, start=start, stop=stop), r=r, w=w)

    def act(self, out_ap, in_ap, func, r, w, bias=None, scale=None):
        kw = {}
        if bias is not None:
            kw["bias"] = bias
        if scale is not None:
            kw["scale"] = scale
        return self.S.op("act", lambda e: e.activation(out=out_ap, in_=in_ap, func=func, **kw), r=r, w=w)

    def tt(self, eng, out_ap, in0, in1, op, r, w):
        return self.S.op(eng, lambda e: e.tensor_tensor(out=out_ap, in0=in0, in1=in1, op=op), r=r, w=w)

    def ts(self, eng, out_ap, in0, s1, op0, r, w, s2=None, op1=None):
        if op1 is None:
            return self.S.op(eng, lambda e: e.tensor_scalar(out=out_ap, in0=in0, scalar1=s1, scalar2=None, op0=op0), r=r, w=w)
        return self.S.op(eng, lambda e: e.tensor_scalar(out=out_ap, in0=in0, scalar1=s1, scalar2=s2, op0=op0, op1=op1), r=r, w=w)

    def stt(self, eng, out_ap, in0, sc, in1, op0, op1, r, w):
        return self.S.op(eng, lambda e: e.scalar_tensor_tensor(out=out_ap, in0=in0, scalar=sc, in1=in1, op0=op0, op1=op1), r=r, w=w)

    def cp(self, eng, out_ap, in_ap, r, w):
        return self.S.op(eng, lambda e: e.tensor_copy(out=out_ap, in_=in_ap), r=r, w=w)

    def memset(self, eng, ap, val, w):
        return self.S.op(eng, lambda e: e.memset(ap, val), r=(), w=w)

    def ring_init(self, nslot):
        self.ring = [T(self.A.alloc("slab", [128, 1024], BF16), "slab%d" % i) for i in range(nslot)]
        self.rnext = 0

    def slab(self, l, s):
        t = self.ring[self.rnext]
        self.rnext = (self.rnext + 1) % len(self.ring)
        row = (l * NSLAB + s) * 128
        self.S.dma("sp", t.h[:, :], self.wb[row:row + 128, :], r=[self.r_wb[l][s // 8]], w=[t.r])
        return t

    def cast_weights(self, l):
        for c in range(13):
            r0 = (l * NSLAB + c * 8) * 128
            r1 = (l * NSLAB + min(NSLAB, c * 8 + 8)) * 128
            self.S.dma("pool", self.wb[r0:r1, :], self.wf[r0:r1, :], r=(), w=[self.r_wb[l][c]])

    def prologue(self):
        S = self.S
        self.cast_weights(0)
        S.dma("sp", self.cst_t.h[:, :], self.cst, w=[self.cst_t.r])
        S.dma("sp", self.spm_t.h[:, :], self.spm, w=[self.spm_t.r])
        for b in range(NB):
            for k in range(KT):
                c = self.hcol(k, b)
                S.dma("sp", self.hT[:, c:c + TB], self.xT[k * 128:(k + 1) * 128, b * TB:(b + 1) * TB], w=[self.r_h[k][b]])
        cb = self.cbf
        self.cp("dve", cb.h[:, 0:256], self.cst_t.h[:, 0:256], r=[self.cst_t.r], w=[cb.r])
        self.memset("dve", cb.h[:, 256:384], 1.0 / 1024, w=[cb.r])
        self.memset("dve", cb.h[:, 384:512], 1.0 / 128, w=[cb.r])
        self.memset("dve", cb.h[:, 512:640], 1.0, w=[cb.r])
        self.cp("dve", cb.h[:, 640:1152], self.cst_t.h[:, C_DH:C_DH + 512], r=[self.cst_t.r], w=[cb.r])
        self.memset("dve", self.misc.h[:, 0:1], EPS, w=[self.misc.r])
        self.memset("dve", self.misc.h[:, 1:2], math.pi / 2, w=[self.misc.r])
        self.memset("dve", self.misc.h[:, 2:3], 0.0, w=[self.misc.r])
        zt = self.zt
        self.memset("dve", zt.h[:, :], 0.0, w=[zt.r])
        for ct in range(2):
            S.dma("sp", self.hcv_d[ct * 128:(ct + 1) * 128, 0:32], zt.h[:, :], r=[zt.r], w=[self.r_halo])
        self.ident_bf = cb.h[:, 0:128]
        self.bd64 = cb.h[:, 128:256]
        self.ones1024 = cb.h[:, 256:384]
        self.ones128 = cb.h[:, 384:512]
        self.ones1 = cb.h[:, 512:640]
        self.eps_ap = self.misc.h[:, 0:1]
        self.halfpi_ap = self.misc.h[:, 1:2]
        S.barrier()

    def dh(self, h):
        return self.cbf.h[:, 640 + 128 * h:640 + 128 * (h + 1)]

    def rstd_from(self, ps_t, n, out_t, eps_ap=None):
        eps_ap = self.eps_ap if eps_ap is None else eps_ap
        self.act(out_t.h[:, 0:n], ps_t.h[:, 0:n], AF.Ln, r=[ps_t.r, self.misc.r], w=[out_t.r], bias=eps_ap)
        self.act(out_t.h[:, 0:n], out_t.h[:, 0:n], AF.Exp, r=[out_t.r], w=[out_t.r], scale=-0.5)

    def rmsnorm_a(self, b, sq8):
        for k in range(KT):
            c = self.hcol(k, b)
            self.act(sq8.h[:, k * TB:(k + 1) * TB], self.hT[:, c:c + TB], AF.Square, r=[self.r_h[k][b]], w=[sq8.sub[k]])

    def rmsnorm_b(self, l, goff, b, nT, sq8, rstd):
        ps = self.bank()
        for k in range(KT):
            self.mm(ps.h[:, :], self.ones1024, sq8.h[:, k * TB:(k + 1) * TB], k == 0, k == KT - 1, r=[sq8.sub[k], self.cbf.r], w=[ps.r])
        self.rstd_from(ps, TB, rstd)
        for k in range(KT):
            c = self.hcol(k, b)
            self.stt("dve", nT.h[:, k * TB:(k + 1) * TB], self.hT[:, c:c + TB], self.sp_col(l, goff + k), rstd.h[:, :],
                     ALU.mult, ALU.mult, r=[self.r_h[k][b], self.spm_t.r, rstd.r], w=[nT.sub[k]])

    def rmsnorm(self, l, goff, b, nT, sq, rstd):
        ps = self.bank()
        for k in range(KT):
            c = self.hcol(k, b)
            s = sq[k % len(sq)]
            self.act(s.h[:, :], self.hT[:, c:c + TB], AF.Square, r=[self.r_h[k][b]], w=[s.r])
            self.mm(ps.h[:, :], self.ones1024, s.h[:, :], k == 0, k == KT - 1, r=[s.r, self.cbf.r], w=[ps.r])
        self.rstd_from(ps, TB, rstd)
        for k in range(KT):
            c = self.hcol(k, b)
            eng = "dve"
            self.stt(eng, nT.h[:, k * TB:(k + 1) * TB], self.hT[:, c:c + TB], self.sp_col(l, goff + k), rstd.h[:, :],
                     ALU.mult, ALU.mult, r=[self.r_h[k][b], self.spm_t.r, rstd.r], w=[nT.sub[k]])


    def phase1(self, l):
        A, S = self.A, self.S
        m0 = A.mark()
        self.ring_init(8)
        nTs = [T(A.alloc("nT", [128, KT * TB], BF16), "nT%d" % i, KT) for i in range(2)]
        sq8 = T(A.alloc("sq8", [128, KT * TB], BF16), "sq8", KT)
        rstd = T(A.alloc("rstd", [128, TB], F32), "rstd")
        sqk = [T(A.alloc("sqk", [128, TB], BF16), "sqk%d" % i) for i in range(2)]
        rs2 = [T(A.alloc("rs2", [128, TB], F32), "rs2%d" % i) for i in range(2)]
        ob = [T(A.alloc("ob", [128, TB], BF16), "ob%d" % i) for i in range(4)]
        sg = [T(A.alloc("sg", [128, TB], F32), "sg%d" % i) for i in range(2)]
        onext = [0]

        def obuf():
            t = ob[onext[0]]
            onext[0] = (onext[0] + 1) % len(ob)
            return t
        self.ts("dve", self.misc.h[:, 3:4], self.sp_col(l, SP_QG), 0.125, ALU.mult, r=[self.spm_t.r], w=[self.misc.r])
        self.rmsnorm_a(0, sq8)
        self.rmsnorm_b(l, SP_G1, 0, nTs[0], sq8, rstd)
        for b in range(NB):
            nT = nTs[b % 2]

            def qk_post(mt, ps, b=b):
                i = mt % 2
                ps2 = self.bank()
                self.mm(ps2.h[:, :], self.bd64, sqk[i].h[:, :], True, True, r=[sqk[i].r, self.cbf.r], w=[ps2.r])
                self.rstd_from(ps2, TB, rs2[i])
                gain = self.misc.h[:, 3:4] if mt < 4 else self.sp_col(l, SP_KG)
                o = obuf()
                self.stt("dve", o.h[:, :], ps.h[:, :], gain, rs2[i].h[:, :], ALU.mult, ALU.mult,
                         r=[ps.r, rs2[i].r, self.misc.r, self.spm_t.r], w=[o.r])
                h = mt % 4
                dst, rr = (self.qT_d, self.r_qT) if mt < 4 else (self.kT_d, self.r_kT)
                S.dma("pool", dst[h * 128:(h + 1) * 128, b * TB:(b + 1) * TB], o.h[:, :], r=[o.r], w=[rr[h][b]])
            pend = None
            for mt in range(8):
                sl = self.slab(l, SLAB_WIN + mt)
                ps = self.bank()
                for k in range(KT):
                    self.mm(ps.h[:, :], sl.h[:, k * 128:(k + 1) * 128], nT.h[:, k * TB:(k + 1) * TB], k == 0, k == KT - 1,
                            r=[sl.r, nT.sub[k]], w=[ps.r])
                self.act(sqk[mt % 2].h[:, :], ps.h[:, :], AF.Square, r=[ps.r], w=[sqk[mt % 2].r])
                if pend is not None:
                    qk_post(*pend)
                pend = (mt, ps)
            if b + 1 < NB:
                self.rmsnorm_a(b + 1, sq8)
            vs = [self.slab(l, SLAB_WIN + 8 + j) for j in range(4)]
            for tt in range(4):
                ps = self.bank()
                for k in range(KT):
                    sl = vs[k // 2]
                    self.mm(ps.h[:, :], nT.h[:, k * TB + tt * 128:k * TB + (tt + 1) * 128], sl.h[:, (k % 2) * 512:(k % 2 + 1) * 512],
                            k == 0, k == KT - 1, r=[sl.r, nT.sub[k]], w=[ps.r])
                if tt == 0:
                    qk_post(*pend)
                o = obuf()
                self.act(o.h[:, :], ps.h[:, :], AF.Copy, r=[ps.r], w=[o.r])
                gt = b * 4 + tt
                S.dma("pool", self.v_d[gt * 128:(gt + 1) * 128, :], o.h[:, :], r=[o.r], w=[self.r_v[gt]])
            cs = [self.slab(l, SLAB_WIN + 12 + j) for j in range(4)]
            for ct in range(2):
                psa = self.bank()
                psg = self.bank()
                for k in range(KT):
                    self.mm(psa.h[:, :], cs[ct].h[:, k * 128:(k + 1) * 128], nT.h[:, k * TB:(k + 1) * TB], k == 0, k == KT - 1,
                            r=[cs[ct].r, nT.sub[k]], w=[psa.r])
                for k in range(KT):
                    self.mm(psg.h[:, :], cs[2 + ct].h[:, k * 128:(k + 1) * 128], nT.h[:, k * TB:(k + 1) * TB], k == 0, k == KT - 1,
                            r=[cs[2 + ct].r, nT.sub[k]], w=[psg.r])
                self.act(sg[ct].h[:, :], psg.h[:, :], AF.Sigmoid, r=[psg.r], w=[sg[ct].r])
                o = obuf()
                self.tt("dve", o.h[:, :], psa.h[:, :], sg[ct].h[:, :], ALU.mult, r=[psa.r, sg[ct].r], w=[o.r])
                S.dma("pool", self.hcv_d[ct * 128:(ct + 1) * 128, 32 + b * TB:32 + (b + 1) * TB], o.h[:, :], r=[o.r], w=[self.r_hcv[ct][b]])
            for ct in range(2):
                sl = self.slab(l, SLAB_WIN + 16 + ct)
                ps = self.bank()
                for k in range(KT):
                    self.mm(ps.h[:, :], sl.h[:, k * 128:(k + 1) * 128], nT.h[:, k * TB:(k + 1) * TB], k == 0, k == KT - 1,
                            r=[sl.r, nT.sub[k]], w=[ps.r])
                o = obuf()
                self.act(o.h[:, :], ps.h[:, :], AF.Copy, r=[ps.r], w=[o.r])
                S.dma("pool", self.u_d[ct * 128:(ct + 1) * 128, b * TB:(b + 1) * TB], o.h[:, :], r=[o.r], w=[self.r_u[ct][b]])
            if b + 1 < NB:
                self.rmsnorm_b(l, SP_G1, b + 1, nTs[(b + 1) % 2], sq8, rstd)
        S.barrier()
        A.reset(m0)

    def phase2(self, l):
        A, S = self.A, self.S
        m0 = A.mark()
        dg = T(A.alloc("dg", [128, 62 * 128], BF16), "dg", 62)
        idf = T(A.alloc("idf", [128, 128], F32), "idf")
        onesf = T(A.alloc("onesf", [128, 128], F32), "onesf")
        hc = [T(A.alloc("hc", [128, 2 * 544], BF16), "hc%d" % i, 2) for i in range(2)]
        cvs = [T(A.alloc("cv", [128, 2 * TB], F32), "cv%d" % i, 2) for i in range(2)]
        cens = [T(A.alloc("cen", [128, 2 * TB], F32), "cen%d" % i, 2) for i in range(2)]
        sqfs = [T(A.alloc("sqf", [128, 2 * TB], F32), "sqf%d" % i, 2) for i in range(2)]
        rss = [T(A.alloc("rs", [128, TB], F32), "rs%d" % i) for i in range(2)]
        obs = [T(A.alloc("ob", [128, TB], BF16), "ob%d" % i) for i in range(4)]
        S.dma("sp", idf.h[:, :], self.cst[:, C_ID:C_ID + 128], w=[idf.r])
        self.memset("dve", onesf.h[:, :], 1.0 / 256, w=[onesf.r])
        for ct in range(2):
            for j in range(31):
                i = ct * 31 + j
                self.ts("dve" if i % 2 else "pool", dg.h[:, i * 128:(i + 1) * 128], idf.h[:, :], self.sp_col(l, SP_CW + i), ALU.mult,
                        r=[idf.r, self.spm_t.r], w=[dg.sub[i]])
        def stage_a(b):
            h_ = hc[b % 2]
            cv = cvs[b % 2]
            for ct in range(2):
                rr = [self.r_hcv[ct][b], self.r_halo] + ([self.r_hcv[ct][b - 1]] if b else [])
                S.dma("sp", h_.h[:, ct * 544:ct * 544 + 544], self.hcv_d[ct * 128:(ct + 1) * 128, b * TB:b * TB + 544], r=rr, w=[h_.sub[ct]])
            for ct in range(2):
                ps = self.bank()
                for j in range(31):
                    i = ct * 31 + j
                    self.mm(ps.h[:, :], dg.h[:, i * 128:(i + 1) * 128], h_.h[:, ct * 544 + 2 + j:ct * 544 + 2 + j + TB], j == 0, j == 30,
                            r=[dg.sub[i], h_.sub[ct]], w=[ps.r])
                self.act(cv.h[:, ct * TB:(ct + 1) * TB], ps.h[:, :], AF.Identity, r=[ps.r, self.spm_t.r], w=[cv.sub[ct]],
                         bias=self.sp_col(l, SP_CB + ct))

        def stage_b(b):
            cv, cen, sqf, rs, ob = cvs[b % 2], cens[b % 2], sqfs[b % 2], rss[b % 2], obs[(b % 2) * 2:(b % 2) * 2 + 2]
            psm = self.bank()
            for ct in range(2):
                self.mm(psm.h[:, :], onesf.h[:, :], cv.h[:, ct * TB:(ct + 1) * TB], ct == 0, ct == 1, r=[onesf.r, cv.sub[ct]], w=[psm.r])
            for ct in range(2):
                self.tt("dve", cen.h[:, ct * TB:(ct + 1) * TB], cv.h[:, ct * TB:(ct + 1) * TB], psm.h[:, :], ALU.subtract,
                        r=[cv.sub[ct], psm.r], w=[cen.sub[ct]])
                self.act(sqf.h[:, ct * TB:(ct + 1) * TB], cen.h[:, ct * TB:(ct + 1) * TB], AF.Square, r=[cen.sub[ct]], w=[sqf.sub[ct]])
            psv = self.bank()
            for ct in range(2):
                self.mm(psv.h[:, :], onesf.h[:, :], sqf.h[:, ct * TB:(ct + 1) * TB], ct == 0, ct == 1, r=[onesf.r, sqf.sub[ct]], w=[psv.r])
            self.rstd_from(psv, TB, rs)
            for ct in range(2):
                self.tt("dve", cen.h[:, ct * TB:(ct + 1) * TB], cen.h[:, ct * TB:(ct + 1) * TB], rs.h[:, :], ALU.mult,
                        r=[rs.r, cen.sub[ct]], w=[cen.sub[ct]])
                o = ob[ct]
                self.act(o.h[:, :], cen.h[:, ct * TB:(ct + 1) * TB], AF.Silu, r=[cen.sub[ct], self.spm_t.r], w=[o.r],
                         bias=self.sp_col(l, SP_LNB + ct), scale=self.sp_col(l, SP_LNG + ct))
                S.dma("sp", self.mixT_d[512 + ct * 128:512 + (ct + 1) * 128, b * TB:(b + 1) * TB], o.h[:, :], r=[o.r], w=[self.r_mix[4 + ct][b]])
        stage_a(0)
        for b in range(NB):
            if b + 1 < NB:
                stage_a(b + 1)
            stage_b(b)
        S.barrier()
        A.reset(m0)


    def phase3(self, l):
        A, S = self.A, self.S
        m0 = A.mark()
        cosT = T(A.alloc("cosT", [128, 8 * TS], F32), "cosT", 8)
        sinT = T(A.alloc("sinT", [128, 8 * TS], F32), "sinT", 8)
        rT = T(A.alloc("rT", [128, 8 * TS], F32), "rT", 8)
        Bb = T(A.alloc("Bb", [128, 16 * 128], BF16), "Bb")
        CA = T(A.alloc("CA", [128, 8 * 128], BF16), "CA")
        CB = T(A.alloc("CB", [128, 8 * 128], BF16), "CB")
        glu = T(A.alloc("glu", [128, 1024], BF16), "glu")
        CAn = T(A.alloc("CAn", [128, 8 * 128], BF16), "CAn")
        idf = T(A.alloc("idf", [128, 256], F32), "idf")
        dD = T(A.alloc("dD", [128, 256], BF16), "dD")
        sm = T(A.alloc("sm", [128, 8 * 24], F32), "sm")
        ini = [T(A.alloc("ini", [128, 16], F32), "ini%d" % i, 8) for i in range(2)]
        m1 = A.mark()

        def col(i, k=None):
            return sm.h[:, 8 * i:8 * i + 8] if k is None else sm.h[:, 8 * i + k:8 * i + k + 1]
        (DT, TH, LR, R_, C_, S_, T1, T2, T3, ABR, ABI, DEN, INV, BRE, BIM, NBR, NBI, U1, U2, CT, ST) = range(21)
        spc = self.spm_t.r
        are, aim, ldt = self.sp_col(l, SP_ARE, 8), self.sp_col(l, SP_AIM, 8), self.sp_col(l, SP_LDT, 8)

        def sact(o, i, f, **kw):
            self.act(col(o), i, f, r=[sm.r, spc, self.misc.r], w=[sm.r], **kw)

        def stt_(o, a, b, op):
            self.tt("dve", col(o), a, b, op, r=[sm.r, spc], w=[sm.r])
        sact(DT, ldt, AF.Exp)
        stt_(TH, aim, col(DT), ALU.mult)
        stt_(LR, are, col(DT), ALU.mult)
        sact(R_, col(LR), AF.Exp)
        sact(S_, col(TH), AF.Sin, scale=1.0 / 32)
        sact(C_, col(TH), AF.Sin, scale=1.0 / 32, bias=self.halfpi_ap)

        def square_cs():
            stt_(T1, col(C_), col(C_), ALU.mult)
            stt_(T2, col(S_), col(S_), ALU.mult)
            stt_(T3, col(C_), col(S_), ALU.mult)
            stt_(C_, col(T1), col(T2), ALU.subtract)
            self.ts("dve", col(S_), col(T3), 2.0, ALU.mult, r=[sm.r], w=[sm.r])
        for _ in range(5):
            square_cs()
        stt_(ABR, col(R_), col(C_), ALU.mult)
        stt_(ABI, col(R_), col(S_), ALU.mult)
        self.ts("dve", col(ABR), col(ABR), -1.0, ALU.add, r=[sm.r], w=[sm.r])
        stt_(T1, are, are, ALU.mult)
        stt_(T2, aim, aim, ALU.mult)
        stt_(DEN, col(T1), col(T2), ALU.add)
        S.op("dve", lambda e: e.reciprocal(out=col(INV), in_=col(DEN)), r=[sm.r], w=[sm.r])
        stt_(T1, col(ABR), are, ALU.mult)
        stt_(T2, col(ABI), aim, ALU.mult)
        stt_(T3, col(T1), col(T2), ALU.add)
        stt_(BRE, col(T3), col(INV), ALU.mult)
        stt_(T1, col(ABI), are, ALU.mult)
        stt_(T2, col(ABR), aim, ALU.mult)
        stt_(T3, col(T1), col(T2), ALU.subtract)
        stt_(BIM, col(T3), col(INV), ALU.mult)
        self.ts("dve", col(NBI), col(BIM), -1.0, ALU.mult, r=[sm.r], w=[sm.r])
        tmp = T(A.alloc("tmp", [128, 2 * TS], F32), "tmp")
        onesr = T(A.alloc("onesr", [128, TS], F32), "onesr")
        Cst = T(A.alloc("Cst", [128, 2048], F32), "Cst")
        self.memset("dve", onesr.h[:, :], 1.0, w=[onesr.r])
        for k in range(8):
            self.memset("dve", cosT.h[:, k * TS:k * TS + 1], 1.0, w=[cosT.sub[k]])
            self.memset("dve", sinT.h[:, k * TS:k * TS + 1], 0.0, w=[sinT.sub[k]])
            self.act(rT.h[:, k * TS:(k + 1) * TS], onesr.h[:, :], AF.Copy, r=[onesr.r, sm.r], w=[rT.sub[k]], scale=col(R_, k))
        tb = T(A.alloc("tb", [128, 2048], F32), "tb")
        c3 = cosT.h[:, :].rearrange("p (k t) -> p k t", k=8)
        s3 = sinT.h[:, :].rearrange("p (k t) -> p k t", k=8)
        allr = [sm.r] + cosT.sub + sinT.sub
        m = 1
        while m < TS:
            cb = col(C_).rearrange("p (k o) -> p k o", o=1).broadcast_to([128, 8, m])
            sb = col(S_).rearrange("p (k o) -> p k o", o=1).broadcast_to([128, 8, m])
            t1 = tb.h[:, 0:8 * m].rearrange("p (k t) -> p k t", k=8)
            t2 = tb.h[:, 1024:1024 + 8 * m].rearrange("p (k t) -> p k t", k=8)
            pre, pim = c3[:, :, 0:m], s3[:, :, 0:m]
            self.tt("dve", t1, pre, cb, ALU.mult, r=allr, w=[tb.r])
            self.tt("dve", t2, pim, sb, ALU.mult, r=allr, w=[tb.r])
            self.tt("dve", c3[:, :, m:2 * m], t1, t2, ALU.subtract, r=[tb.r], w=cosT.sub)
            self.tt("dve", t1, pre, sb, ALU.mult, r=allr, w=[tb.r])
            self.tt("dve", t2, pim, cb, ALU.mult, r=allr, w=[tb.r])
            self.tt("dve", s3[:, :, m:2 * m], t1, t2, ALU.add, r=[tb.r], w=sinT.sub)
            square_cs()
            m *= 2
        self.cp("dve", col(CT), col(C_), r=[sm.r], w=[sm.r])
        self.cp("dve", col(ST), col(S_), r=[sm.r], w=[sm.r])
        S.dma("sp", idf.h[:, 0:128], self.cst[:, C_ID:C_ID + 128], w=[idf.r])
        self.ts("dve", idf.h[:, 128:256], idf.h[:, 0:128], -1.0, ALU.mult, r=[idf.r], w=[idf.r])
        r0 = l * 128
        S.dma("pool", Bb.h[:, :], self.ssmm[r0:r0 + 128, 0:2048], w=[Bb.r])
        S.dma("sp", Cst.h[:, :], self.ssmm[r0:r0 + 128, 2048:4096], w=[Cst.r])
        row = (l * NSLAB + SLAB_GLU) * 128
        S.dma("sp", glu.h[:, :], self.wb[row:row + 128, :], r=[self.r_wb[l][SLAB_GLU // 8]], w=[glu.r])
        for k in range(8):
            cre, cim = Cst.h[:, k * 128:(k + 1) * 128], Cst.h[:, (8 + k) * 128:(9 + k) * 128]
            self.ts("dve", tmp.h[:, 0:128], cim, col(BIM, k), ALU.mult, r=[Cst.r, sm.r], w=[tmp.r])
            self.stt("dve", CA.h[:, k * 128:(k + 1) * 128], cre, col(BRE, k), tmp.h[:, 0:128], ALU.mult, ALU.subtract, r=[Cst.r, sm.r, tmp.r], w=[CA.r])
            self.ts("dve", tmp.h[:, 128:256], cim, col(BRE, k), ALU.mult, r=[Cst.r, sm.r], w=[tmp.r])
            self.stt("dve", CB.h[:, k * 128:(k + 1) * 128], cre, col(NBI, k), tmp.h[:, 128:256], ALU.mult, ALU.subtract, r=[Cst.r, sm.r, tmp.r], w=[CB.r])
        self.ts("dve", CAn.h[:, :], CA.h[:, :], -1.0, ALU.mult, r=[CA.r], w=[CAn.r])
        for ct in range(2):
            self.ts("dve", dD.h[:, ct * 128:(ct + 1) * 128], idf.h[:, 0:128], self.sp_col(l, SP_SD + ct), ALU.mult, r=[idf.r, spc], w=[dD.r])
        S.barrier()
        A.reset(m1)
        Aw = [T(A.alloc("Aw", [128, 2 * TS], F32), "Aw%d" % i) for i in range(2)]
        Bw = [T(A.alloc("Bw", [128, 2 * TS], F32), "Bw%d" % i) for i in range(2)]
        vv = [T(A.alloc("vv", [128, 2 * TS], F32), "vv%d" % i) for i in range(2)]
        A2 = [T(A.alloc("A2", [128, 2 * TS], BF16), "A2%d" % i) for i in range(2)]
        B2 = [T(A.alloc("B2", [128, 2 * TS], BF16), "B2%d" % i) for i in range(2)]
        ub = [T(A.alloc("ub", [128, 2 * TS], BF16), "ub%d" % i, 2) for i in range(2)]
        ysb = T(A.alloc("ysb", [128, TS], F32), "ysb")
        g1 = T(A.alloc("g1", [128, TS], F32), "g1")
        g2 = T(A.alloc("g2", [128, TS], F32), "g2")
        yg = T(A.alloc("yg", [128, 2 * TS], BF16), "yg", 2)
        sgl = T(A.alloc("sgl", [128, TS], F32), "sgl")
        ob = [T(A.alloc("ob", [128, TS], BF16), "ob%d" % i) for i in range(2)]
        it = [T(A.alloc("it", [128, 2], F32), "it%d" % i) for i in range(2)]
        self.ts("dve", col(T1), col(ST), -1.0, ALU.mult, r=[sm.r], w=[sm.r])
        steps = [(c, k) for c in range(NCH) for k in range(8)]
        yps_h = [None]
        wps_h = {}
        Id, nId = idf.h[:, 0:128], idf.h[:, 128:256]

        def h2(ap):
            return ap.rearrange("p (h t) -> p h t", h=2)

        def bc2(ap):
            return ap.rearrange("p (o t) -> p o t", o=1).broadcast_to([128, 2, TS])

        bps_h = {}

        def stage0(i):
            c, k = steps[i]
            t0 = c * TS
            u_ = ub[c % 3]
            if k == 0:
                for ct in range(2):
                    S.dma("sp", u_.h[:, ct * TS:(ct + 1) * TS], self.u_d[ct * 128:(ct + 1) * 128, t0:t0 + TS], r=[self.r_u[ct][t0 // TB]], w=[u_.sub[ct]])
            ct = k // 4
            bps = self.bank()
            bps_h[i] = bps
            self.mm(bps.h[:, 0:TS], Bb.h[:, k * 128:(k + 1) * 128], u_.h[:, ct * TS:(ct + 1) * TS], True, True, r=[Bb.r, u_.sub[ct]], w=[bps.r])
            self.mm(bps.h[:, TS:2 * TS], Bb.h[:, (8 + k) * 128:(9 + k) * 128], u_.h[:, ct * TS:(ct + 1) * TS], True, True, r=[Bb.r, u_.sub[ct]], w=[bps.r])

        def stage1(i):
            c, k = steps[i]
            i2 = i % 2
            o = k * TS
            bps = bps_h.pop(i)
            a_, b_ = Aw[i2], Bw[i2]
            ck, sk = cosT.h[:, o:o + TS], sinT.h[:, o:o + TS]
            self.tt("dve", h2(a_.h[:, :]), h2(bps.h[:, :]), bc2(ck), ALU.mult, r=[bps.r, cosT.sub[k]], w=[a_.r])
            self.tt("dve", h2(b_.h[:, :]), h2(bps.h[:, :]), bc2(sk), ALU.mult, r=[bps.r, sinT.sub[k]], w=[b_.r])
            wps = self.bank()
            wps_h[i] = wps
            self.mm(wps.h[:, 0:TS], Id, a_.h[:, 0:TS], True, False, r=[idf.r, a_.r], w=[wps.r])
            self.mm(wps.h[:, 0:TS], Id, b_.h[:, TS:2 * TS], False, True, r=[idf.r, b_.r], w=[wps.r])
            self.mm(wps.h[:, TS:2 * TS], Id, a_.h[:, TS:2 * TS], True, False, r=[idf.r, a_.r], w=[wps.r])
            self.mm(wps.h[:, TS:2 * TS], nId, b_.h[:, 0:TS], False, True, r=[idf.r, b_.r], w=[wps.r])

        def stage2(i):
            c, k = steps[i]
            t0 = c * TS
            u_ = ub[c % 3]
            ct = k // 4
            i2 = i % 2
            o = k * TS
            ini_c, ini_n = ini[c % 2], ini[(c + 1) % 2]
            v_, a2, b2 = vv[i2], A2[i2], B2[i2]
            wps = wps_h.pop(i)
            ck, sk = cosT.h[:, o:o + TS], sinT.h[:, o:o + TS]
            for hf in range(2):
                init = 0.0 if c == 0 else ini_c.h[:, hf * 8 + k:hf * 8 + k + 1]
                S.op("dve", lambda e, hf=hf, v_=v_, wps=wps, init=init, o=o: e.tensor_tensor_scan(
                    out=v_.h[:, hf * TS:(hf + 1) * TS], data0=rT.h[:, o:o + TS], data1=wps.h[:, hf * TS:(hf + 1) * TS],
                    initial=init, op0=ALU.mult, op1=ALU.add), r=[rT.sub[k], wps.r] + ([ini_c.sub[k]] if c else []), w=[v_.r])
            if c < NCH - 1:
                vre, vim = v_.h[:, TS - 1:TS], v_.h[:, 2 * TS - 1:2 * TS]
                it_ = it[i2]
                self.act(it_.h[:, 0:1], vim, AF.Copy, r=[v_.r, sm.r], w=[it_.r], scale=col(T1, k))
                self.act(it_.h[:, 1:2], vre, AF.Copy, r=[v_.r, sm.r], w=[it_.r], scale=col(ST, k))
                self.act(ini_n.h[:, k:k + 1], vre, AF.Identity, r=[v_.r, sm.r, it_.r], w=[ini_n.sub[k]], scale=col(CT, k), bias=it_.h[:, 0:1])
                self.act(ini_n.h[:, 8 + k:9 + k], vim, AF.Identity, r=[v_.r, sm.r, it_.r], w=[ini_n.sub[k]], scale=col(CT, k), bias=it_.h[:, 1:2])
            self.tt("pool", h2(a2.h[:, :]), h2(v_.h[:, :]), bc2(ck), ALU.mult, r=[v_.r, cosT.sub[k]], w=[a2.r])
            self.tt("pool", h2(b2.h[:, :]), h2(v_.h[:, :]), bc2(sk), ALU.mult, r=[v_.r, sinT.sub[k]], w=[b2.r])
            yps = self.psum[6 + ct]
            kk = slice(k * 128, (k + 1) * 128)
            self.mm(yps.h[:, 0:TS], CA.h[:, kk], a2.h[:, 0:TS], k % 4 == 0, False, r=[CA.r, a2.r], w=[yps.r])
            self.mm(yps.h[:, 0:TS], CAn.h[:, kk], b2.h[:, TS:2 * TS], False, False, r=[CAn.r, b2.r], w=[yps.r])
            self.mm(yps.h[:, 0:TS], CB.h[:, kk], b2.h[:, 0:TS], False, False, r=[CB.r, b2.r], w=[yps.r])
            self.mm(yps.h[:, 0:TS], CB.h[:, kk], a2.h[:, TS:2 * TS], False, False, r=[CB.r, a2.r], w=[yps.r])
            if k % 4 == 3:
                self.mm(yps.h[:, 0:TS], dD.h[:, ct * 128:(ct + 1) * 128], u_.h[:, ct * TS:(ct + 1) * TS], False, True, r=[dD.r, u_.sub[ct]], w=[yps.r])

        def epilogue(i):
            c, k = steps[i]
            t0 = c * TS
            ct = k // 4
            yps = self.psum[6 + ct]
            self.act(yg.h[:, ct * TS:(ct + 1) * TS], yps.h[:, 0:TS], AF.Gelu_apprx_tanh, r=[yps.r], w=[yg.sub[ct]])
            if k == 7:
                for mt in range(2):
                    zps = self.psum[5]
                    for kt in range(2):
                        a = mt * 2 + kt
                        self.mm(zps.h[:, mt * TS:(mt + 1) * TS], glu.h[:, a * 128:(a + 1) * 128], yg.h[:, kt * TS:(kt + 1) * TS], kt == 0, kt == 1, r=[glu.r, yg.sub[kt]], w=[zps.r])
                    self.act(sgl.h[:, :], zps.h[:, mt * TS:(mt + 1) * TS], AF.Sigmoid, r=[zps.r, spc], w=[sgl.r], bias=self.sp_col(l, SP_GB + mt))
                    o_ = ob[mt]
                    self.tt("pool", o_.h[:, :], yg.h[:, mt * TS:(mt + 1) * TS], sgl.h[:, :], ALU.mult, r=[yg.sub[mt], sgl.r], w=[o_.r])
                    S.dma("sp", self.mixT_d[768 + mt * 128:768 + (mt + 1) * 128, t0:t0 + TS], o_.h[:, :], r=[o_.r], w=[self.r_mix[6 + mt][t0 // TB]])
        ub.append(T(A.alloc("ub", [128, 2 * TS], BF16), "ub2", 2))
        n = len(steps)
        self.nrot = 5
        self.pnext = 0
        stage0(0)
        stage0(1)
        stage1(0)
        for i in range(n):
            if i + 2 < n:
                stage0(i + 2)
            if i + 1 < n:
                stage1(i + 1)
            stage2(i)
            if i >= 2 and steps[i - 2][1] % 4 == 3:
                epilogue(i - 2)
        for i in (n - 2, n - 1):
            if steps[i][1] % 4 == 3:
                epilogue(i)
        self.nrot = 8
        S.barrier()
        A.reset(m0)

    def phase4(self, l):
        A, S = self.A, self.S
        m0 = A.mark()
        lam_init = 0.8 - 0.6 * math.exp(-0.3 * l)
        QTh = T(A.alloc("QTh", [128, NT], BF16), "QTh", NB)
        KTh = T(A.alloc("KTh", [128, NT], BF16), "KTh", NB)
        Vh = T(A.alloc("Vh", [128, NT], BF16), "Vh", 32)
        bct = T(A.alloc("bct", [128, 4 * 36], F32), "bct")
        pt = [T(A.alloc("pt", [128, TB], BF16), "pt%d" % i) for i in range(4)]
        fin = [T(A.alloc("fin", [128, TB], F32), "fin%d" % i) for i in range(5)]
        sqa = T(A.alloc("sqa", [128, TB], BF16), "sqa")
        rs = T(A.alloc("rs", [128, TB], F32), "rs")
        ob = [T(A.alloc("ob", [128, TB], BF16), "ob%d" % i) for i in range(2)]
        lt = T(A.alloc("lt", [128, 128], F32), "lt")
        mc = self.misc
        S.dma("sp", bct.h[:, :], self.bctd, w=[bct.r])
        for j in range(2):
            c = l * SP_N + SP_LAM + 128 * j
            self.tt("dve", lt.h[:, 64 * j:64 * j + 64], self.spm_t.h[:, c:c + 64], self.spm_t.h[:, c + 64:c + 128], ALU.mult,
                    r=[self.spm_t.r], w=[lt.r])
            S.op("dve", lambda e, j=j: e.reduce_sum(out=mc.h[:, 8 + j:9 + j], in_=lt.h[:, 64 * j:64 * j + 64], axis=mybir.AxisListType.X),
                 r=[lt.r], w=[mc.r])
        self.act(mc.h[:, 8:10], mc.h[:, 8:10], AF.Exp, r=[mc.r], w=[mc.r])
        self.tt("dve", mc.h[:, 5:6], mc.h[:, 9:10], mc.h[:, 8:9], ALU.subtract, r=[mc.r], w=[mc.r])
        self.ts("dve", mc.h[:, 5:6], mc.h[:, 5:6], -lam_init, ALU.add, r=[mc.r], w=[mc.r])
        self.ts("dve", mc.h[:, 6:7], self.sp_col(l, SP_OG), 1.0 - lam_init, ALU.mult, r=[self.spm_t.r], w=[mc.r])
        neg_lam = mc.h[:, 5:6]
        ogs = mc.h[:, 6:7]
        O = [self.psum[0], self.psum[1]]
        Ls = [self.psum[2], self.psum[3]]
        sc = [self.psum[4], self.psum[5], self.psum[6], self.psum[7]]
        vd = self.v_d.rearrange("(t p) c -> p t c", p=128)
        pend = [None]
        raw = [T(A.alloc("raw", [128, TB], F32), "raw%d" % i) for i in range(4)]

        def fin1():
            for c in range(2):
                self.cp("dve", raw[c].h[:, :], O[c].h[:, :], r=[O[c].r], w=[raw[c].r])
                self.act(raw[2 + c].h[:, :], Ls[c].h[:, :], AF.Copy, r=[Ls[c].r], w=[raw[2 + c].r])

        def fin2a():
            for c in range(2):
                S.op("dve", lambda e, c=c: e.reciprocal(out=fin[c].h[:, :], in_=raw[2 + c].h[:, :]), r=[raw[2 + c].r], w=[fin[c].r])
                self.tt("dve", fin[2 + c].h[:, :], raw[c].h[:, :], fin[c].h[:, :], ALU.mult, r=[raw[c].r, fin[c].r], w=[fin[2 + c].r])
            a = fin[4]
            self.stt("dve", a.h[:, :], fin[3].h[:, :], neg_lam, fin[2].h[:, :], ALU.mult, ALU.add, r=[fin[2].r, fin[3].r, mc.r], w=[a.r])
            self.act(sqa.h[:, :], a.h[:, :], AF.Square, r=[a.r], w=[sqa.r])

        def fin2b(hq, psn):
            h_, qb_ = hq
            a = fin[4]
            self.mm(psn.h[:, :], self.ones128, sqa.h[:, :], True, True, r=[sqa.r, self.cbf.r], w=[psn.r])
            self.rstd_from(psn, TB, rs)
            o = ob[qb_ % 2]
            self.stt("dve", o.h[:, :], a.h[:, :], ogs, rs.h[:, :], ALU.mult, ALU.mult, r=[a.r, rs.r, mc.r], w=[o.r])
            S.dma("pool", self.mixT_d[h_ * 128:(h_ + 1) * 128, qb_ * TB:(qb_ + 1) * TB], o.h[:, :], r=[o.r], w=[self.r_mix[h_][qb_]])
        for h in range(4):
            G = 256 if h == 0 else 512
            for b in range(NB):
                S.dma("pool", QTh.h[:, b * TB:(b + 1) * TB], self.qT_d[h * 128:(h + 1) * 128, b * TB:(b + 1) * TB], r=[self.r_qT[h][b]], w=[QTh.sub[b]])
                S.dma("pool", KTh.h[:, b * TB:(b + 1) * TB], self.kT_d[h * 128:(h + 1) * 128, b * TB:(b + 1) * TB], r=[self.r_kT[h][b]], w=[KTh.sub[b]])
            for g in range(4):
                S.dma("pool", Vh.h[:, g * 1024:(g + 1) * 1024].rearrange("p (t d) -> p t d", d=128), vd[:, g * 8:(g + 1) * 8, h * 128:(h + 1) * 128],
                      r=self.r_v[g * 8:(g + 1) * 8], w=Vh.sub[g * 8:(g + 1) * 8])
            for qb in range(NB):
                q0 = qb * TB
                nk = 4 * qb + 4

                def geom(kt):
                    jl = kt - 4 * qb
                    return jl, (128 * jl if jl >= 0 else 0)

                def emitS(kt):
                    jl, c0 = geom(kt)
                    k0 = kt * 128
                    for c in range(2):
                        Sb = sc[(kt % 2) * 2 + c]
                        pr = slice(64 * c, 64 * c + 64)
                        self.mm(Sb.h[:, c0:TB], KTh.h[pr, k0:k0 + 128], QTh.h[pr, q0 + c0:q0 + TB], True, jl < 0,
                                r=[KTh.sub[kt // 4], QTh.sub[qb]], w=[Sb.r])
                        if jl >= 0:
                            self.mm(Sb.h[:, c0:c0 + 128], self.ident_bf, self.dh(h), False, True, r=[self.cbf.r], w=[Sb.r])

                def emitE(kt):
                    jl, c0 = geom(kt)
                    for c in range(2):
                        Sb = sc[(kt % 2) * 2 + c]
                        p_ = pt[(kt % 2) * 2 + c]
                        for g in range(TB // G):
                            lo, hi = max(c0, g * G), (g + 1) * G
                            if lo >= hi:
                                continue
                            mprime = (q0 + g * G - kt * 128) // 128
                            bcol = bct.h[:, h * 36 + mprime + 3:h * 36 + mprime + 4]
                            self.act(p_.h[:, lo:hi], Sb.h[:, lo:hi], AF.Exp, r=[Sb.r, bct.r], w=[p_.r], bias=bcol)

                def emitAV(kt):
                    jl, c0 = geom(kt)
                    for c in range(2):
                        p_ = pt[(kt % 2) * 2 + c]
                        self.mm(O[c].h[:, c0:TB], Vh.h[:, kt * 128:(kt + 1) * 128], p_.h[:, c0:TB], kt == 0, kt == nk - 1,
                                r=[Vh.sub[kt], p_.r], w=[O[c].r])
                        self.mm(Ls[c].h[:, c0:TB], self.ones1, p_.h[:, c0:TB], kt == 0, kt == nk - 1, r=[self.cbf.r, p_.r], w=[Ls[c].r])
                emitS(0)
                emitS(1)
                for kt in range(nk):
                    emitE(kt)
                    if pend[0] is not None and kt == 1:
                        fin2a()
                    if pend[0] is not None and kt == min(nk - 1, 10):
                        fin2b(pend[0], sc[(kt % 2) * 2])
                        pend[0] = None
                    if kt + 2 < nk:
                        emitS(kt + 2)
                    emitAV(kt)
                fin1()
                pend[0] = (h, qb)
        fin2a()
        fin2b(pend[0], sc[0])
        S.barrier()
        A.reset(m0)

    def phase5(self, l):
        A, S = self.A, self.S
        m0 = A.mark()
        self.ring_init(6)
        mixb = [T(A.alloc("mixb", [128, KT * TB], BF16), "mixb%d" % i, KT) for i in range(1)]
        nT = T(A.alloc("nT", [128, KT * TB], BF16), "nT", KT)
        hid = T(A.alloc("hid", [128, 32 * TB], BF16), "hid", 32)
        sq = [T(A.alloc("sq", [128, TB], BF16), "sq%d" % i) for i in range(2)]
        rstd = T(A.alloc("rstd", [128, TB], F32), "rstd")
        rl = [T(A.alloc("rl", [128, TB], F32), "rl%d" % i) for i in range(2)]
        pTb = T(A.alloc("pTb", [128, 2 * TB], BF16), "pTb", 2)
        for b in range(NB):
            mb = mixb[0]
            for k in range(KT):
                S.dma("pool", mb.h[:, k * TB:(k + 1) * TB], self.mixT_d[k * 128:(k + 1) * 128, b * TB:(b + 1) * TB],
                      r=[self.r_mix[k][b]], w=[mb.sub[k]])
            for mt in range(KT):
                sl = self.slab(l, SLAB_WOUT + mt)
                ps = self.bank()
                for k in range(KT):
                    self.mm(ps.h[:, :], sl.h[:, k * 128:(k + 1) * 128], mb.h[:, k * TB:(k + 1) * TB], k == 0, k == KT - 1,
                            r=[sl.r, mb.sub[k]], w=[ps.r])
                c = self.hcol(mt, b)
                self.tt("dve", self.hT[:, c:c + TB], self.hT[:, c:c + TB], ps.h[:, :], ALU.add, r=[ps.r], w=[self.r_h[mt][b]])
            self.rmsnorm(l, SP_G2, b, nT, sq, rstd)
            for ft in range(32):
                sl = self.slab(l, SLAB_W1 + ft)
                ps = self.bank()
                for k in range(KT):
                    self.mm(ps.h[:, :], sl.h[:, k * 128:(k + 1) * 128], nT.h[:, k * TB:(k + 1) * TB], k == 0, k == KT - 1,
                            r=[sl.r, nT.sub[k]], w=[ps.r])
                r_ = rl[ft % 2]
                self.act(r_.h[:, :], ps.h[:, :], AF.Relu, r=[ps.r], w=[r_.r])
                self.tt("pool" if ft % 2 else "dve", hid.h[:, ft * TB:(ft + 1) * TB], r_.h[:, :], r_.h[:, :], ALU.mult, r=[r_.r], w=[hid.sub[ft]])
            for mt in range(KT):
                ps = self.bank()
                for q in range(4):
                    sl = self.slab(l, SLAB_W2 + mt * 4 + q)
                    for k in range(8):
                        f = q * 8 + k
                        self.mm(ps.h[:, :], sl.h[:, k * 128:(k + 1) * 128], hid.h[:, f * TB:(f + 1) * TB], f == 0, f == 31,
                                r=[sl.r, hid.sub[f]], w=[ps.r])
                c = self.hcol(mt, b)
                self.tt("dve", self.hT[:, c:c + TB], self.hT[:, c:c + TB], ps.h[:, :], ALU.add, r=[ps.r], w=[self.r_h[mt][b]])
            self.rmsnorm(l, SP_G3, b, nT, sq, rstd)
            for k in range(2):
                r0 = l * 256 + k * 128
                S.dma("pool", pTb.h[:, k * TB:(k + 1) * TB], self.pT[r0:r0 + 128, b * TB:(b + 1) * TB], w=[pTb.sub[k]])
            for mt in range(KT):
                if mt % 4 == 0:
                    pls = self.slab(l, SLAB_PLE + mt // 4)
                sl = self.slab(l, SLAB_GATE + mt)
                psg = self.bank()
                for k in range(KT):
                    self.mm(psg.h[:, :], sl.h[:, k * 128:(k + 1) * 128], nT.h[:, k * TB:(k + 1) * TB], k == 0, k == KT - 1,
                            r=[sl.r, nT.sub[k]], w=[psg.r])
                psp = self.bank()
                for k in range(2):
                    a = (mt % 4) * 2 + k
                    self.mm(psp.h[:, :], pls.h[:, a * 128:(a + 1) * 128], pTb.h[:, k * TB:(k + 1) * TB], k == 0, k == 1,
                            r=[pls.r, pTb.sub[k]], w=[psp.r])
                r_ = rl[mt % 2]
                self.act(r_.h[:, :], psg.h[:, :], AF.Sigmoid, r=[psg.r], w=[r_.r])
                self.tt("dve", r_.h[:, :], r_.h[:, :], psp.h[:, :], ALU.mult, r=[psp.r, r_.r], w=[r_.r])
                c = self.hcol(mt, b)
                self.tt("pool", self.hT[:, c:c + TB], self.hT[:, c:c + TB], r_.h[:, :], ALU.add, r=[r_.r], w=[self.r_h[mt][b]])
                if l == self.n_layers - 1:
                    S.dma("pool", self.outT[mt * 128:(mt + 1) * 128, b * TB:(b + 1) * TB], self.hT[:, c:c + TB],
                          r=[self.r_h[mt][b]], w=())
        S.barrier()
        A.reset(m0)

    def epilogue(self):
        self.S.barrier()

    def dump(self, name, src):
        if name in self.dbg:
            self.S.barrier()
            self.S.dma("sp", self.dbg[name], src, w=[self.r_dummy])
            self.S.barrier()

    def mark(self, name):
        self.marks.append((name, dict(self.S.cnt)))

    def build(self):
        self.marks = []
        self.prologue()
        for l in range(self.n_layers):
            ph = self.phases
            self.mark("L%d.P1" % l)
            if ph is None or 1 in ph:
                self.phase1(l)
            self.mark("L%d.P2" % l)
            if l + 1 < self.n_layers:
                self.cast_weights(l + 1)
            if ph is None or 2 in ph:
                self.phase2(l)
            self.mark("L%d.P3" % l)
            if ph is None or 3 in ph:
                self.phase3(l)
            self.mark("L%d.P4" % l)
            if ph is None or 4 in ph:
                self.phase4(l)
            self.mark("L%d.P5" % l)
            if l == 0:
                self.dump("d_qT", self.qT_d)
                self.dump("d_kT", self.kT_d)
                self.dump("d_v", self.v_d)
                self.dump("d_mix", self.mixT_d)
            if ph is None or 5 in ph:
                self.phase5(l)
        self.mark("END")
        self.epilogue()
        self.S.emit()
        return self.nc


def _tiles(W, order):
    return [W[kt * 128:(kt + 1) * 128, mt * 128:(mt + 1) * 128] for (kt, mt) in order]


def pack_weights(inp):
    out = np.zeros((L, NSLAB, 128, 8, 128), np.float32)
    for l in range(L):
        tl = []
        w_in = inp["w_in"][l]
        for grp0, nmt in ((0, 4), (4, 4)):
            tl += _tiles(w_in, [(kt, grp0 + mt) for mt in range(nmt) for kt in range(8)])
        tl += _tiles(w_in, [(kt, 8 + n) for kt in range(8) for n in range(4)])
        tl += _tiles(w_in, [(kt, 12 + mt) for mt in range(4) for kt in range(8)])
        tl += _tiles(w_in, [(kt, 16 + mt) for mt in range(2) for kt in range(8)])
        tl += _tiles(inp["w_out"][l], [(kt, mt) for mt in range(8) for kt in range(8)])
        tl += _tiles(inp["w_ff1"][l], [(kt, mt) for mt in range(32) for kt in range(8)])
        tl += _tiles(inp["w_ff2"][l], [(kt, mt) for mt in range(8) for kt in range(32)])
        tl += _tiles(inp["w_ple_gate"][l], [(kt, mt) for mt in range(8) for kt in range(8)])
        tl += _tiles(inp["w_ple_proj"][l], [(kt, mt) for mt in range(8) for kt in range(2)])
        tl += _tiles(inp["ssm_glu_w"][l], [(kt, mt) for mt in range(2) for kt in range(2)])
        for i, t in enumerate(tl):
            out[l, i // 8, :, i % 8, :] = t
    return out.reshape(L * NSLAB * 128, 1024)


def pack_small(inp):
    sp = np.zeros((128, L, SP_N), np.float32)
    p = np.arange(128)
    for l in range(L):
        for off, key in ((SP_G1, "mix_norm"), (SP_G2, "mlp_norm"), (SP_G3, "ple_norm")):
            sp[:, l, off:off + 8] = inp[key][l].reshape(8, 128).T
        sp[:, l, SP_QG] = inp["q_norm"][l][p % 64]
        sp[:, l, SP_KG] = inp["k_norm"][l][p % 64]
        for i, key in enumerate(("lam_q1", "lam_k1", "lam_q2", "lam_k2")):
            sp[:, l, SP_LAM + 64 * i:SP_LAM + 64 * (i + 1)] = inp[key][l][None, :]
        sp[:, l, SP_OG] = inp["att_out_norm"][l]
        for ct in range(2):
            sp[:, l, SP_CW + ct * 31:SP_CW + (ct + 1) * 31] = inp["conv_w"][l][:, ct * 128:(ct + 1) * 128].T
        sp[:, l, SP_CB:SP_CB + 2] = inp["conv_b"][l].reshape(2, 128).T
        sp[:, l, SP_LNG:SP_LNG + 2] = inp["conv_ln_g"][l].reshape(2, 128).T
        sp[:, l, SP_LNB:SP_LNB + 2] = inp["conv_ln_b"][l].reshape(2, 128).T
        for k in range(8):
            g = 2 * k + p // 64
            sp[:, l, SP_ARE + k] = inp["ssm_a_re"][l][g, p % 64]
            sp[:, l, SP_AIM + k] = inp["ssm_a_im"][l][g, p % 64]
            sp[:, l, SP_LDT + k] = inp["ssm_log_dt"][l][g]
        sp[:, l, SP_SD:SP_SD + 2] = inp["ssm_d"][l].reshape(2, 128).T
        sp[:, l, SP_GB:SP_GB + 2] = inp["ssm_glu_b"][l].reshape(2, 128).T
    return sp.reshape(128, L * SP_N)


def pack_ssm(inp):
    out = np.zeros((L, 128, 32, 128), np.float32)
    for l in range(L):
        for k in range(8):
            for half in range(2):
                g = 2 * k + half
                chl = (g % 8) * 16
                rows = slice(half * 64, half * 64 + 64)
                out[l, chl:chl + 16, k, rows] = inp["ssm_b_re"][l][g].T
                out[l, chl:chl + 16, 8 + k, rows] = inp["ssm_b_im"][l][g].T
                out[l, rows, 16 + k, chl:chl + 16] = inp["ssm_c_re"][l][g].T
                out[l, rows, 24 + k, chl:chl + 16] = inp["ssm_c_im"][l][g].T
    return out.reshape(L * 128, 4096)


def make_consts():
    cst = np.zeros((128, C_N), np.float32)
    cst[:, C_ID:C_ID + 128] = np.eye(128, dtype=np.float32)
    p = np.arange(128)
    cst[:, C_BD:C_BD + 128] = (p[:, None] // 64 == p[None, :] // 64).astype(np.float32) / 64.0
    k = p[:, None]
    q = p[None, :]
    for h in range(4):
        same = (k // 64) == (q // 64)
        cst[:, C_DH + 128 * h:C_DH + 128 * (h + 1)] = np.where(k <= q, 0.0, np.where(same, -2.0 * SLOPES[h] * (k - q), -30000.0))
    bct = np.zeros((128, 4, 36), np.float32)
    for h in range(4):
        for mi in range(36):
            bct[:, h, mi] = SLOPES[h] * (p - 128.0 * (mi - 3))
    return cst, bct.reshape(128, 144)


_NC_CACHE = {}


def kernel(**inp):
    inp = {k: np.asarray(v) for k, v in inp.items()}
    key = "full"
    if key not in _NC_CACHE:
        _NC_CACHE[key] = Builder().build()
    nc = _NC_CACHE[key]
    wf = pack_weights(inp)
    spm = pack_small(inp)
    ssmm = pack_ssm(inp)
    cst, bct = make_consts()
    in_maps = []
    for c in range(8):
        in_maps.append({
            "xT": np.ascontiguousarray(inp["x"][c].T),
            "pT": np.ascontiguousarray(inp["p"][:, c].transpose(0, 2, 1)).reshape(L * 256, NT),
            "wf": wf, "spm": spm, "ssmm": ssmm, "cst": cst, "bct": bct,
        })
    res = run_bass_kernel_spmd(nc, in_maps, core_ids=list(range(8)))
    out = np.stack([np.ascontiguousarray(res.results[c]["outT"].T) for c in range(8)], axis=0)
    return out.astype(np.float32)
```
